# Optimizing a Trainium2 kernel written in Bass

```python
import math
import jax, jax.numpy as jnp
from jax import lax
import numpy as np

D_MODEL = 2048
BATCH = 1
SEQ = 8192
DEPTH = 2

EPS = 1e-6
BLK = 128
DA_HEADS = 8
DA_QK_DIM = 64
DA_V_DIM = 2 * DA_QK_DIM
DA_WIDTH = DA_HEADS * DA_V_DIM
RET_HEADS = 8
RET_QK_DIM = 64
RET_V_DIM = 128
RET_WIDTH = RET_HEADS * RET_V_DIM
RET_CHUNK = 128
ROT_BASE = 10000.0
DIL_CONFIGS = ((128, 1), (512, 4), (2048, 16))
DIL_GROUPS = 3
DIL_HEADS = 8
DIL_HEAD_DIM = 128
DIL_WIDTH = DIL_HEADS * DIL_HEAD_DIM
DIL_QKV = DIL_GROUPS * DIL_HEADS * DIL_HEAD_DIM
IN_SPLITS = (
    DA_HEADS * 2 * DA_QK_DIM, DA_HEADS * 2 * DA_QK_DIM, DA_WIDTH, DA_WIDTH,
    RET_HEADS * RET_QK_DIM, RET_HEADS * RET_QK_DIM, RET_WIDTH, RET_WIDTH,
    DIL_QKV, DIL_QKV, DIL_QKV, DIL_WIDTH,
    D_MODEL, D_MODEL, D_MODEL,
)
IN_WIDTH = sum(IN_SPLITS)

kernel_name = "hybrid_diffattn_retention_dilated_gated"


def rms_norm(x, w):
    xf = x.astype(jnp.float32)
    y = xf * lax.rsqrt(jnp.mean(xf * xf, axis=-1, keepdims=True) + EPS)
    return (y * w.astype(jnp.float32)).astype(x.dtype)


def diff_attention(q, k, v, lam, lam_init, norm_w):
    B, S, H, _, dq = q.shape
    dv = v.shape[-1]
    lf = lam.astype(jnp.float32)
    lam_full = jnp.exp(jnp.sum(lf[0] * lf[1])) - jnp.exp(jnp.sum(lf[2] * lf[3])) + lam_init
    qf = q.astype(jnp.float32) * (dq ** -0.5)
    kf = k.astype(jnp.float32)
    vf = v.astype(jnp.float32)
    nb = S // BLK
    q_blocks = jnp.moveaxis(qf.reshape(B, nb, BLK, H, 2, dq), 1, 0)
    k_pos = jnp.arange(S)

    def one_block(args):
        qb, start = args
        s = jnp.einsum('bqhmd,bkhmd->bhmqk', qb, kf)
        q_pos = start + jnp.arange(BLK)
        causal = k_pos[None, :] <= q_pos[:, None]
        p = jax.nn.softmax(jnp.where(causal, s, -jnp.inf), axis=-1)
        a = p[:, :, 0] - lam_full * p[:, :, 1]
        return jnp.einsum('bhqk,bkhd->bqhd', a, vf)

    o = lax.map(one_block, (q_blocks, jnp.arange(nb) * BLK))
    o = jnp.moveaxis(o, 0, 1).reshape(B, S, H, dv)
    o = rms_norm(o, norm_w) * (1.0 - lam_init)
    return o.reshape(B, S, H * dv)


def rotate_pairs(x, pos):
    d = x.shape[-1]
    theta = 1.0 / (ROT_BASE ** jnp.linspace(0.0, 1.0, d // 2, dtype=jnp.float32))
    ang = pos.astype(jnp.float32)[:, None] * theta[None, :]
    cos = jnp.repeat(jnp.cos(ang), 2, axis=-1)[None, :, None, :]
    sin = jnp.repeat(jnp.sin(ang), 2, axis=-1)[None, :, None, :]
    rot = jnp.stack([-x[..., 1::2], x[..., 0::2]], axis=-1).reshape(x.shape)
    return x * cos + rot * sin


def retention(q, k, v, norm_w):
    B, S, H, dk = q.shape
    dv = v.shape[-1]
    C = RET_CHUNK
    N = S // C
    pos = jnp.arange(S)
    q = rotate_pairs(q.astype(jnp.float32), pos)
    k = rotate_pairs(k.astype(jnp.float32), pos) * (dk ** -0.5)
    v = v.astype(jnp.float32)
    log_g = jnp.log1p(-jnp.exp2(-5.0 - jnp.arange(H, dtype=jnp.float32)))
    qc = q.reshape(B, N, C, H, dk)
    kc = k.reshape(B, N, C, H, dk)
    vc = v.reshape(B, N, C, H, dv)
    i = jnp.arange(C, dtype=jnp.float32)
    diff = i[:, None] - i[None, :]
    decay = jnp.where(diff >= 0, jnp.exp(jnp.maximum(diff, 0.0)[None] * log_g[:, None, None]), 0.0)
    scores = jnp.einsum('bnihd,bnjhd->bnhij', qc, kc) * decay[None, None]
    o_inner = jnp.einsum('bnhij,bnjhe->bnihe', scores, vc)
    k_dec = jnp.exp((C - 1.0 - i)[:, None] * log_g[None, :])
    kv = jnp.einsum('bnjhd,bnjhe->nbhde', kc * k_dec[None, None, :, :, None], vc)
    chunk_decay = jnp.exp(C * log_g)[None, :, None, None]

    def step(R, kv_n):
        return chunk_decay * R + kv_n, R

    _, R_prev = lax.scan(step, jnp.zeros((B, H, dk, dv), jnp.float32), kv)
    q_dec = jnp.exp((i + 1.0)[:, None] * log_g[None, :])
    o_cross = jnp.einsum('bnihd,nbhde->bnihe', qc * q_dec[None, None, :, :, None], R_prev)
    o = (o_inner + o_cross).reshape(B, S, H, dv)
    o = rms_norm(o, norm_w)
    return o.reshape(B, S, H * dv)


def dilated_group(q, k, v, window, dilation):
    B, S, H, dh = q.shape
    radius = window // dilation
    span = dilation * BLK
    L = -(-S // span) * span
    M = L // dilation
    nb = M // BLK

    def to_sub(t):
        t = jnp.pad(t.astype(jnp.float32), ((0, 0), (0, L - S), (0, 0), (0, 0)))
        t = t.reshape(B, M, dilation, H, dh).transpose(0, 2, 3, 1, 4)
        return t.reshape(B, dilation, H, nb, BLK, dh)

    qs, ks, vs = to_sub(q), to_sub(k), to_sub(v)

    def with_prev(t):
        prev = jnp.pad(t[:, :, :, :-1], ((0, 0), (0, 0), (0, 0), (1, 0), (0, 0), (0, 0)))
        return jnp.concatenate([prev, t], axis=4)

    kk, vv = with_prev(ks), with_prev(vs)
    s = jnp.einsum('brhnqd,brhnkd->brhnqk', qs * (dh ** -0.5), kk)
    blk_idx = jnp.arange(nb)[:, None, None]
    qi = jnp.arange(BLK)[None, :, None]
    kj = jnp.arange(2 * BLK)[None, None, :]
    dist = BLK + qi - kj
    valid = (dist >= 0) & (dist <= radius) & ((blk_idx > 0) | (kj >= BLK))
    s = jnp.where(valid, s, -jnp.inf)
    lse = jax.nn.logsumexp(s, axis=-1)
    p = jnp.exp(s - lse[..., None])
    o = jnp.einsum('brhnqk,brhnkd->brhnqd', p, vv)
    o = o.reshape(B, dilation, H, M, dh).transpose(0, 3, 1, 2, 4).reshape(B, L, H, dh)[:, :S]
    lse = lse.reshape(B, dilation, H, M).transpose(0, 3, 1, 2).reshape(B, L, H)[:, :S]
    return o, lse


def dilated_mixture(q, k, v):
    B, S, G, H, dh = q.shape
    outs, lses = [], []
    for g, (w, d) in enumerate(DIL_CONFIGS):
        o, l = dilated_group(q[:, :, g], k[:, :, g], v[:, :, g], w, d)
        outs.append(o)
        lses.append(l)
    wts = jax.nn.softmax(jnp.stack(lses, axis=0), axis=0)
    o = jnp.sum(wts[..., None] * jnp.stack(outs, axis=0), axis=0)
    return o.reshape(B, S, H * dh)


def hybrid_layer(x, c, layer, w_in, w_proj_a, w_proj_b, w_proj_c, w_out, w_ada, b_ada,
                 norm_w, lam, da_norm_w, ret_norm_w):
    B, S, _ = x.shape
    dt = x.dtype
    mod = jnp.einsum('bd,de->be', jax.nn.silu(c), w_ada) + b_ada
    shift, scale, gate = jnp.split(mod, 3, axis=-1)
    u = rms_norm(x, norm_w) * (1.0 + scale[:, None]) + shift[:, None]
    proj = jnp.einsum('bsd,de->bse', u, w_in)
    offsets = np.cumsum(IN_SPLITS)[:-1].tolist()
    (a_q, a_k, a_v, a_z, b_q, b_k, b_v, b_z,
     c_q, c_k, c_v, c_z, g_a, g_b, g_c) = jnp.split(proj, offsets, axis=-1)

    lam_init = 0.8 - 0.6 * math.exp(-0.3 * layer)
    o_a = diff_attention(a_q.reshape(B, S, DA_HEADS, 2, DA_QK_DIM),
                         a_k.reshape(B, S, DA_HEADS, 2, DA_QK_DIM),
                         a_v.reshape(B, S, DA_HEADS, DA_V_DIM), lam, lam_init, da_norm_w)
    o_b = retention(b_q.reshape(B, S, RET_HEADS, RET_QK_DIM),
                    b_k.reshape(B, S, RET_HEADS, RET_QK_DIM),
                    b_v.reshape(B, S, RET_HEADS, RET_V_DIM), ret_norm_w)
    dil_shape = (B, S, DIL_GROUPS, DIL_HEADS, DIL_HEAD_DIM)
    o_c = dilated_mixture(c_q.reshape(dil_shape), c_k.reshape(dil_shape), c_v.reshape(dil_shape))

    y_a = jnp.einsum('bsw,wd->bsd', o_a.astype(dt) * jax.nn.silu(a_z), w_proj_a)
    y_b = jnp.einsum('bsw,wd->bsd', o_b.astype(dt) * jax.nn.silu(b_z), w_proj_b)
    y_c = jnp.einsum('bsw,wd->bsd', o_c.astype(dt) * jax.nn.silu(c_z), w_proj_c)
    merged = jax.nn.sigmoid(g_a) * y_a + jax.nn.sigmoid(g_b) * y_b + jax.nn.sigmoid(g_c) * y_c
    y = jnp.einsum('bsd,de->bse', merged, w_out)
    return x + gate[:, None] * y


def setup_inputs(seed: int = 0) -> dict:
    key = jax.random.key(seed)
    ks = jax.random.split(key, 16)
    f32 = jnp.float32
    D = D_MODEL
    x = jax.random.normal(ks[0], (BATCH, SEQ, D), f32)
    c = jax.random.normal(ks[1], (BATCH, D), f32)
    w_in = jax.random.normal(ks[2], (DEPTH, D, IN_WIDTH), f32) * D ** -0.5
    w_proj_a = jax.random.normal(ks[3], (DEPTH, DA_WIDTH, D), f32) * DA_WIDTH ** -0.5
    w_proj_b = jax.random.normal(ks[4], (DEPTH, RET_WIDTH, D), f32) * RET_WIDTH ** -0.5
    w_proj_c = jax.random.normal(ks[5], (DEPTH, DIL_WIDTH, D), f32) * DIL_WIDTH ** -0.5
    w_out = jax.random.normal(ks[6], (DEPTH, D, D), f32) * D ** -0.5
    w_ada = jax.random.normal(ks[7], (DEPTH, D, 3 * D), f32) * (0.5 * D ** -0.5)
    b_ada = jax.random.normal(ks[8], (DEPTH, 3 * D), f32) * 0.01
    norm_w = 1.0 + 0.02 * jax.random.normal(ks[9], (DEPTH, D), f32)
    lam = 0.1 * jax.random.normal(ks[10], (DEPTH, 4, DA_QK_DIM), f32)
    da_norm_w = 1.0 + 0.02 * jax.random.normal(ks[11], (DEPTH, DA_V_DIM), f32)
    ret_norm_w = 1.0 + 0.02 * jax.random.normal(ks[12], (DEPTH, RET_V_DIM), f32)
    final_norm_w = 1.0 + 0.02 * jax.random.normal(ks[13], (D,), f32)
    return {"x": x, "c": c, "w_in": w_in, "w_proj_a": w_proj_a, "w_proj_b": w_proj_b,
            "w_proj_c": w_proj_c, "w_out": w_out, "w_ada": w_ada, "b_ada": b_ada,
            "norm_w": norm_w, "lam": lam, "da_norm_w": da_norm_w, "ret_norm_w": ret_norm_w,
            "final_norm_w": final_norm_w}


def reference(x, c, w_in, w_proj_a, w_proj_b, w_proj_c, w_out, w_ada, b_ada, norm_w, lam,
              da_norm_w, ret_norm_w, final_norm_w):
    h = x
    for layer in range(DEPTH):
        h = hybrid_layer(h, c, layer, w_in[layer], w_proj_a[layer], w_proj_b[layer],
                         w_proj_c[layer], w_out[layer], w_ada[layer], b_ada[layer],
                         norm_w[layer], lam[layer], da_norm_w[layer], ret_norm_w[layer])
    return rms_norm(h, final_norm_w)
```

```python
import math
from contextlib import ExitStack

import numpy as np
import concourse.bass as bass
import concourse.mybir as mybir
from concourse.bass_utils import run_bass_kernel_spmd

F32 = mybir.dt.float32
BF16 = mybir.dt.bfloat16
AF = mybir.ActivationFunctionType
ALU = mybir.AluOpType

D = 2048
S = 8192
NCORES = 8
TOK = S // NCORES
EPS = 1e-6
IN_SPLITS = (1024, 1024, 1024, 1024, 512, 512, 1024, 1024, 3072, 3072, 3072, 1024, 2048, 2048, 2048)
OFFS = [0] + list(np.cumsum(IN_SPLITS))
(O_AQ, O_AK, O_AV, O_AZ, O_BQ, O_BK, O_BV, O_BZ, O_CQ, O_CK, O_CV, O_CZ, O_GA, O_GB, O_GC) = OFFS[:15]
DIL = (1, 4, 16)
DBG2 = 99

ENGS = ("pe", "act", "dve", "pool", "sp")


class Prog:
    def __init__(self, nc, dma_keys):
        self.nc = nc
        self.ops = []
        self.last_writer = {}
        self.readers = {}
        self.eng_count = {e: 0 for e in ENGS}
        self.dma_keys = list(dma_keys)
        self.dma_count = {k: 0 for k in self.dma_keys}

    def op(self, eng, fn, reads=(), writes=(), dma=None, extra_waits=None):
        ps_r = [r for r in reads if isinstance(r, tuple) and r[0] == "ps"]
        if ps_r:
            reads = [r for r in reads if r not in ps_r]
            writes = list(writes) + ps_r
        deps = set()
        for r in reads:
            w = self.last_writer.get(r)
            if w is not None:
                deps.add(w)
        for r in writes:
            w = self.last_writer.get(r)
            if w is not None:
                deps.add(w)
            for x in self.readers.get(r, ()):
                deps.add(x)
        idx = len(self.ops)
        waits = dict(extra_waits or {})
        for d in deps:
            o = self.ops[d]
            t = o["token"]
            if t[0] == "e":
                if t[1] == "pe" and eng == "pe" and dma is None:
                    continue
                key = ("e", t[1])
                waits[key] = max(waits.get(key, 0), t[2])
            else:
                key = ("d", t[1])
                waits[key] = max(waits.get(key, 0), self.dma_count[t[1]])
        if dma is None:
            self.eng_count[eng] += 1
            token = ("e", eng, self.eng_count[eng])
        else:
            self.dma_count[dma] += 16
            token = ("d", dma, self.dma_count[dma])
        self.ops.append(dict(eng=eng, fn=fn, token=token, waits=waits, dma=dma))
        for r in reads:
            self.readers.setdefault(r, []).append(idx)
        for r in writes:
            self.last_writer[r] = idx
            self.readers[r] = []
        return idx

    def barrier(self):
        snap = {}
        for e in ENGS:
            if self.eng_count[e]:
                snap[("e", e)] = self.eng_count[e]
        for k in self.dma_keys:
            if self.dma_count[k]:
                snap[("d", k)] = self.dma_count[k]
        for e in ENGS:
            self.ops.append(dict(eng=e, fn=None, token=None, waits=dict(snap), dma=None))
        self.last_writer = {}
        self.readers = {}

    def emit(self):
        nc = self.nc
        with ExitStack() as es:
            sems = {}
            for e in ENGS:
                sems[("e", e)] = es.enter_context(nc.semaphore("s_" + e))
            for k in self.dma_keys:
                sems[("d", k)] = es.enter_context(nc.semaphore("d_" + str(k)))
            block = es.enter_context(nc.Block())
            per_eng = {e: [o for o in self.ops if o["eng"] == e] for e in ENGS}
            final = {}
            for e in ENGS:
                if self.eng_count[e]:
                    final[("e", e)] = self.eng_count[e]
            for k in self.dma_keys:
                if self.dma_count[k]:
                    final[("d", k)] = self.dma_count[k]

            def run(e, h):
                waited = {}
                for o in per_eng[e]:
                    for key, val in o["waits"].items():
                        if key == ("e", e) and e == "pe":
                            continue
                        if waited.get(key, 0) >= val:
                            continue
                        h.wait_ge(sems[key], val)
                        waited[key] = val
                    if o["fn"] is None:
                        continue
                    ins = o["fn"](h)
                    t = o["token"]
                    if t[0] == "e":
                        ins.then_inc(sems[("e", e)], 1)
                    else:
                        ins.then_inc(sems[("d", t[1])], 16)
                if e == "sp":
                    for key, val in final.items():
                        if waited.get(key, 0) >= val:
                            continue
                        h.wait_ge(sems[key], val)

            @block.tensor
            def _(h):
                run("pe", h)

            @block.scalar
            def _(h):
                run("act", h)

            @block.vector
            def _(h):
                run("dve", h)

            @block.gpsimd
            def _(h):
                run("pool", h)

            @block.sync
            def _(h):
                run("sp", h)


class Arena:
    def __init__(self, ap, nbytes):
        self.ap = ap
        self.nbytes = nbytes
        self.off = 0

    def alloc(self, shape, dtype):
        n = 1
        for s_ in shape[1:]:
            n *= s_
        esz = 4 if dtype == F32 else 2
        nb = n * esz
        off = (self.off + 63) // 64 * 64
        assert off + nb <= self.nbytes, f"arena overflow {off + nb} > {self.nbytes}"
        self.off = off + nb
        v = self.ap[0:shape[0], off // 2:(off + nb) // 2]
        if dtype == F32:
            v = v.bitcast(F32)
        if len(shape) == 3:
            v = v.rearrange("p (a b) -> p a b", a=shape[1])
        elif len(shape) == 4:
            v = v.rearrange("p (a b c) -> p a b c", a=shape[1], b=shape[2])
        return v


def _bf16_np():
    import ml_dtypes
    return ml_dtypes.bfloat16


def emit_mod_acc(P, A, d_wada, d_cT, d_brow, cols=(0, 6144)):
    c0, c1 = cols
    n = c1 - c0
    cT = A.alloc([128, 16], F32)
    sc = A.alloc([128, 16], F32)
    acc = A.alloc([128, n], F32)
    wk = [A.alloc([128, n], F32), A.alloc([128, n], F32)]
    brow = A.alloc([1, n], F32)
    P.op("sp", lambda e: e.dma_start(out=cT, in_=d_cT), writes=["cT"], dma="misc")
    P.op("sp", lambda e: e.dma_start(out=brow, in_=d_brow[:, c0:c1]), writes=["brow"], dma="misc")
    P.op("act", lambda e: e.activation(out=sc, in_=cT, func=AF.Silu), reads=["cT"], writes=["sc"])
    for k in range(16):
        b = k % 2
        P.op("sp", lambda e, k=k, b=b: e.dma_start(out=wk[b], in_=d_wada[k * 128:(k + 1) * 128, c0:c1]),
             writes=[("wk", b)], dma="wk%d" % b)
        if k == 0:
            P.op("dve", lambda e, b=b: e.tensor_scalar(out=acc, in0=wk[b], scalar1=sc[:, 0:1], scalar2=None,
                                                        op0=ALU.mult), reads=[("wk", b), "sc"], writes=["acc"])
        else:
            P.op("dve", lambda e, k=k, b=b: e.scalar_tensor_tensor(out=acc, in0=wk[b], scalar=sc[:, k:k + 1], in1=acc,
                                                                   op0=ALU.mult, op1=ALU.add),
                 reads=[("wk", b), "sc", "acc"], writes=["acc"])
    P.op("dve", lambda e: e.tensor_tensor(out=acc[0:1, :], in0=acc[0:1, :], in1=brow[0:1, :], op=ALU.add),
         reads=["acc", "brow"], writes=["acc"])
    if c0 <= 2048 and c1 >= 4096:
        P.op("dve", lambda e: e.tensor_scalar(out=acc[0:1, 2048 - c0:4096 - c0], in0=acc[0:1, 2048 - c0:4096 - c0],
                                              scalar1=1.0, scalar2=None, op0=ALU.add), reads=["acc"], writes=["acc"])
    return acc


def build_prep():
    nc = bass.Bass("TRN2", target_bir_lowering=False)
    d_x = nc.dram_tensor("x", [TOK, D], F32, kind="ExternalInput").ap()
    d_cT = nc.dram_tensor("cT", [128, 16], F32, kind="ExternalInput").ap()
    d_wada = nc.dram_tensor("w_ada", [D, 3 * D], F32, kind="ExternalInput").ap()
    d_brow = nc.dram_tensor("brow", [1, 3 * D], F32, kind="ExternalInput").ap()
    d_nwT = nc.dram_tensor("nwT", [128, 16], F32, kind="ExternalInput").ap()
    d_ident = nc.dram_tensor("ident", [128, 128], F32, kind="ExternalInput").ap()
    d_uT = nc.dram_tensor("uT", [D, TOK], BF16, kind="ExternalOutput").ap()
    with ExitStack() as es:
        arena_t = es.enter_context(nc.sbuf_tensor("arena", [128, 90 * 1024], BF16))
        A = Arena(arena_t[:, :], 180 * 1024)
        ps = [es.enter_context(nc.psum_tensor("ps%d" % i, [128, 512], F32)) for i in range(8)]
        P = Prog(nc, ["misc", "wk0", "wk1", "x0", "x1", "out"])
        ident = A.alloc([128, 128], F32)
        ones = A.alloc([128, 2], F32)
        nwT = A.alloc([128, 16], F32)
        P.op("sp", lambda e: e.dma_start(out=ident, in_=d_ident), writes=["ident"], dma="misc")
        P.op("sp", lambda e: e.dma_start(out=nwT, in_=d_nwT), writes=["nwT"], dma="misc")
        P.op("dve", lambda e: e.memset(ones, 1.0), writes=["ones"])
        acc = emit_mod_acc(P, A, d_wada, d_cT, d_brow, cols=(0, 4096))
        modT = A.alloc([128, 32], F32)
        AT = A.alloc([128, 16], F32)
        for j in range(32):
            P.op("pe", lambda e, j=j: e.matmul(ps[0][:, 2 * j:2 * j + 2], lhsT=acc[:, j * 128:(j + 1) * 128],
                                               rhs=ones[:, 0:2], start=True, stop=True),
                 reads=["acc", "ones"], writes=[("ps", 0)])
        P.op("dve", lambda e: e.tensor_copy(modT, ps[0][:, 0:64:2]), reads=[("ps", 0)], writes=["modT"])
        P.op("dve", lambda e: e.tensor_tensor(out=AT, in0=modT[:, 16:32], in1=nwT, op=ALU.mult),
             reads=["modT", "nwT"], writes=["AT"])
        xt = [A.alloc([128, D], F32), A.alloc([128, D], F32)]
        junk = A.alloc([128, D], F32)
        uT = A.alloc([128, 16, TOK], BF16)
        ss = A.alloc([128, 8], F32)
        var = A.alloc([128, 8], F32)
        rstd = A.alloc([128, 8], F32)
        epsc = A.alloc([128, 1], F32)
        P.op("dve", lambda e: e.memset(epsc, EPS), writes=["epsc"])
        for t in range(8):
            b = t % 2
            P.op("sp", lambda e, t=t, b=b: e.dma_start(out=xt[b], in_=d_x[t * 128:(t + 1) * 128, :]),
                 writes=[("xt", b)], dma="x%d" % b)
            P.op("act", lambda e, t=t, b=b: e.activation(out=junk, in_=xt[b], func=AF.Square, accum_out=ss[:, t:t + 1]),
                 reads=[("xt", b)], writes=["junk", ("ss", t)])
            P.op("act", lambda e, t=t: e.activation(out=var[:, t:t + 1], in_=ss[:, t:t + 1], func=AF.Sqrt, scale=1.0 / D,
                                                    bias=epsc), reads=[("ss", t), "epsc"], writes=[("var", t)])
            P.op("dve", lambda e, t=t: e.reciprocal(rstd[:, t:t + 1], var[:, t:t + 1]),
                 reads=[("var", t)], writes=[("rstd", t)])
            P.op("dve", lambda e, t=t, b=b: e.tensor_scalar(out=xt[b], in0=xt[b], scalar1=rstd[:, t:t + 1], scalar2=None,
                                                            op0=ALU.mult), reads=[("xt", b), ("rstd", t)], writes=[("xt", b)])
            for g in range(4):
                pb = 1 + (t * 4 + g) % 4
                for q in range(4):
                    c = g * 4 + q
                    P.op("pe", lambda e, pb=pb, q=q, c=c, b=b: e.transpose(ps[pb][:, q * 128:(q + 1) * 128],
                                                                          xt[b][:, c * 128:(c + 1) * 128], ident),
                         reads=[("xt", b), "ident"], writes=[("ps", pb)])
                for q in range(4):
                    c = g * 4 + q
                    P.op("dve", lambda e, pb=pb, q=q, c=c, t=t: e.tensor_scalar(
                        out=uT[:, c, t * 128:(t + 1) * 128], in0=ps[pb][:, q * 128:(q + 1) * 128],
                        scalar1=AT[:, c:c + 1], scalar2=modT[:, c:c + 1], op0=ALU.mult, op1=ALU.add),
                        reads=[("ps", pb), "AT", "modT"], writes=[("uT", t)])
        P.op("sp", lambda e: e.dma_start(out=d_uT.rearrange("(c p) t -> p c t", p=128), in_=uT),
             reads=[("uT", t) for t in range(8)], dma="out")
        P.emit()
    return nc


NV_LAM = 0
NV_DNW = 256
NV_RNW = 384
NV_TRI = 512
NV_MPREV = 640
NV_IDENT = 768
NV_G = 896
NV_TOT = 904
WMIX = 2176
WA0, WB0, WC0 = 0, 512, 896


def build_mix(layer, passes=("A", "B", "C"), ntiles=16, dbg=99):
    lam_init = 0.8 - 0.6 * math.exp(-0.3 * layer)
    nc = bass.Bass("TRN2", target_bir_lowering=False)
    d_uT = nc.dram_tensor("uT", [D, S], BF16, kind="ExternalInput").ap()
    d_w = nc.dram_tensor("w_mix", [D, WMIX], F32, kind="ExternalInput").ap()
    d_vecs = nc.dram_tensor("vecs", [128, NV_TOT], F32, kind="ExternalInput").ap()
    d_rotC = nc.dram_tensor("rotC", [S, 128], F32, kind="ExternalInput").ap()
    d_rotS = nc.dram_tensor("rotS", [S, 128], F32, kind="ExternalInput").ap()
    d_oz = nc.dram_tensor("ozT", [384, S], BF16, kind="ExternalOutput").ap()
    uT_v = d_uT.rearrange("(c p) t -> p c t", p=128)
    w_v = d_w.rearrange("(c p) n -> p c n", p=128)
    with ExitStack() as es:
        arena_t = es.enter_context(nc.sbuf_tensor("arena", [128, 92 * 1024], BF16))
        A = Arena(arena_t[:, :], 184 * 1024)
        ps = [es.enter_context(nc.psum_tensor("ps%d" % i, [128, 512], F32)) for i in range(8)]
        P = Prog(nc, ["misc", "w", "u0", "u1", "rot0", "rot1", "out0", "out1"])
        vecs = A.alloc([128, NV_TOT], F32)
        P.op("sp", lambda e: e.dma_start(out=vecs, in_=d_vecs), writes=["vecs"], dma="misc")
        tri_bf = A.alloc([128, 128], BF16)
        mask_c = A.alloc([128, 2, 2, 128], BF16)
        ones_bf = A.alloc([128, 128], BF16)
        epsc = A.alloc([128, 1], F32)
        ident = vecs[:, NV_IDENT:NV_IDENT + 128]
        tri_f = vecs[:, NV_TRI:NV_TRI + 128]
        P.op("dve", lambda e: e.tensor_copy(tri_bf, tri_f), reads=["vecs"], writes=["consts"])
        for r_ in range(2):
            P.op("dve", lambda e, r_=r_: e.tensor_copy(mask_c[:, r_, 0, :], vecs[:, NV_MPREV:NV_MPREV + 128]),
                 reads=["vecs"], writes=["consts"])
            P.op("dve", lambda e, r_=r_: e.tensor_copy(mask_c[:, r_, 1, :], tri_f), reads=["vecs"], writes=["consts"])
        P.op("dve", lambda e: e.memset(ones_bf, 1.0), writes=["consts"])
        P.op("dve", lambda e: e.memset(epsc, EPS), writes=["epsc"])
        U = [A.alloc([128, 16, 512], BF16), A.alloc([128, 16, 512], BF16)]
        mark = A.off

        def load_U(j):
            b = j % 2
            for q4 in range(4):
                P.op("sp", lambda e, j=j, b=b, q4=q4: e.dma_start(out=U[b][:, q4 * 4:(q4 + 1) * 4, :],
                                                                 in_=uT_v[:, q4 * 4:(q4 + 1) * 4, j * 512:(j + 1) * 512]),
                     writes=[("U", b, q4)], dma="u%d" % b)

        def load_w(wt, c0, ncols):
            for half in range(2):
                P.op("pool", lambda e, half=half: e.dma_start(out=wt[:, half * 8:(half + 1) * 8, :],
                                                             in_=w_v[:, half * 8:(half + 1) * 8, c0:c0 + ncols]),
                     writes=[("w", half)], dma="w")

        ostate = {"n": 0}

        def store_oz(stage_alloc, row0, j, fill):
            b = ostate["n"] % 2
            ostate["n"] += 1
            st = stage_alloc[b]
            fill(st, ("ozst", b))
            P.op("sp", lambda e, st=st, j=j: e.dma_start(out=d_oz[row0:row0 + 128, j * 512:(j + 1) * 512], in_=st),
                 reads=[("ozst", b)], dma="out%d" % b)

        def pass_a():
            A.off = mark
            wA = A.alloc([128, 16, 512], BF16)
            kT = A.alloc([128, S], BF16)
            V = A.alloc([128, 64, 144], BF16)
            qT = [A.alloc([128, 512], BF16), A.alloc([128, 512], BF16)]
            sz = [A.alloc([128, 4, 128], F32), A.alloc([128, 4, 128], F32)]
            NPT = 4
            pT = [A.alloc([128, 512], BF16) for _ in range(NPT)]
            om0 = A.alloc([128, 4, 128], F32)
            om1 = A.alloc([128, 4, 128], F32)
            osum = A.alloc([128, 4, 128], F32)
            junk = A.alloc([128, 128], F32)
            ozb = [A.alloc([128, 4, 128], F32), A.alloc([128, 4, 128], F32)]
            ozst = [A.alloc([128, 512], BF16), A.alloc([128, 512], BF16)]
            rs = A.alloc([128, 8], F32)
            ssq = A.alloc([128, 4], F32)
            var = A.alloc([128, 4], F32)
            rstd = A.alloc([128, 4], F32)
            lf = A.alloc([128, 8], F32)
            dnw = A.alloc([128, 128], F32)
            load_w(wA, WA0, 512)
            load_U(0)
            P.op("dve", lambda e: e.memset(V[:, :, 128:144], 1.0), writes=["Vones"])
            lam = vecs[:, NV_LAM:NV_LAM + 256]
            P.op("dve", lambda e: e.scalar_tensor_tensor(out=junk[:, 0:64], in0=lam[:, 0:64], scalar=1.0, in1=lam[:, 64:128],
                                                         op0=ALU.mult, op1=ALU.mult, accum_out=lf[:, 0:1]),
                 reads=["vecs"], writes=["junk", "lf"])
            P.op("dve", lambda e: e.scalar_tensor_tensor(out=junk[:, 0:64], in0=lam[:, 128:192], scalar=1.0, in1=lam[:, 192:256],
                                                         op0=ALU.mult, op1=ALU.mult, accum_out=lf[:, 1:2]),
                 reads=["vecs", "junk", "lf"], writes=["junk", "lf"])
            P.op("act", lambda e: e.activation(out=lf[:, 2:4], in_=lf[:, 0:2], func=AF.Exp), reads=["lf"], writes=["lf"])
            P.op("dve", lambda e: e.tensor_tensor(out=lf[:, 4:5], in0=lf[:, 3:4], in1=lf[:, 2:3], op=ALU.subtract),
                 reads=["lf"], writes=["lf"])
            P.op("dve", lambda e: e.tensor_scalar(out=lf[:, 4:5], in0=lf[:, 4:5], scalar1=-lam_init, scalar2=None, op0=ALU.add),
                 reads=["lf"], writes=["lf"])
            P.op("dve", lambda e: e.tensor_scalar(out=dnw, in0=vecs[:, NV_DNW:NV_DNW + 128], scalar1=1.0 - lam_init,
                                                  scalar2=None, op0=ALU.mult), reads=["vecs"], writes=["dnw"])
            acc_b = [0, 1, 2, 3]
            sc_b = [4, 5]
            pj_b = [6, 7]
            cnt = {"s": 0, "pj": 0}

            def projA(j):
                b = j % 2
                rd = [("U", b, 0), ("U", b, 1), ("U", b, 2), ("U", b, 3), ("w", 0), ("w", 1)]
                for which in range(2 if DBG2 >= 1 else 1):
                    pb = pj_b[cnt["pj"] % 2]
                    cnt["pj"] += 1
                    for c in range(16):
                        P.op("pe", lambda e, pb=pb, c=c, b=b, which=which: e.matmul(
                            ps[pb][:, :], lhsT=wA[:, c, which * 128:(which + 1) * 128], rhs=U[b][:, c, :],
                            start=(c == 0), stop=(c == 15)), reads=rd, writes=[("ps", pb)])
                    if which == 0:
                        P.op("dve", lambda e, pb=pb, b=b: e.tensor_copy(qT[b], ps[pb][:, :]), reads=[("ps", pb)],
                             writes=[("qT", b)])
                    else:
                        P.op("dve", lambda e, pb=pb, j=j: e.tensor_copy(kT[:, j * 512:(j + 1) * 512], ps[pb][:, :]),
                             reads=[("ps", pb)], writes=[("kT", j)])
                if DBG2 < 2:
                    return
                for s_ in range({3: 4, 4: 4, 5: 2}.get(DBG2, 4) if DBG2 >= 3 else 1):
                    pb = pj_b[cnt["pj"] % 2]
                    cnt["pj"] += 1
                    for c in range(16):
                        P.op("pe", lambda e, pb=pb, c=c, b=b, s_=s_: e.matmul(
                            ps[pb][:, 0:256], lhsT=U[b][:, c, s_ * 128:(s_ + 1) * 128], rhs=wA[:, c, 256:512],
                            start=(c == 0), stop=(c == 15)), reads=rd, writes=[("ps", pb)])
                    if DBG2 != 4:
                        P.op("dve", lambda e, pb=pb, j=j, s_=s_: e.tensor_copy(V[:, j * 4 + s_, 0:128], ps[pb][:, 0:128]),
                             reads=[("ps", pb)], writes=[("V", j)])
                    if DBG2 != 3:
                        P.op("act", lambda e, pb=pb, b=b, s_=s_: e.activation(out=sz[b][:, s_, :], in_=ps[pb][:, 128:256],
                                                                             func=AF.Silu), reads=[("ps", pb)], writes=[("sz", b)])

            def attnA(j, m):
                b = j % 2
                nkb = 4 * j + 4
                slots = {}

                def SE(kb):
                    lo = max(0, kb - 4 * j)
                    w_ = (4 - lo) * 128
                    sb_ = sc_b[cnt["s"] % 2]
                    slot = cnt["s"] % NPT
                    cnt["s"] += 1
                    slots[kb] = slot
                    P.op("pe", lambda e: e.matmul(ps[sb_][:, 0:w_], lhsT=kT[64 * m:64 * m + 64, kb * 128:(kb + 1) * 128],
                                                  rhs=qT[b][64 * m:64 * m + 64, lo * 128:512], start=True, stop=True),
                         reads=[("kT", kb // 4), ("qT", b)], writes=[("ps", sb_)])
                    P.op("act", lambda e: e.activation(out=pT[slot][:, 0:w_], in_=ps[sb_][:, 0:w_], func=AF.Exp, scale=0.125),
                         reads=[("ps", sb_)], writes=[("pT", slot)])
                    if kb >= 4 * j:
                        P.op("dve", lambda e: e.tensor_tensor(out=pT[slot][:, 0:128], in0=pT[slot][:, 0:128], in1=tri_bf,
                                                              op=ALU.mult), reads=[("pT", slot), "consts"], writes=[("pT", slot)])

                def PV(kb):
                    lo = max(0, kb - 4 * j)
                    slot = slots[kb]
                    for qb in range(lo, 4):
                        P.op("pe", lambda e, qb=qb: e.matmul(ps[acc_b[qb]][:, 0:129],
                                                             lhsT=pT[slot][:, (qb - lo) * 128:(qb - lo + 1) * 128],
                                                             rhs=V[:, kb, 0:129], start=(kb == 0), stop=(kb == 4 * j + qb)),
                             reads=[("pT", slot), ("V", kb // 4), "Vones"], writes=[("ps", acc_b[qb])])

                SE(0)
                for kb in range(nkb):
                    if kb + 1 < nkb:
                        SE(kb + 1)
                    PV(kb)
                for qb in range(4):
                    ab = acc_b[qb]
                    col = m * 4 + qb
                    P.op("dve", lambda e, ab=ab, col=col: e.reciprocal(rs[:, col:col + 1], ps[ab][:, 128:129]),
                         reads=[("ps", ab)], writes=[("rs", col)])
                    if m == 0:
                        P.op("dve", lambda e, ab=ab, col=col, qb=qb: e.tensor_scalar(
                            out=om0[:, qb, :], in0=ps[ab][:, 0:128], scalar1=rs[:, col:col + 1], scalar2=None, op0=ALU.mult),
                            reads=[("ps", ab), ("rs", col)], writes=["om0"])
                    else:
                        P.op("dve", lambda e, ab=ab, col=col, qb=qb: e.tensor_scalar(
                            out=om1[:, qb, :], in0=ps[ab][:, 0:128], scalar1=rs[:, col:col + 1], scalar2=lf[:, 4:5],
                            op0=ALU.mult, op1=ALU.mult), reads=[("ps", ab), ("rs", col), "lf"], writes=["om1"])

            def combineA(j):
                b = j % 2
                P.op("dve", lambda e: e.tensor_tensor(out=osum, in0=om0, in1=om1, op=ALU.add), reads=["om0", "om1"],
                     writes=["osum"])
                for qb in range(4):
                    P.op("dve", lambda e, qb=qb: e.scalar_tensor_tensor(out=junk, in0=osum[:, qb, :], scalar=1.0, in1=osum[:, qb, :],
                                                                        op0=ALU.mult, op1=ALU.mult,
                                                                        accum_out=ssq[:, qb:qb + 1]),
                         reads=["osum"], writes=["junk", "ssq"])
                P.op("act", lambda e: e.activation(out=var, in_=ssq, func=AF.Sqrt, scale=1.0 / 128, bias=epsc),
                     reads=["ssq", "epsc"], writes=["var"])
                P.op("dve", lambda e: e.reciprocal(rstd, var), reads=["var"], writes=["rstd"])
                for qb in range(4):
                    P.op("dve", lambda e, qb=qb: e.scalar_tensor_tensor(out=osum[:, qb, :], in0=osum[:, qb, :],
                                                                        scalar=rstd[:, qb:qb + 1], in1=dnw, op0=ALU.mult,
                                                                        op1=ALU.mult), reads=["osum", "rstd", "dnw"],
                         writes=["osum"])
                P.op("dve", lambda e: e.tensor_tensor(out=ozb[b], in0=osum, in1=sz[b], op=ALU.mult),
                     reads=["osum", ("sz", b)], writes=[("ozb", b)])

            def finA(j):
                b = j % 2
                pb = pj_b[cnt["pj"] % 2]
                cnt["pj"] += 1
                for qb in range(4):
                    P.op("pe", lambda e, qb=qb: e.transpose(ps[pb][:, qb * 128:(qb + 1) * 128], ozb[b][:, qb, :], ident),
                         reads=[("ozb", b), "vecs"], writes=[("ps", pb)])

                def fill(st, res):
                    P.op("dve", lambda e: e.tensor_copy(st, ps[pb][:, :]), reads=[("ps", pb)], writes=[res])
                store_oz(ozst, 0, j, fill)

            for j in range(ntiles):
                if j + 1 < ntiles:
                    load_U(j + 1)
                if dbg >= 1:
                    projA(j)
                if j > 0 and dbg >= 4:
                    finA(j - 1)
                if dbg >= 2:
                    attnA(j, 0)
                    attnA(j, 1)
                if dbg >= 3:
                    combineA(j)
            if dbg >= 4:
                finA(ntiles - 1)
            P.barrier()
        if "A" in passes:
            pass_a()

        def pass_b():
            A.off = mark
            wB = A.alloc([128, 16, 384], BF16)
            rC = [A.alloc([128, 4, 128], F32), A.alloc([128, 4, 128], F32)]
            rS = [A.alloc([128, 4, 128], F32), A.alloc([128, 4, 128], F32)]
            t1 = A.alloc([128, 128], F32)
            t2 = A.alloc([128, 128], F32)
            qk = A.alloc([128, 128], F32)
            kd_bf = A.alloc([128, 64], BF16)
            v_b = A.alloc([128, 128], BF16)
            szb = A.alloc([128, 4, 128], F32)
            qdT = A.alloc([64, 128], BF16)
            kdT = A.alloc([64, 128], BF16)
            pTb = A.alloc([128, 128], BF16)
            Sst = A.alloc([64, 128], F32)
            tmpS = A.alloc([64, 128], F32)
            Rbf = A.alloc([64, 128], BF16)
            ob = A.alloc([128, 4, 128], F32)
            junkb = A.alloc([128, 128], F32)
            ssq = A.alloc([128, 4], F32)
            var = A.alloc([128, 4], F32)
            rstd = A.alloc([128, 4], F32)
            ozbB = A.alloc([128, 4, 128], F32)
            ozstB = [A.alloc([128, 512], BF16), A.alloc([128, 512], BF16)]
            rnw = vecs[:, NV_RNW:NV_RNW + 128]
            gcol = vecs[0:64, NV_G:NV_G + 1]
            load_w(wB, WB0, 384)
            load_U(0)

            def load_rot(j):
                b = j % 2
                P.op("sp", lambda e: e.dma_start(out=rC[b], in_=d_rotC[j * 512:(j + 1) * 512, :].rearrange("(c p) n -> p c n", p=128)),
                     writes=[("rC", b)], dma="rot%d" % b)
                P.op("sp", lambda e: e.dma_start(out=rS[b], in_=d_rotS[j * 512:(j + 1) * 512, :].rearrange("(c p) n -> p c n", p=128)),
                     writes=[("rS", b)], dma="rot%d" % b)

            load_rot(0)
            PJ, TQ, TK, SC, OO, KV, TR = 0, 1, 2, 3, 4, 5, 6
            for j in range(ntiles):
                b = j % 2
                if j + 1 < ntiles:
                    load_U(j + 1)
                    load_rot(j + 1)
                rd = [("U", b, 0), ("U", b, 1), ("U", b, 2), ("U", b, 3), ("w", 0), ("w", 1)]
                for ci in range(4):
                    n = j * 4 + ci
                    for c in range(16):
                        P.op("pe", lambda e, c=c, ci=ci, b=b: e.matmul(ps[PJ][:, 0:384], lhsT=U[b][:, c, ci * 128:(ci + 1) * 128],
                                                                       rhs=wB[:, c, :], start=(c == 0), stop=(c == 15)),
                             reads=rd, writes=[("ps", PJ)])
                    P.op("dve", lambda e, ci=ci, b=b: e.tensor_tensor(out=t1, in0=ps[PJ][:, 0:128], in1=rC[b][:, ci, :], op=ALU.mult),
                         reads=[("ps", PJ), ("rC", b)], writes=["t1"])
                    P.op("dve", lambda e, ci=ci, b=b: e.tensor_tensor(out=t2[:, 0:128:2], in0=ps[PJ][:, 1:128:2],
                                                                      in1=rS[b][:, ci, 0:128:2], op=ALU.mult),
                         reads=[("ps", PJ), ("rS", b)], writes=["t2e"])
                    P.op("dve", lambda e, ci=ci, b=b: e.tensor_tensor(out=t2[:, 1:128:2], in0=ps[PJ][:, 0:128:2],
                                                                      in1=rS[b][:, ci, 1:128:2], op=ALU.mult),
                         reads=[("ps", PJ), ("rS", b)], writes=["t2o"])
                    P.op("act", lambda e: e.activation(out=v_b, in_=ps[PJ][:, 128:256], func=AF.Copy),
                         reads=[("ps", PJ)], writes=["v_b"])
                    P.op("act", lambda e, ci=ci: e.activation(out=szb[:, ci, :], in_=ps[PJ][:, 256:384], func=AF.Silu),
                         reads=[("ps", PJ)], writes=[("szb", ci)])
                    P.op("dve", lambda e: e.tensor_tensor(out=qk, in0=t1, in1=t2, op=ALU.add), reads=["t1", "t2e", "t2o"],
                         writes=["qk"])
                    P.op("dve", lambda e: e.tensor_copy(kd_bf, qk[:, 64:128]), reads=["qk"], writes=["kd_bf"])
                    P.op("pe", lambda e: e.transpose(ps[TQ][0:64, 0:128], qk[:, 0:64], ident), reads=["qk", "vecs"],
                         writes=[("ps", TQ)])
                    P.op("pe", lambda e: e.transpose(ps[TK][0:64, 0:128], qk[:, 64:128], ident), reads=["qk", "vecs"],
                         writes=[("ps", TK)])
                    P.op("act", lambda e: e.activation(out=qdT, in_=ps[TQ][0:64, 0:128], func=AF.Copy), reads=[("ps", TQ)],
                         writes=["qdT"])
                    P.op("dve", lambda e: e.tensor_copy(kdT, ps[TK][0:64, 0:128]), reads=[("ps", TK)], writes=["kdT"])
                    P.op("pe", lambda e: e.matmul(ps[SC][:, 0:128], lhsT=kdT, rhs=qdT, start=True, stop=True),
                         reads=["kdT", "qdT"], writes=[("ps", SC)])
                    P.op("dve", lambda e: e.tensor_tensor(out=pTb, in0=ps[SC][:, 0:128], in1=tri_f, op=ALU.mult),
                         reads=[("ps", SC), "vecs"], writes=["pTb"])
                    P.op("pe", lambda e, n=n: e.matmul(ps[OO][:, 0:128], lhsT=pTb, rhs=v_b, start=True, stop=(n == 0)),
                         reads=["pTb", "v_b"], writes=[("ps", OO)])
                    if n > 0:
                        P.op("pe", lambda e: e.matmul(ps[OO][:, 0:128], lhsT=qdT, rhs=Rbf, start=False, stop=True),
                             reads=["qdT", "Rbf"], writes=[("ps", OO)])
                    P.op("act", lambda e, ci=ci: e.activation(out=ob[:, ci, :], in_=ps[OO][:, 0:128], func=AF.Copy),
                         reads=[("ps", OO)], writes=[("ob", ci)])
                    P.op("pe", lambda e: e.matmul(ps[KV][0:64, 0:128], lhsT=kd_bf, rhs=v_b, start=True, stop=True),
                         reads=["kd_bf", "v_b"], writes=[("ps", KV)])
                    if n == 0:
                        P.op("dve", lambda e: e.tensor_scalar(out=Sst, in0=ps[KV][0:64, 0:128], scalar1=gcol, scalar2=None,
                                                              op0=ALU.mult), reads=[("ps", KV), "vecs"], writes=["Sst"])
                    else:
                        P.op("dve", lambda e: e.tensor_tensor(out=tmpS, in0=ps[KV][0:64, 0:128], in1=Sst, op=ALU.add),
                             reads=[("ps", KV), "Sst"], writes=["tmpS"])
                        P.op("dve", lambda e: e.tensor_scalar(out=Sst, in0=tmpS, scalar1=gcol, scalar2=None, op0=ALU.mult),
                             reads=["tmpS", "vecs"], writes=["Sst"])
                    P.op("dve", lambda e: e.tensor_copy(Rbf, Sst), reads=["Sst"], writes=["Rbf"])
                    P.op("dve", lambda e, ci=ci: e.scalar_tensor_tensor(out=junkb, in0=ob[:, ci, :], scalar=1.0, in1=ob[:, ci, :],
                                                                        op0=ALU.mult, op1=ALU.mult, accum_out=ssq[:, ci:ci + 1]),
                         reads=[("ob", ci)], writes=["junkb", ("ssq", ci)])
                P.op("act", lambda e: e.activation(out=var, in_=ssq, func=AF.Sqrt, scale=1.0 / 128, bias=epsc),
                     reads=[("ssq", 0), ("ssq", 1), ("ssq", 2), ("ssq", 3), "epsc"], writes=["var"])
                P.op("dve", lambda e: e.reciprocal(rstd, var), reads=["var"], writes=["rstd"])
                for ci in range(4):
                    P.op("dve", lambda e, ci=ci: e.scalar_tensor_tensor(out=ob[:, ci, :], in0=ob[:, ci, :], scalar=rstd[:, ci:ci + 1],
                                                                        in1=rnw, op0=ALU.mult, op1=ALU.mult),
                         reads=[("ob", ci), "rstd", "vecs"], writes=[("ob", ci)])
                    P.op("dve", lambda e, ci=ci: e.tensor_tensor(out=ozbB[:, ci, :], in0=ob[:, ci, :], in1=szb[:, ci, :], op=ALU.mult),
                         reads=[("ob", ci), ("szb", ci)], writes=[("ozbB", ci)])
                for ci in range(4):
                    P.op("pe", lambda e, ci=ci: e.transpose(ps[TR][:, ci * 128:(ci + 1) * 128], ozbB[:, ci, :], ident),
                         reads=[("ozbB", ci), "vecs"], writes=[("ps", TR)])

                def fill(st, res):
                    P.op("act", lambda e: e.activation(out=st, in_=ps[TR][:, :], func=AF.Copy), reads=[("ps", TR)], writes=[res])
                store_oz(ozstB, 128, j, fill)
            P.barrier()
        if "B" in passes:
            pass_b()

        def pass_c():
            A.off = mark
            wC = A.alloc([128, 16, 512], BF16)
            qTs = A.alloc([128, 2048], BF16)
            kTs = A.alloc([128, S], BF16)
            vTs = A.alloc([128, 2048], F32)
            Vs = A.alloc([128, 64, 128], BF16)
            accO = A.alloc([128, S], F32)
            accZ = A.alloc([128, S], F32)
            szT = A.alloc([128, 2048], F32)
            NPC = 3
            pTc = [A.alloc([128, 2, 2, 128], BF16) for _ in range(NPC)]
            rz = A.alloc([128, 512], F32)
            ozstC = [A.alloc([128, 512], BF16), A.alloc([128, 512], BF16)]
            PJc = [0, 1]
            SCc = [2, 3]
            PO, PZ, TRc = 4, 5, 6
            cntc = {"pj": 0, "sc": 0}
            SCALE_C = 128.0 ** -0.5

            for g in range(3):
                d = DIL[g]
                Ls = 128 * d
                ncols = 512 if g == 2 else 384
                c0 = WC0 + g * 384
                if g == 2:
                    pass
                load_w(wC[:, :, 0:ncols], c0, ncols)
                load_U(0)
                tiles_per_span = max(1, Ls // 512)
                for j in range(ntiles):
                    b = j % 2
                    if j + 1 < ntiles:
                        load_U(j + 1)
                    rd = [("U", b, 0), ("U", b, 1), ("U", b, 2), ("U", b, 3), ("w", 0), ("w", 1)]
                    soff = (j * 512) % Ls
                    for which in range(ncols // 128):
                        pb = PJc[cntc["pj"] % 2]
                        cntc["pj"] += 1
                        for c in range(16):
                            P.op("pe", lambda e, pb=pb, c=c, b=b, which=which: e.matmul(
                                ps[pb][:, :], lhsT=wC[:, c, which * 128:(which + 1) * 128], rhs=U[b][:, c, :],
                                start=(c == 0), stop=(c == 15)), reads=rd, writes=[("ps", pb)])
                        if which == 3:
                            P.op("act", lambda e, pb=pb, soff=soff: e.activation(out=szT[:, soff:soff + 512], in_=ps[pb][:, :],
                                                                                func=AF.Silu), reads=[("ps", pb)], writes=[("szT", j % 4)])
                            continue
                        if which == 0:
                            dst_buf, dbase, res = qTs, 0, ("qTs", j % tiles_per_span)
                        elif which == 1:
                            dst_buf, dbase, res = kTs, (j * 512) // Ls * Ls, ("kTs", j)
                        else:
                            dst_buf, dbase, res = vTs, 0, ("vTs", j % tiles_per_span)
                        if d == 1:
                            dst = dst_buf[:, dbase + soff:dbase + soff + 512] if which == 1 else dst_buf[:, 0:512]
                            src = ps[pb][:, :]
                        elif d == 4:
                            dst = dst_buf[:, dbase:dbase + 512].rearrange("p (r i) -> p r i", r=4)
                            src = ps[pb][:, :].rearrange("p (i r) -> p r i", r=4)
                        else:
                            q4 = soff // 512
                            dst = dst_buf[:, dbase:dbase + 2048].rearrange("p (r i) -> p r i", r=16)[:, :, 32 * q4:32 * q4 + 32]
                            src = ps[pb][:, :].rearrange("p (i r) -> p r i", r=16)
                        if which == 1:
                            P.op("act", lambda e, dst=dst, src=src: e.activation(out=dst, in_=src, func=AF.Copy),
                                 reads=[("ps", pb)], writes=[res])
                        else:
                            P.op("dve", lambda e, dst=dst, src=src: e.tensor_copy(dst, src), reads=[("ps", pb)], writes=[res])
                    if (j * 512 + 512) % Ls != 0 and Ls > 512:
                        continue
                    if d == 1:
                        spans = [j * 4 + q for q in range(4)]
                    else:
                        spans = [(j * 512) // Ls]
                    blocks = []
                    for n in spans:
                        for r in range(d):
                            blk = n * d + r
                            if d == 1:
                                lpos = (n - j * 4) * 128
                            else:
                                lpos = r * 128
                            blocks.append((n, r, blk, lpos))
                    tl = [j - t_ for t_ in range(tiles_per_span)]
                    for gi in range(0, len(blocks), 4):
                        grp = blocks[gi:gi + 4]
                        for k_, (n, r, blk, lpos) in enumerate(grp):
                            P.op("pe", lambda e, k_=k_, lpos=lpos: e.transpose(ps[TRc][:, k_ * 128:(k_ + 1) * 128],
                                                                              vTs[:, lpos:lpos + 128], ident),
                                 reads=[("vTs", t_ % tiles_per_span) for t_ in tl] + ["vecs"], writes=[("ps", TRc)])
                        blk0 = grp[0][2]
                        P.op("dve", lambda e, blk0=blk0, ng=len(grp): e.tensor_copy(
                            Vs[:, blk0:blk0 + ng, :], ps[TRc][:, 0:ng * 128].rearrange("p (a b) -> p a b", a=ng)),
                            reads=[("ps", TRc)], writes=[("Vs", blk0 // 4)])
                    for pi in range(0, len(blocks), 2):
                        pair = blocks[pi:pi + 2]
                        firsts = [p_[0] == 0 for p_ in pair]
                        sb_ = SCc[cntc["sc"] % 2]
                        slot = cntc["sc"] % NPC
                        cntc["sc"] += 1
                        qres = [("qTs", t_ % tiles_per_span) for t_ in tl]
                        for u, (n, r, blk, lpos) in enumerate(pair):
                            if not firsts[u]:
                                P.op("pe", lambda e, u=u, blk=blk, lpos=lpos, sb_=sb_, d=d: e.matmul(
                                    ps[sb_][:, u * 256:u * 256 + 128], lhsT=kTs[:, (blk - d) * 128:(blk - d + 1) * 128],
                                    rhs=qTs[:, lpos:lpos + 128], start=True, stop=True),
                                    reads=qres + [("kTs", ((blk - d) * 128) // 512)], writes=[("ps", sb_)])
                            P.op("pe", lambda e, u=u, blk=blk, lpos=lpos, sb_=sb_: e.matmul(
                                ps[sb_][:, u * 256 + 128:u * 256 + 256], lhsT=kTs[:, blk * 128:(blk + 1) * 128],
                                rhs=qTs[:, lpos:lpos + 128], start=True, stop=True),
                                reads=qres + [("kTs", t_) for t_ in tl], writes=[("ps", sb_)])
                        psv = ps[sb_][:, :].rearrange("p (u h q) -> p u h q", u=2, h=2)
                        if not any(firsts):
                            P.op("act", lambda e, psv=psv, slot=slot: e.activation(out=pTc[slot], in_=psv, func=AF.Exp, scale=SCALE_C),
                                 reads=[("ps", sb_)], writes=[("pTc", slot)])
                            P.op("dve", lambda e, slot=slot: e.tensor_tensor(out=pTc[slot], in0=pTc[slot], in1=mask_c, op=ALU.mult),
                                 reads=[("pTc", slot), "consts"], writes=[("pTc", slot)])
                        else:
                            for u in range(len(pair)):
                                if firsts[u]:
                                    P.op("act", lambda e, psv=psv, slot=slot, u=u: e.activation(
                                        out=pTc[slot][:, u, 1, :], in_=psv[:, u, 1, :], func=AF.Exp, scale=SCALE_C),
                                        reads=[("ps", sb_)], writes=[("pTc", slot)])
                                    P.op("dve", lambda e, slot=slot, u=u: e.tensor_tensor(
                                        out=pTc[slot][:, u, 1, :], in0=pTc[slot][:, u, 1, :], in1=mask_c[:, u, 1, :], op=ALU.mult),
                                        reads=[("pTc", slot), "consts"], writes=[("pTc", slot)])
                                else:
                                    P.op("act", lambda e, psv=psv, slot=slot, u=u: e.activation(
                                        out=pTc[slot][:, u, :, :], in_=psv[:, u, :, :], func=AF.Exp, scale=SCALE_C),
                                        reads=[("ps", sb_)], writes=[("pTc", slot)])
                                    P.op("dve", lambda e, slot=slot, u=u: e.tensor_tensor(
                                        out=pTc[slot][:, u, :, :], in0=pTc[slot][:, u, :, :], in1=mask_c[:, u, :, :], op=ALU.mult),
                                        reads=[("pTc", slot), "consts"], writes=[("pTc", slot)])
                        for u, (n, r, blk, lpos) in enumerate(pair):
                            vres = [("Vs", blk // 4), ("Vs", max(0, blk - d) // 4)]
                            if not firsts[u]:
                                P.op("pe", lambda e, u=u, blk=blk, slot=slot, d=d: e.matmul(
                                    ps[PO][:, u * 128:(u + 1) * 128], lhsT=Vs[:, blk - d, :], rhs=pTc[slot][:, u, 0, :],
                                    start=True, stop=False), reads=[("pTc", slot)] + vres, writes=[("ps", PO)])
                            P.op("pe", lambda e, u=u, blk=blk, slot=slot, first=firsts[u]: e.matmul(
                                ps[PO][:, u * 128:(u + 1) * 128], lhsT=Vs[:, blk, :], rhs=pTc[slot][:, u, 1, :],
                                start=first, stop=True), reads=[("pTc", slot)] + vres, writes=[("ps", PO)])
                        for u, (n, r, blk, lpos) in enumerate(pair):
                            if not firsts[u]:
                                P.op("pe", lambda e, u=u, slot=slot: e.matmul(
                                    ps[PZ][:, u * 128:(u + 1) * 128], lhsT=ones_bf, rhs=pTc[slot][:, u, 0, :],
                                    start=True, stop=False), reads=[("pTc", slot), "consts"], writes=[("ps", PZ)])
                            P.op("pe", lambda e, u=u, slot=slot, first=firsts[u]: e.matmul(
                                ps[PZ][:, u * 128:(u + 1) * 128], lhsT=ones_bf, rhs=pTc[slot][:, u, 1, :],
                                start=first, stop=True), reads=[("pTc", slot), "consts"], writes=[("ps", PZ)])
                        n0, r0 = pair[0][0], pair[0][1]
                        if d == 1:
                            t0_ = n0 * 128
                            vO = accO[:, t0_:t0_ + 256]
                            vZ = accZ[:, t0_:t0_ + 256]
                            sO = ps[PO][:, 0:256]
                            sZ = ps[PZ][:, 0:256]
                        else:
                            vO = accO[:, n0 * Ls:(n0 + 1) * Ls].rearrange("p (i r) -> p r i", r=d)[:, r0:r0 + 2, :]
                            vZ = accZ[:, n0 * Ls:(n0 + 1) * Ls].rearrange("p (i r) -> p r i", r=d)[:, r0:r0 + 2, :]
                            sO = ps[PO][:, 0:256].rearrange("p (u q) -> p u q", u=2)
                            sZ = ps[PZ][:, 0:256].rearrange("p (u q) -> p u q", u=2)
                        tt = (n0 * Ls) // 512
                        ares = [("acc", tt + t_) for t_ in range(tiles_per_span)]
                        if g == 0:
                            P.op("dve", lambda e, vO=vO, sO=sO: e.tensor_copy(vO, sO), reads=[("ps", PO)], writes=ares)
                            P.op("act", lambda e, vZ=vZ, sZ=sZ: e.activation(out=vZ, in_=sZ, func=AF.Copy), reads=[("ps", PZ)],
                                 writes=[("accz", a_[1]) for a_ in ares])
                        else:
                            P.op("dve", lambda e, vO=vO, sO=sO: e.tensor_tensor(out=vO, in0=vO, in1=sO, op=ALU.add),
                                 reads=[("ps", PO)] + ares, writes=ares)
                            P.op("dve", lambda e, vZ=vZ, sZ=sZ: e.tensor_tensor(out=vZ, in0=vZ, in1=sZ, op=ALU.add),
                                 reads=[("ps", PZ)] + [("accz", a_[1]) for a_ in ares], writes=[("accz", a_[1]) for a_ in ares])
                    if g == 2:
                        n0 = (j * 512) // Ls
                        for q4 in range(4):
                            tt = n0 * 4 + q4
                            if tt >= ntiles:
                                continue
                            sl = slice(tt * 512, (tt + 1) * 512)

                            def fill(st, res, sl=sl, q4=q4, tt=tt):
                                P.op("dve", lambda e: e.reciprocal(rz, accZ[:, sl]), reads=[("accz", tt)], writes=["rz"])
                                P.op("dve", lambda e: e.tensor_tensor(out=rz, in0=rz, in1=accO[:, sl], op=ALU.mult),
                                     reads=["rz", ("acc", tt)], writes=["rz"])
                                P.op("dve", lambda e: e.tensor_tensor(out=st, in0=rz, in1=szT[:, q4 * 512:(q4 + 1) * 512], op=ALU.mult),
                                     reads=["rz", ("szT", q4)], writes=[res])
                            store_oz(ozstC, 256, tt, fill)
                P.barrier()
        if "C" in passes:
            pass_c()

        P.emit()
    return nc


def _const_tables():
    p = np.arange(128)
    tri = (p[None, :] >= p[:, None]).astype(np.float32)
    mprev = (p[:, None] >= p[None, :]).astype(np.float32)
    return tri, mprev, np.eye(128, dtype=np.float32)


def _rot_tables(h):
    dk = 64
    theta = 1.0 / (10000.0 ** np.linspace(0.0, 1.0, dk // 2, dtype=np.float32)).astype(np.float32)
    pos = np.arange(S, dtype=np.float32)
    ang = (pos[:, None] * theta[None, :]).astype(np.float32).astype(np.float64)
    cos = np.repeat(np.cos(ang), 2, axis=-1)
    sin = np.repeat(np.sin(ang), 2, axis=-1)
    sgn = np.tile(np.array([-1.0, 1.0]), dk // 2)[None, :]
    log_g = np.log1p(-np.exp2(-5.0 - float(h)))
    i = (np.arange(S) % 128).astype(np.float64)
    qdec = np.exp((i + 1.0) * log_g)[:, None]
    kdec = np.exp(-(i + 1.0) * log_g)[:, None] * dk ** -0.5
    C = np.concatenate([cos * qdec, cos * kdec], axis=1).astype(np.float32)
    Sg = np.concatenate([sin * sgn * qdec, sin * sgn * kdec], axis=1).astype(np.float32)
    g128 = float(np.exp(128.0 * log_g))
    return C, Sg, g128


def _mix_weight_cols(h):
    cols = []
    cols += list(range(O_AQ + h * 128, O_AQ + (h + 1) * 128))
    cols += list(range(O_AK + h * 128, O_AK + (h + 1) * 128))
    cols += list(range(O_AV + h * 128, O_AV + (h + 1) * 128))
    cols += list(range(O_AZ + h * 128, O_AZ + (h + 1) * 128))
    cols += list(range(O_BQ + h * 64, O_BQ + (h + 1) * 64))
    cols += list(range(O_BK + h * 64, O_BK + (h + 1) * 64))
    cols += list(range(O_BV + h * 128, O_BV + (h + 1) * 128))
    cols += list(range(O_BZ + h * 128, O_BZ + (h + 1) * 128))
    for g in range(3):
        base = g * 1024 + h * 128
        cols += list(range(O_CQ + base, O_CQ + base + 128))
        cols += list(range(O_CK + base, O_CK + base + 128))
        cols += list(range(O_CV + base, O_CV + base + 128))
    cols += list(range(O_CZ + h * 128, O_CZ + (h + 1) * 128))
    return np.array(cols)


def mix_inputs(h, w_in_l, lam_l, da_nw_l, ret_nw_l):
    tri, mprev, ident = _const_tables()
    C, Sg, g128 = _rot_tables(h)
    vecs = np.zeros((128, NV_TOT), np.float32)
    vecs[:, NV_LAM:NV_LAM + 256] = lam_l.reshape(1, 256)
    vecs[:, NV_DNW:NV_DNW + 128] = da_nw_l.reshape(1, 128)
    vecs[:, NV_RNW:NV_RNW + 128] = ret_nw_l.reshape(1, 128)
    vecs[:, NV_TRI:NV_TRI + 128] = tri
    vecs[:, NV_MPREV:NV_MPREV + 128] = mprev
    vecs[:, NV_IDENT:NV_IDENT + 128] = ident
    vecs[:, NV_G] = g128
    w = np.ascontiguousarray(w_in_l[:, _mix_weight_cols(h)])
    return dict(w_mix=w, vecs=vecs, rotC=C, rotS=Sg)


def build_merge(final):
    nc = bass.Bass("TRN2", target_bir_lowering=False)
    d_x = nc.dram_tensor("x", [TOK, D], F32, kind="ExternalInput").ap()
    d_uT = nc.dram_tensor("uT", [D, TOK], BF16, kind="ExternalInput").ap()
    d_oz = nc.dram_tensor("ozT", [3072, TOK], BF16, kind="ExternalInput").ap()
    d_wg = nc.dram_tensor("w_g", [D, 3 * D], F32, kind="ExternalInput").ap()
    d_wp = nc.dram_tensor("w_p", [3, 1024, D], F32, kind="ExternalInput").ap()
    d_wo = nc.dram_tensor("w_out", [D, D], F32, kind="ExternalInput").ap()
    d_cT = nc.dram_tensor("cT", [128, 16], F32, kind="ExternalInput").ap()
    d_wada = nc.dram_tensor("w_ada", [D, 3 * D], F32, kind="ExternalInput").ap()
    d_brow = nc.dram_tensor("brow", [1, 3 * D], F32, kind="ExternalInput").ap()
    d_fnw = nc.dram_tensor("fnw", [128, D], F32, kind="ExternalInput").ap()
    d_out = nc.dram_tensor("out", [TOK, D], F32, kind="ExternalOutput").ap()
    uT_v = d_uT.rearrange("(c p) t -> p c t", p=128)
    oz_v = d_oz.rearrange("(c p) t -> p c t", p=128)
    wg_v = d_wg.rearrange("(c p) n -> p c n", p=128)
    wo_v = d_wo.rearrange("(c p) n -> p c n", p=128)
    with ExitStack() as es:
        arena_t = es.enter_context(nc.sbuf_tensor("arena", [128, 95 * 1024], BF16))
        A = Arena(arena_t[:, :], 190 * 1024)
        ps = [es.enter_context(nc.psum_tensor("ps%d" % i, [128, 512], F32)) for i in range(8)]
        P = Prog(nc, ["misc", "wk0", "wk1", "uT", "oz", "wg0", "wg1", "wp0", "wp1", "wo", "x0", "x1", "out0", "out1"])
        ones = A.alloc([128, 128], F32)
        gate_bc = A.alloc([128, D], F32)
        epsc = A.alloc([128, 1], F32)
        mT = A.alloc([128, 16, TOK], BF16)
        P.op("dve", lambda e: e.memset(ones, 1.0), writes=["ones"])
        P.op("dve", lambda e: e.memset(epsc, EPS), writes=["epsc"])
        mark = A.off
        uT = A.alloc([128, 16, TOK], BF16)
        oz = A.alloc([128, 24, TOK], BF16)
        wg = [A.alloc([128, 16, 512], BF16), A.alloc([128, 16, 512], BF16)]
        wp = [A.alloc([128, 8, 512], BF16), A.alloc([128, 8, 512], BF16)]
        mark2 = A.off
        for q in range(4):
            P.op("sp", lambda e, q=q: e.dma_start(out=uT[:, q * 4:(q + 1) * 4, :], in_=uT_v[:, q * 4:(q + 1) * 4, :]),
                 writes=[("uT", q)], dma="uT")
        for q in range(6):
            P.op("sp", lambda e, q=q: e.dma_start(out=oz[:, q * 4:(q + 1) * 4, :], in_=oz_v[:, q * 4:(q + 1) * 4, :]),
                 writes=[("oz", q)], dma="oz")
        for hf in range(2):
            A.off = mark2
            acc = emit_mod_acc(P, A, d_wada, d_cT, d_brow, cols=(4096 + 1024 * hf, 5120 + 1024 * hf))
            for n2 in range(2):
                n = hf * 2 + n2
                P.op("pe", lambda e, n=n, n2=n2, acc=acc: e.matmul(ps[4 + n][:, :], lhsT=ones, rhs=acc[:, n2 * 512:(n2 + 1) * 512],
                                                                   start=True, stop=True),
                     reads=["acc", "ones"], writes=[("ps", 4 + n)])
                P.op("dve", lambda e, n=n: e.tensor_copy(gate_bc[:, n * 512:(n + 1) * 512], ps[4 + n][:, :]),
                     reads=[("ps", 4 + n)], writes=["gate_bc"])
        A.off = mark2
        accm = A.alloc([128, 4, TOK], F32)
        sig = [A.alloc([128, 512], F32), A.alloc([128, 512], F32)]
        tmp = sig
        units = [(ct, b) for ct in range(4) for b in range(3)]

        def load_unit(i):
            ct, b = units[i]
            sl = i % 2
            for half in range(2):
                P.op("pool", lambda e, half=half: e.dma_start(
                    out=wg[sl][:, half * 8:(half + 1) * 8, :],
                    in_=wg_v[:, half * 8:(half + 1) * 8, b * D + ct * 512:b * D + (ct + 1) * 512]),
                    writes=[("wg", sl, half)], dma="wg%d" % sl)
            P.op("pool", lambda e: e.dma_start(out=wp[sl], in_=d_wp[b, :, ct * 512:(ct + 1) * 512].rearrange("(c p) n -> p c n", p=128)),
                 writes=[("wp", sl)], dma="wp%d" % sl)

        load_unit(0)
        cnt = {"g": 0}
        for i, (ct, b) in enumerate(units):
            if i + 1 < len(units):
                load_unit(i + 1)
            sl = i % 2
            for m in range(4):
                for nt in range(2):
                    k_ = cnt["g"] % 2
                    cnt["g"] += 1
                    G, Y = k_, 2 + k_
                    for c in range(16):
                        P.op("pe", lambda e, c=c, m=m, nt=nt, G=G, sl=sl: e.matmul(
                            ps[G][:, :], lhsT=wg[sl][:, c, m * 128:(m + 1) * 128], rhs=uT[:, c, nt * 512:(nt + 1) * 512],
                            start=(c == 0), stop=(c == 15)),
                            reads=[("wg", sl, 0), ("wg", sl, 1)] + [("uT", q) for q in range(4)], writes=[("ps", G)])
                    for kc in range(8):
                        P.op("pe", lambda e, kc=kc, m=m, nt=nt, Y=Y, sl=sl, b=b: e.matmul(
                            ps[Y][:, :], lhsT=wp[sl][:, kc, m * 128:(m + 1) * 128], rhs=oz[:, b * 8 + kc, nt * 512:(nt + 1) * 512],
                            start=(kc == 0), stop=(kc == 7)),
                            reads=[("wp", sl), ("oz", 2 * b), ("oz", 2 * b + 1)], writes=[("ps", Y)])
                    P.op("act", lambda e, G=G, k_=k_: e.activation(out=sig[k_], in_=ps[G][:, :], func=AF.Sigmoid),
                         reads=[("ps", G)], writes=[("sig", k_)])
                    av = accm[:, m, nt * 512:(nt + 1) * 512]
                    ares = ("accm", m, nt)
                    if b == 0:
                        P.op("dve", lambda e, Y=Y, k_=k_, av=av: e.tensor_tensor(out=av, in0=ps[Y][:, :], in1=sig[k_], op=ALU.mult),
                             reads=[("ps", Y), ("sig", k_)], writes=[ares])
                    else:
                        P.op("dve", lambda e, Y=Y, k_=k_: e.tensor_tensor(out=tmp[k_], in0=ps[Y][:, :], in1=sig[k_], op=ALU.mult),
                             reads=[("ps", Y), ("sig", k_)], writes=[("sig", k_)])
                        if b == 1:
                            P.op("dve", lambda e, k_=k_, av=av: e.tensor_tensor(out=av, in0=av, in1=tmp[k_], op=ALU.add),
                                 reads=[("sig", k_), ares], writes=[ares])
                        else:
                            mv = mT[:, ct * 4 + m, nt * 512:(nt + 1) * 512]
                            P.op("dve", lambda e, k_=k_, av=av, mv=mv: e.tensor_tensor(out=mv, in0=av, in1=tmp[k_], op=ALU.add),
                                 reads=[("sig", k_), ares], writes=[("mT", ct)])
        P.barrier()
        A.off = mark
        wo = A.alloc([128, 16, D], BF16)
        xr = [A.alloc([128, D], F32), A.alloc([128, D], F32)]
        orow = [A.alloc([128, D], F32), A.alloc([128, D], F32)]
        junk = A.alloc([128, 512], F32)
        ssq = A.alloc([128, 8], F32)
        var = A.alloc([128, 2], F32)
        rstd = A.alloc([128, 2], F32)
        fnw = A.alloc([128, D], F32)
        for ct2 in range(4):
            for half in range(2):
                P.op("pool", lambda e, ct2=ct2, half=half: e.dma_start(
                    out=wo[:, half * 8:(half + 1) * 8, ct2 * 512:(ct2 + 1) * 512],
                    in_=wo_v[:, half * 8:(half + 1) * 8, ct2 * 512:(ct2 + 1) * 512]), writes=[("wo", ct2, half)], dma="wo")
        if final:
            P.op("sp", lambda e: e.dma_start(out=fnw, in_=d_fnw), writes=["fnw"], dma="misc")
        for t in range(8):
            xb = t % 2
            P.op("sp", lambda e, t=t, xb=xb: e.dma_start(out=xr[xb], in_=d_x[t * 128:(t + 1) * 128, :]), writes=[("xr", xb)],
                 dma="x%d" % xb)
            for ct2 in range(4):
                ob_ = 4 + ct2
                for c in range(16):
                    P.op("pe", lambda e, c=c, t=t, ct2=ct2, ob_=ob_: e.matmul(
                        ps[ob_][:, :], lhsT=mT[:, c, t * 128:(t + 1) * 128], rhs=wo[:, c, ct2 * 512:(ct2 + 1) * 512],
                        start=(c == 0), stop=(c == 15)),
                        reads=[("wo", ct2, 0), ("wo", ct2, 1)] + [("mT", q) for q in range(4)], writes=[("ps", ob_)])
                cs = slice(ct2 * 512, (ct2 + 1) * 512)
                P.op("dve", lambda e, ob_=ob_, cs=cs, xb=xb: e.tensor_tensor(out=orow[xb][:, cs], in0=ps[ob_][:, :], in1=gate_bc[:, cs],
                                                                            op=ALU.mult),
                     reads=[("ps", ob_), "gate_bc"], writes=[("orow", xb, ct2)])
                P.op("dve", lambda e, cs=cs, xb=xb: e.tensor_tensor(out=orow[xb][:, cs], in0=orow[xb][:, cs], in1=xr[xb][:, cs], op=ALU.add),
                     reads=[("orow", xb, ct2), ("xr", xb)], writes=[("orow", xb, ct2)])
                if final:
                    P.op("dve", lambda e, cs=cs, xb=xb, ct2=ct2: e.scalar_tensor_tensor(
                        out=junk, in0=orow[xb][:, cs], scalar=1.0, in1=orow[xb][:, cs], op0=ALU.mult, op1=ALU.mult,
                        accum_out=ssq[:, xb * 4 + ct2:xb * 4 + ct2 + 1]),
                        reads=[("orow", xb, ct2)], writes=["junk", ("ssq", xb, ct2)])
            ores = [("orow", xb, q) for q in range(4)]
            if final:
                P.op("dve", lambda e, xb=xb: e.tensor_tensor(out=ssq[:, xb * 4:xb * 4 + 2], in0=ssq[:, xb * 4:xb * 4 + 2],
                                                             in1=ssq[:, xb * 4 + 2:xb * 4 + 4], op=ALU.add),
                     reads=[("ssq", xb, q) for q in range(4)], writes=[("ssq", xb, 0), ("ssq", xb, 1)])
                P.op("dve", lambda e, xb=xb: e.tensor_tensor(out=ssq[:, xb * 4:xb * 4 + 1], in0=ssq[:, xb * 4:xb * 4 + 1],
                                                             in1=ssq[:, xb * 4 + 1:xb * 4 + 2], op=ALU.add),
                     reads=[("ssq", xb, 0), ("ssq", xb, 1)], writes=[("ssq", xb, 0)])
                P.op("act", lambda e, xb=xb: e.activation(out=var[:, xb:xb + 1], in_=ssq[:, xb * 4:xb * 4 + 1], func=AF.Sqrt,
                                                          scale=1.0 / D, bias=epsc), reads=[("ssq", xb, 0), "epsc"], writes=[("var", xb)])
                P.op("dve", lambda e, xb=xb: e.reciprocal(rstd[:, xb:xb + 1], var[:, xb:xb + 1]), reads=[("var", xb)],
                     writes=[("rstd", xb)])
                P.op("dve", lambda e, xb=xb: e.scalar_tensor_tensor(out=orow[xb], in0=orow[xb], scalar=rstd[:, xb:xb + 1], in1=fnw,
                                                                    op0=ALU.mult, op1=ALU.mult),
                     reads=ores + [("rstd", xb), "fnw"], writes=ores)
            P.op("sp", lambda e, t=t, xb=xb: e.dma_start(out=d_out[t * 128:(t + 1) * 128, :], in_=orow[xb]), reads=ores,
                 dma="out%d" % xb)
        P.emit()
    return nc


def merge_inputs(x_own, uT_own, ozT_own, layer, inp):
    bf = _bf16_np()
    return dict(
        x=np.ascontiguousarray(x_own, dtype=np.float32), uT=np.ascontiguousarray(uT_own).astype(bf),
        ozT=np.ascontiguousarray(ozT_own).astype(bf),
        w_g=np.ascontiguousarray(inp["w_in"][layer][:, O_GA:O_GA + 3 * D]),
        w_p=np.ascontiguousarray(np.stack([inp["w_proj_a"][layer], inp["w_proj_b"][layer], inp["w_proj_c"][layer]])),
        w_out=np.ascontiguousarray(inp["w_out"][layer]),
        cT=np.ascontiguousarray(inp["c"].reshape(16, 128).T), w_ada=np.ascontiguousarray(inp["w_ada"][layer]),
        brow=np.ascontiguousarray(inp["b_ada"][layer].reshape(1, -1)),
        fnw=np.ascontiguousarray(np.broadcast_to(inp["final_norm_w"].reshape(1, D), (128, D))),
    )


_PROGS = {}
_DBG = None


def _prog(key, builder):
    if key not in _PROGS:
        _PROGS[key] = builder()
    return _PROGS[key]


def _run(nc, in_maps):
    return run_bass_kernel_spmd(nc, in_maps, core_ids=list(range(NCORES))).results


def kernel(x, c, w_in, w_proj_a, w_proj_b, w_proj_c, w_out, w_ada, b_ada, norm_w, lam, da_norm_w, ret_norm_w,
           final_norm_w):
    inp = dict(x=x, c=c, w_in=w_in, w_proj_a=w_proj_a, w_proj_b=w_proj_b, w_proj_c=w_proj_c, w_out=w_out,
               w_ada=w_ada, b_ada=b_ada, norm_w=norm_w, lam=lam, da_norm_w=da_norm_w, ret_norm_w=ret_norm_w,
               final_norm_w=final_norm_w)
    inp = {k: np.asarray(v, dtype=np.float32) for k, v in inp.items()}
    h = np.ascontiguousarray(inp["x"].reshape(S, D))
    cT = np.ascontiguousarray(inp["c"].reshape(16, 128).T)
    ident = np.eye(128, dtype=np.float32)
    nlayers = inp["w_in"].shape[0]
    for L in range(nlayers):
        nc = _prog("prep", build_prep)
        maps = [dict(x=np.ascontiguousarray(h[i * TOK:(i + 1) * TOK]), cT=cT, w_ada=np.ascontiguousarray(inp["w_ada"][L]),
                     brow=np.ascontiguousarray(inp["b_ada"][L].reshape(1, -1)),
                     nwT=np.ascontiguousarray(inp["norm_w"][L].reshape(16, 128).T), ident=ident) for i in range(NCORES)]
        res = _run(nc, maps)
        uT_own = [np.asarray(res[i]["uT"]) for i in range(NCORES)]
        uT_full = np.ascontiguousarray(np.concatenate(uT_own, axis=1))
        if _DBG is not None:
            _DBG[('uT', L)] = uT_full
        nc = _prog(("mix", L), lambda: build_mix(L))
        maps = []
        for hd in range(NCORES):
            m = mix_inputs(hd, inp["w_in"][L], inp["lam"][L], inp["da_norm_w"][L], inp["ret_norm_w"][L])
            m["uT"] = uT_full
            maps.append(m)
        res = _run(nc, maps)
        ozT = [np.asarray(res[hd]["ozT"]) for hd in range(NCORES)]
        if _DBG is not None:
            _DBG[('ozT', L)] = ozT
        final = (L == nlayers - 1)
        nc = _prog(("merge", final), lambda: build_merge(final))
        maps = []
        for i in range(NCORES):
            ts = slice(i * TOK, (i + 1) * TOK)
            oz_own = np.concatenate([ozT[hd][b * 128:(b + 1) * 128, ts] for b in range(3) for hd in range(NCORES)], axis=0)
            maps.append(merge_inputs(h[ts], uT_own[i], oz_own, L, inp))
        res = _run(nc, maps)
        h = np.ascontiguousarray(np.concatenate([np.asarray(res[i]["out"]) for i in range(NCORES)], axis=0))
        if _DBG is not None:
            _DBG[('h', L)] = h
    return h.reshape(1, S, D).astype(np.float32)
```

```python
import math
from contextlib import ExitStack

import numpy as np
import concourse.bass as bass
import concourse.mybir as mybir
from concourse.bass_utils import run_bass_kernel_spmd

F32 = mybir.dt.float32
BF16 = mybir.dt.bfloat16
AF = mybir.ActivationFunctionType
ALU = mybir.AluOpType

D = 2048
S = 8192
NCORES = 8
TOK = S // NCORES
EPS = 1e-6
IN_SPLITS = (1024, 1024, 1024, 1024, 512, 512, 1024, 1024, 3072, 3072, 3072, 1024, 2048, 2048, 2048)
OFFS = [0] + list(np.cumsum(IN_SPLITS))
(O_AQ, O_AK, O_AV, O_AZ, O_BQ, O_BK, O_BV, O_BZ, O_CQ, O_CK, O_CV, O_CZ, O_GA, O_GB, O_GC) = OFFS[:15]
DIL = (1, 4, 16)
DBG2 = 99
ALL_KEYS = ["misc", "wk0", "wk1", "x0", "x1", "out", "w", "u0", "u1", "rot0", "rot1", "out0", "out1", "uT", "oz", "wg0", "wg1", "wp0", "wp1", "wo", "cc", "hx"]

ENGS = ("pe", "act", "dve", "pool", "sp")


class Prog:
    def __init__(self, nc, dma_keys):
        self.nc = nc
        self.ops = []
        self.last_writer = {}
        self.readers = {}
        self.eng_count = {e: 0 for e in ENGS}
        self.dma_keys = list(dma_keys)
        self.dma_count = {k: 0 for k in self.dma_keys}

    def op(self, eng, fn, reads=(), writes=(), dma=None, extra_waits=None, inc=16):
        ps_r = [r for r in reads if isinstance(r, tuple) and r[0] == "ps"]
        if ps_r:
            reads = [r for r in reads if r not in ps_r]
            writes = list(writes) + ps_r
        deps = set()
        for r in reads:
            w = self.last_writer.get(r)
            if w is not None:
                deps.add(w)
        for r in writes:
            w = self.last_writer.get(r)
            if w is not None:
                deps.add(w)
            for x in self.readers.get(r, ()):
                deps.add(x)
        idx = len(self.ops)
        waits = dict(extra_waits or {})
        for d in deps:
            o = self.ops[d]
            t = o["token"]
            if t[0] == "e":
                if t[1] == "pe" and eng == "pe" and dma is None:
                    continue
                key = ("e", t[1])
                waits[key] = max(waits.get(key, 0), t[2])
            else:
                key = ("d", t[1])
                waits[key] = max(waits.get(key, 0), self.dma_count[t[1]])
        if dma is None:
            self.eng_count[eng] += 1
            token = ("e", eng, self.eng_count[eng])
        else:
            self.dma_count[dma] += inc
            token = ("d", dma, self.dma_count[dma])
        self.ops.append(dict(eng=eng, fn=fn, token=token, waits=waits, dma=dma, inc=inc))
        for r in reads:
            self.readers.setdefault(r, []).append(idx)
        for r in writes:
            self.last_writer[r] = idx
            self.readers[r] = []
        return idx

    def barrier(self):
        snap = {}
        for e in ENGS:
            if self.eng_count[e]:
                snap[("e", e)] = self.eng_count[e]
        for k in self.dma_keys:
            if self.dma_count[k]:
                snap[("d", k)] = self.dma_count[k]
        for e in ENGS:
            self.ops.append(dict(eng=e, fn=None, token=None, waits=dict(snap), dma=None, inc=0))
        self.last_writer = {}
        self.readers = {}

    def emit(self):
        nc = self.nc
        with ExitStack() as es:
            sems = {}
            for e in ENGS:
                sems[("e", e)] = es.enter_context(nc.semaphore("s_" + e))
            for k in self.dma_keys:
                sems[("d", k)] = es.enter_context(nc.semaphore("d_" + str(k)))
            block = es.enter_context(nc.Block())
            per_eng = {e: [o for o in self.ops if o["eng"] == e] for e in ENGS}
            final = {}
            for e in ENGS:
                if self.eng_count[e]:
                    final[("e", e)] = self.eng_count[e]
            for k in self.dma_keys:
                if self.dma_count[k]:
                    final[("d", k)] = self.dma_count[k]

            def run(e, h):
                waited = {}
                for o in per_eng[e]:
                    for key, val in o["waits"].items():
                        if key == ("e", e) and e == "pe":
                            continue
                        if waited.get(key, 0) >= val:
                            continue
                        h.wait_ge(sems[key], val)
                        waited[key] = val
                    if o["fn"] is None:
                        continue
                    ins = o["fn"](h)
                    t = o["token"]
                    if t[0] == "e":
                        ins.then_inc(sems[("e", e)], 1)
                    else:
                        ins.then_inc(sems[("d", t[1])], o["inc"])
                if e == "sp":
                    for key, val in final.items():
                        if waited.get(key, 0) >= val:
                            continue
                        h.wait_ge(sems[key], val)

            @block.tensor
            def _(h):
                run("pe", h)

            @block.scalar
            def _(h):
                run("act", h)

            @block.vector
            def _(h):
                run("dve", h)

            @block.gpsimd
            def _(h):
                run("pool", h)

            @block.sync
            def _(h):
                run("sp", h)


class Arena:
    def __init__(self, ap, nbytes):
        self.ap = ap
        self.nbytes = nbytes
        self.off = 0

    def alloc(self, shape, dtype):
        n = 1
        for s_ in shape[1:]:
            n *= s_
        esz = 4 if dtype == F32 else 2
        nb = n * esz
        off = (self.off + 63) // 64 * 64
        assert off + nb <= self.nbytes, f"arena overflow {off + nb} > {self.nbytes}"
        self.off = off + nb
        v = self.ap[0:shape[0], off // 2:(off + nb) // 2]
        if dtype == F32:
            v = v.bitcast(F32)
        if len(shape) == 3:
            v = v.rearrange("p (a b) -> p a b", a=shape[1])
        elif len(shape) == 4:
            v = v.rearrange("p (a b c) -> p a b c", a=shape[1], b=shape[2])
        return v


class _PSList(list):
    pass


def _ps_banks(nc, es):
    t = es.enter_context(nc.psum_tensor("psall", [128, 4096], F32))
    lst = _PSList(t[:, i * 512:(i + 1) * 512] for i in range(8))
    lst.all = t
    return lst


def _bf16_np():
    import ml_dtypes
    return ml_dtypes.bfloat16


def emit_mod_acc(P, A, d_wada, d_cT, d_brow, cols=(0, 6144)):
    c0, c1 = cols
    n = c1 - c0
    cT = A.alloc([128, 16], F32)
    sc = A.alloc([128, 16], F32)
    acc = A.alloc([128, n], F32)
    wk = [A.alloc([128, n], F32), A.alloc([128, n], F32)]
    brow = A.alloc([1, n], F32)
    P.op("sp", lambda e: e.dma_start(out=cT, in_=d_cT), writes=["cT"], dma="misc")
    P.op("sp", lambda e: e.dma_start(out=brow, in_=d_brow[:, c0:c1]), writes=["brow"], dma="misc")
    P.op("act", lambda e: e.activation(out=sc, in_=cT, func=AF.Silu), reads=["cT"], writes=["sc"])
    for k in range(16):
        b = k % 2
        P.op("sp", lambda e, k=k, b=b: e.dma_start(out=wk[b], in_=d_wada[k * 128:(k + 1) * 128, c0:c1]),
             writes=[("wk", b)], dma="wk%d" % b)
        if k == 0:
            P.op("dve", lambda e, b=b: e.tensor_scalar(out=acc, in0=wk[b], scalar1=sc[:, 0:1], scalar2=None,
                                                        op0=ALU.mult), reads=[("wk", b), "sc"], writes=["acc"])
        else:
            P.op("dve", lambda e, k=k, b=b: e.scalar_tensor_tensor(out=acc, in0=wk[b], scalar=sc[:, k:k + 1], in1=acc,
                                                                   op0=ALU.mult, op1=ALU.add),
                 reads=[("wk", b), "sc", "acc"], writes=["acc"])
    P.op("dve", lambda e: e.tensor_tensor(out=acc[0:1, :], in0=acc[0:1, :], in1=brow[0:1, :], op=ALU.add),
         reads=["acc", "brow"], writes=["acc"])
    if c0 <= 2048 and c1 >= 4096:
        P.op("dve", lambda e: e.tensor_scalar(out=acc[0:1, 2048 - c0:4096 - c0], in0=acc[0:1, 2048 - c0:4096 - c0],
                                              scalar1=1.0, scalar2=None, op0=ALU.add), reads=["acc"], writes=["acc"])
    return acc


def emit_prep(P, A, ps, d_x, d_cT, d_wada, d_brow, d_nwT, d_ident, d_uT):
    ident = A.alloc([128, 128], F32)
    ones = A.alloc([128, 2], F32)
    nwT = A.alloc([128, 16], F32)
    P.op("sp", lambda e: e.dma_start(out=ident, in_=d_ident), writes=["ident"], dma="misc")
    P.op("sp", lambda e: e.dma_start(out=nwT, in_=d_nwT), writes=["nwT"], dma="misc")
    P.op("dve", lambda e: e.memset(ones, 1.0), writes=["ones"])
    acc = emit_mod_acc(P, A, d_wada, d_cT, d_brow, cols=(0, 4096))
    modT = A.alloc([128, 32], F32)
    AT = A.alloc([128, 16], F32)
    for j in range(32):
        P.op("pe", lambda e, j=j: e.matmul(ps[0][:, 2 * j:2 * j + 2], lhsT=acc[:, j * 128:(j + 1) * 128],
                                           rhs=ones[:, 0:2], start=True, stop=True),
             reads=["acc", "ones"], writes=[("ps", 0)])
    P.op("dve", lambda e: e.tensor_copy(modT, ps[0][:, 0:64:2]), reads=[("ps", 0)], writes=["modT"])
    P.op("dve", lambda e: e.tensor_tensor(out=AT, in0=modT[:, 16:32], in1=nwT, op=ALU.mult),
         reads=["modT", "nwT"], writes=["AT"])
    xt = [A.alloc([128, D], F32), A.alloc([128, D], F32)]
    junk = A.alloc([128, D], F32)
    uT = A.alloc([128, 16, TOK], BF16)
    ss = A.alloc([128, 8], F32)
    var = A.alloc([128, 8], F32)
    rstd = A.alloc([128, 8], F32)
    epsc = A.alloc([128, 1], F32)
    P.op("dve", lambda e: e.memset(epsc, EPS), writes=["epsc"])
    for t in range(8):
        b = t % 2
        P.op("sp", lambda e, t=t, b=b: e.dma_start(out=xt[b], in_=d_x[t * 128:(t + 1) * 128, :]),
             writes=[("xt", b)], dma="x%d" % b)
        P.op("act", lambda e, t=t, b=b: e.activation(out=junk, in_=xt[b], func=AF.Square, accum_out=ss[:, t:t + 1]),
             reads=[("xt", b)], writes=["junk", ("ss", t)])
        P.op("act", lambda e, t=t: e.activation(out=var[:, t:t + 1], in_=ss[:, t:t + 1], func=AF.Sqrt, scale=1.0 / D,
                                                bias=epsc), reads=[("ss", t), "epsc"], writes=[("var", t)])
        P.op("dve", lambda e, t=t: e.reciprocal(rstd[:, t:t + 1], var[:, t:t + 1]),
             reads=[("var", t)], writes=[("rstd", t)])
        P.op("dve", lambda e, t=t, b=b: e.tensor_scalar(out=xt[b], in0=xt[b], scalar1=rstd[:, t:t + 1], scalar2=None,
                                                        op0=ALU.mult), reads=[("xt", b), ("rstd", t)], writes=[("xt", b)])
        for g in range(4):
            pb = 1 + (t * 4 + g) % 4
            for q in range(4):
                c = g * 4 + q
                P.op("pe", lambda e, pb=pb, q=q, c=c, b=b: e.transpose(ps[pb][:, q * 128:(q + 1) * 128],
                                                                      xt[b][:, c * 128:(c + 1) * 128], ident),
                     reads=[("xt", b), "ident"], writes=[("ps", pb)])
            for q in range(4):
                c = g * 4 + q
                P.op("dve", lambda e, pb=pb, q=q, c=c, t=t: e.tensor_scalar(
                    out=uT[:, c, t * 128:(t + 1) * 128], in0=ps[pb][:, q * 128:(q + 1) * 128],
                    scalar1=AT[:, c:c + 1], scalar2=modT[:, c:c + 1], op0=ALU.mult, op1=ALU.add),
                    reads=[("ps", pb), "AT", "modT"], writes=[("uT", t)])
    P.op("sp", lambda e: e.dma_start(out=d_uT.rearrange("(c p) t -> p c t", p=128), in_=uT),
         reads=[("uT", t) for t in range(8)], dma="out")


def build_prep():
    nc = bass.Bass("TRN2", target_bir_lowering=False)
    d_x = nc.dram_tensor("x", [TOK, D], F32, kind="ExternalInput").ap()
    d_cT = nc.dram_tensor("cT", [128, 16], F32, kind="ExternalInput").ap()
    d_wada = nc.dram_tensor("w_ada", [D, 3 * D], F32, kind="ExternalInput").ap()
    d_brow = nc.dram_tensor("brow", [1, 3 * D], F32, kind="ExternalInput").ap()
    d_nwT = nc.dram_tensor("nwT", [128, 16], F32, kind="ExternalInput").ap()
    d_ident = nc.dram_tensor("ident", [128, 128], F32, kind="ExternalInput").ap()
    d_uT = nc.dram_tensor("uT", [D, TOK], BF16, kind="ExternalOutput").ap()
    with ExitStack() as es:
        arena_t = es.enter_context(nc.sbuf_tensor("arena", [128, 95 * 1024], BF16))
        A = Arena(arena_t[:, :], 190 * 1024)
        ps = _ps_banks(nc, es)
        P = Prog(nc, ALL_KEYS)
        emit_prep(P, A, ps, d_x, d_cT, d_wada, d_brow, d_nwT, d_ident, d_uT)
        P.emit()
    return nc


NV_LAM = 0
NV_DNW = 256
NV_RNW = 384
NV_TRI = 512
NV_MPREV = 640
NV_IDENT = 768
NV_G = 896
NV_TOT = 904
WMIX = 2176
WA0, WB0, WC0 = 0, 512, 896


def emit_mix(P, A, ps, layer, d_uT_tile, d_w, d_vecs, d_rotC, d_rotS, d_oz, passes=("A", "B", "C"), ntiles=16, dbg=99):
    lam_init = 0.8 - 0.6 * math.exp(-0.3 * layer)
    w_v = d_w.rearrange("(c p) n -> p c n", p=128)
    vecs = A.alloc([128, NV_TOT], F32)
    P.op("sp", lambda e: e.dma_start(out=vecs, in_=d_vecs), writes=["vecs"], dma="misc")
    tri_bf = A.alloc([128, 128], BF16)
    mask_c = A.alloc([128, 2, 2, 128], BF16)
    ones_bf = A.alloc([128, 128], BF16)
    epsc = A.alloc([128, 1], F32)
    ident = vecs[:, NV_IDENT:NV_IDENT + 128]
    tri_f = vecs[:, NV_TRI:NV_TRI + 128]
    P.op("dve", lambda e: e.tensor_copy(tri_bf, tri_f), reads=["vecs"], writes=["consts"])
    for r_ in range(2):
        P.op("dve", lambda e, r_=r_: e.tensor_copy(mask_c[:, r_, 0, :], vecs[:, NV_MPREV:NV_MPREV + 128]),
             reads=["vecs"], writes=["consts"])
        P.op("dve", lambda e, r_=r_: e.tensor_copy(mask_c[:, r_, 1, :], tri_f), reads=["vecs"], writes=["consts"])
    P.op("dve", lambda e: e.memset(ones_bf, 1.0), writes=["consts"])
    P.op("dve", lambda e: e.memset(epsc, EPS), writes=["epsc"])
    U = [A.alloc([128, 16, 512], BF16), A.alloc([128, 16, 512], BF16)]
    mark = A.off

    def load_U(j):
        b = j % 2
        for q4 in range(4):
            P.op("sp", lambda e, j=j, b=b, q4=q4: e.dma_start(out=U[b][:, q4 * 4:(q4 + 1) * 4, :],
                                                             in_=d_uT_tile(j, q4)),
                 writes=[("U", b, q4)], dma="u%d" % b)

    def load_w(wt, c0, ncols):
        for half in range(2):
            P.op("pool", lambda e, half=half: e.dma_start(out=wt[:, half * 8:(half + 1) * 8, :],
                                                         in_=w_v[:, half * 8:(half + 1) * 8, c0:c0 + ncols]),
                 writes=[("w", half)], dma="w")

    ostate = {"n": 0}

    def store_oz(stage_alloc, row0, j, fill):
        b = ostate["n"] % 2
        ostate["n"] += 1
        st = stage_alloc[b]
        fill(st, ("ozst", b))
        P.op("sp", lambda e, st=st, j=j: e.dma_start(out=d_oz[row0:row0 + 128, j * 512:(j + 1) * 512], in_=st),
             reads=[("ozst", b)], dma="out%d" % b)

    def pass_a():
        A.off = mark
        wA = A.alloc([128, 16, 512], BF16)
        kT = A.alloc([128, S], BF16)
        V = A.alloc([128, 64, 144], BF16)
        qT = [A.alloc([128, 512], BF16), A.alloc([128, 512], BF16)]
        sz = [A.alloc([128, 4, 128], F32), A.alloc([128, 4, 128], F32)]
        NPT = 4
        pT = [A.alloc([128, 512], BF16) for _ in range(NPT)]
        om0 = A.alloc([128, 4, 128], F32)
        om1 = A.alloc([128, 4, 128], F32)
        osum = A.alloc([128, 4, 128], F32)
        junk = A.alloc([128, 128], F32)
        ozb = [A.alloc([128, 4, 128], F32), A.alloc([128, 4, 128], F32)]
        ozst = [A.alloc([128, 512], BF16), A.alloc([128, 512], BF16)]
        rs = A.alloc([128, 8], F32)
        ssq = A.alloc([128, 4], F32)
        var = A.alloc([128, 4], F32)
        rstd = A.alloc([128, 4], F32)
        lf = A.alloc([128, 8], F32)
        dnw = A.alloc([128, 128], F32)
        load_w(wA, WA0, 512)
        load_U(0)
        P.op("dve", lambda e: e.memset(V[:, :, 128:144], 1.0), writes=["Vones"])
        lam = vecs[:, NV_LAM:NV_LAM + 256]
        P.op("dve", lambda e: e.scalar_tensor_tensor(out=junk[:, 0:64], in0=lam[:, 0:64], scalar=1.0, in1=lam[:, 64:128],
                                                     op0=ALU.mult, op1=ALU.mult, accum_out=lf[:, 0:1]),
             reads=["vecs"], writes=["junk", "lf"])
        P.op("dve", lambda e: e.scalar_tensor_tensor(out=junk[:, 0:64], in0=lam[:, 128:192], scalar=1.0, in1=lam[:, 192:256],
                                                     op0=ALU.mult, op1=ALU.mult, accum_out=lf[:, 1:2]),
             reads=["vecs", "junk", "lf"], writes=["junk", "lf"])
        P.op("act", lambda e: e.activation(out=lf[:, 2:4], in_=lf[:, 0:2], func=AF.Exp), reads=["lf"], writes=["lf"])
        P.op("dve", lambda e: e.tensor_tensor(out=lf[:, 4:5], in0=lf[:, 3:4], in1=lf[:, 2:3], op=ALU.subtract),
             reads=["lf"], writes=["lf"])
        P.op("dve", lambda e: e.tensor_scalar(out=lf[:, 4:5], in0=lf[:, 4:5], scalar1=-lam_init, scalar2=None, op0=ALU.add),
             reads=["lf"], writes=["lf"])
        P.op("dve", lambda e: e.tensor_scalar(out=dnw, in0=vecs[:, NV_DNW:NV_DNW + 128], scalar1=1.0 - lam_init,
                                              scalar2=None, op0=ALU.mult), reads=["vecs"], writes=["dnw"])
        acc_b = [0, 1, 2, 3]
        sc_b = [4, 5]
        pj_b = [6, 7]
        cnt = {"s": 0, "pj": 0}

        def projA(j):
            b = j % 2
            rd = [("U", b, 0), ("U", b, 1), ("U", b, 2), ("U", b, 3), ("w", 0), ("w", 1)]
            for which in range(2 if DBG2 >= 1 else 1):
                pb = pj_b[cnt["pj"] % 2]
                cnt["pj"] += 1
                for c in range(16):
                    P.op("pe", lambda e, pb=pb, c=c, b=b, which=which: e.matmul(
                        ps[pb][:, :], lhsT=wA[:, c, which * 128:(which + 1) * 128], rhs=U[b][:, c, :],
                        start=(c == 0), stop=(c == 15)), reads=rd, writes=[("ps", pb)])
                if which == 0:
                    P.op("dve", lambda e, pb=pb, b=b: e.tensor_copy(qT[b], ps[pb][:, :]), reads=[("ps", pb)],
                         writes=[("qT", b)])
                else:
                    P.op("dve", lambda e, pb=pb, j=j: e.tensor_copy(kT[:, j * 512:(j + 1) * 512], ps[pb][:, :]),
                         reads=[("ps", pb)], writes=[("kT", j)])
            if DBG2 < 2:
                return
            for s_ in range({3: 4, 4: 4, 5: 2}.get(DBG2, 4) if DBG2 >= 3 else 1):
                pb = pj_b[cnt["pj"] % 2]
                cnt["pj"] += 1
                for c in range(16):
                    P.op("pe", lambda e, pb=pb, c=c, b=b, s_=s_: e.matmul(
                        ps[pb][:, 0:256], lhsT=U[b][:, c, s_ * 128:(s_ + 1) * 128], rhs=wA[:, c, 256:512],
                        start=(c == 0), stop=(c == 15)), reads=rd, writes=[("ps", pb)])
                if DBG2 != 4:
                    P.op("dve", lambda e, pb=pb, j=j, s_=s_: e.tensor_copy(V[:, j * 4 + s_, 0:128], ps[pb][:, 0:128]),
                         reads=[("ps", pb)], writes=[("V", j)])
                if DBG2 != 3:
                    P.op("act", lambda e, pb=pb, b=b, s_=s_: e.activation(out=sz[b][:, s_, :], in_=ps[pb][:, 128:256],
                                                                         func=AF.Silu), reads=[("ps", pb)], writes=[("sz", b)])

        def attnA(j, m):
            b = j % 2
            nkb = 4 * j + 4
            slots = {}

            def SE(kb):
                lo = max(0, kb - 4 * j)
                w_ = (4 - lo) * 128
                sb_ = sc_b[cnt["s"] % 2]
                slot = cnt["s"] % NPT
                cnt["s"] += 1
                slots[kb] = slot
                P.op("pe", lambda e: e.matmul(ps[sb_][:, 0:w_], lhsT=kT[64 * m:64 * m + 64, kb * 128:(kb + 1) * 128],
                                              rhs=qT[b][64 * m:64 * m + 64, lo * 128:512], start=True, stop=True),
                     reads=[("kT", kb // 4), ("qT", b)], writes=[("ps", sb_)])
                P.op("act", lambda e: e.activation(out=pT[slot][:, 0:w_], in_=ps[sb_][:, 0:w_], func=AF.Exp, scale=0.125),
                     reads=[("ps", sb_)], writes=[("pT", slot)])
                if kb >= 4 * j:
                    P.op("dve", lambda e: e.tensor_tensor(out=pT[slot][:, 0:128], in0=pT[slot][:, 0:128], in1=tri_bf,
                                                          op=ALU.mult), reads=[("pT", slot), "consts"], writes=[("pT", slot)])

            def PV(kb):
                lo = max(0, kb - 4 * j)
                slot = slots[kb]
                for qb in range(lo, 4):
                    P.op("pe", lambda e, qb=qb: e.matmul(ps[acc_b[qb]][:, 0:129],
                                                         lhsT=pT[slot][:, (qb - lo) * 128:(qb - lo + 1) * 128],
                                                         rhs=V[:, kb, 0:129], start=(kb == 0), stop=(kb == 4 * j + qb)),
                         reads=[("pT", slot), ("V", kb // 4), "Vones"], writes=[("ps", acc_b[qb])])

            SE(0)
            for kb in range(nkb):
                if kb + 1 < nkb:
                    SE(kb + 1)
                PV(kb)
            for qb in range(4):
                ab = acc_b[qb]
                col = m * 4 + qb
                P.op("dve", lambda e, ab=ab, col=col: e.reciprocal(rs[:, col:col + 1], ps[ab][:, 128:129]),
                     reads=[("ps", ab)], writes=[("rs", col)])
                if m == 0:
                    P.op("dve", lambda e, ab=ab, col=col, qb=qb: e.tensor_scalar(
                        out=om0[:, qb, :], in0=ps[ab][:, 0:128], scalar1=rs[:, col:col + 1], scalar2=None, op0=ALU.mult),
                        reads=[("ps", ab), ("rs", col)], writes=["om0"])
                else:
                    P.op("dve", lambda e, ab=ab, col=col, qb=qb: e.tensor_scalar(
                        out=om1[:, qb, :], in0=ps[ab][:, 0:128], scalar1=rs[:, col:col + 1], scalar2=lf[:, 4:5],
                        op0=ALU.mult, op1=ALU.mult), reads=[("ps", ab), ("rs", col), "lf"], writes=["om1"])

        def combineA(j):
            b = j % 2
            P.op("dve", lambda e: e.tensor_tensor(out=osum, in0=om0, in1=om1, op=ALU.add), reads=["om0", "om1"],
                 writes=["osum"])
            for qb in range(4):
                P.op("dve", lambda e, qb=qb: e.scalar_tensor_tensor(out=junk, in0=osum[:, qb, :], scalar=1.0, in1=osum[:, qb, :],
                                                                    op0=ALU.mult, op1=ALU.mult,
                                                                    accum_out=ssq[:, qb:qb + 1]),
                     reads=["osum"], writes=["junk", "ssq"])
            P.op("act", lambda e: e.activation(out=var, in_=ssq, func=AF.Sqrt, scale=1.0 / 128, bias=epsc),
                 reads=["ssq", "epsc"], writes=["var"])
            P.op("dve", lambda e: e.reciprocal(rstd, var), reads=["var"], writes=["rstd"])
            for qb in range(4):
                P.op("dve", lambda e, qb=qb: e.scalar_tensor_tensor(out=osum[:, qb, :], in0=osum[:, qb, :],
                                                                    scalar=rstd[:, qb:qb + 1], in1=dnw, op0=ALU.mult,
                                                                    op1=ALU.mult), reads=["osum", "rstd", "dnw"],
                     writes=["osum"])
            P.op("dve", lambda e: e.tensor_tensor(out=ozb[b], in0=osum, in1=sz[b], op=ALU.mult),
                 reads=["osum", ("sz", b)], writes=[("ozb", b)])

        def finA(j):
            b = j % 2
            pb = pj_b[cnt["pj"] % 2]
            cnt["pj"] += 1
            for qb in range(4):
                P.op("pe", lambda e, qb=qb: e.transpose(ps[pb][:, qb * 128:(qb + 1) * 128], ozb[b][:, qb, :], ident),
                     reads=[("ozb", b), "vecs"], writes=[("ps", pb)])

            def fill(st, res):
                P.op("dve", lambda e: e.tensor_copy(st, ps[pb][:, :]), reads=[("ps", pb)], writes=[res])
            store_oz(ozst, 0, j, fill)

        for j in range(ntiles):
            if j + 1 < ntiles:
                load_U(j + 1)
            if dbg >= 1:
                projA(j)
            if j > 0 and dbg >= 4:
                finA(j - 1)
            if dbg >= 2:
                attnA(j, 0)
                attnA(j, 1)
            if dbg >= 3:
                combineA(j)
        if dbg >= 4:
            finA(ntiles - 1)
        P.barrier()
    if "A" in passes:
        pass_a()

    def pass_b():
        A.off = mark
        wB = A.alloc([128, 16, 384], BF16)
        rC = [A.alloc([128, 4, 128], F32), A.alloc([128, 4, 128], F32)]
        rS = [A.alloc([128, 4, 128], F32), A.alloc([128, 4, 128], F32)]
        tri4 = A.alloc([128, 4, 128], F32)
        t1 = A.alloc([128, 4, 128], F32)
        t2 = A.alloc([128, 4, 128], F32)
        qk = A.alloc([128, 4, 128], F32)
        kd_bf = A.alloc([128, 4, 64], BF16)
        v_b = A.alloc([128, 4, 128], BF16)
        szb = A.alloc([128, 4, 128], F32)
        qdT = A.alloc([64, 4, 128], BF16)
        kdT = A.alloc([64, 4, 128], BF16)
        pTb = A.alloc([128, 4, 128], BF16)
        Sst = A.alloc([64, 128], F32)
        Rbf = [A.alloc([64, 128], BF16), A.alloc([64, 128], BF16)]
        ob = A.alloc([128, 4, 128], F32)
        sq = A.alloc([128, 4, 128], F32)
        ssq = A.alloc([128, 4], F32)
        var = A.alloc([128, 4], F32)
        rstd = A.alloc([128, 4], F32)
        ozbB = A.alloc([128, 4, 128], F32)
        ozstB = [A.alloc([128, 512], BF16), A.alloc([128, 512], BF16)]
        rnw = vecs[:, NV_RNW:NV_RNW + 128]
        gcol = vecs[0:64, NV_G:NV_G + 1]
        load_w(wB, WB0, 384)
        load_U(0)
        for ci in range(4):
            P.op("dve", lambda e, ci=ci: e.tensor_copy(tri4[:, ci, :], tri_f), reads=["vecs"], writes=["tri4"])

        def load_rot(j):
            b = j % 2
            P.op("sp", lambda e: e.dma_start(out=rC[b], in_=d_rotC[j * 512:(j + 1) * 512, :].rearrange("(c p) n -> p c n", p=128)),
                 writes=[("rC", b)], dma="rot%d" % b)
            P.op("sp", lambda e: e.dma_start(out=rS[b], in_=d_rotS[j * 512:(j + 1) * 512, :].rearrange("(c p) n -> p c n", p=128)),
                 writes=[("rS", b)], dma="rot%d" % b)

        load_rot(0)
        pj4 = ps.all[:, 0:2048].rearrange("p (c n) -> p c n", c=4)
        PJR = [("ps", 0), ("ps", 1), ("ps", 2), ("ps", 3)]
        TQK, SC, OO, KV = 4, 5, 6, 7
        nR = {"n": 0}
        for j in range(ntiles):
            b = j % 2
            if j + 1 < ntiles:
                load_U(j + 1)
                load_rot(j + 1)
            rd = [("U", b, 0), ("U", b, 1), ("U", b, 2), ("U", b, 3), ("w", 0), ("w", 1)]
            for ci in range(4):
                for c in range(16):
                    P.op("pe", lambda e, c=c, ci=ci, b=b: e.matmul(ps[ci][:, 0:384], lhsT=U[b][:, c, ci * 128:(ci + 1) * 128],
                                                                   rhs=wB[:, c, :], start=(c == 0), stop=(c == 15)),
                         reads=rd, writes=[("ps", ci)])
            P.op("dve", lambda e, b=b: e.tensor_tensor(out=t1, in0=pj4[:, :, 0:128], in1=rC[b], op=ALU.mult),
                 reads=PJR + [("rC", b)], writes=["t1"])
            P.op("dve", lambda e, b=b: e.tensor_tensor(out=t2[:, :, 0:128:2], in0=pj4[:, :, 1:128:2], in1=rS[b][:, :, 0:128:2],
                                                       op=ALU.mult), reads=PJR + [("rS", b)], writes=["t2e"])
            P.op("dve", lambda e, b=b: e.tensor_tensor(out=t2[:, :, 1:128:2], in0=pj4[:, :, 0:128:2], in1=rS[b][:, :, 1:128:2],
                                                       op=ALU.mult), reads=PJR + [("rS", b)], writes=["t2o"])
            P.op("act", lambda e: e.activation(out=v_b, in_=pj4[:, :, 128:256], func=AF.Copy), reads=PJR, writes=["v_b"])
            P.op("act", lambda e: e.activation(out=szb, in_=pj4[:, :, 256:384], func=AF.Silu), reads=PJR, writes=["szb"])
            P.op("dve", lambda e: e.tensor_tensor(out=qk, in0=t1, in1=t2, op=ALU.add), reads=["t1", "t2e", "t2o"], writes=["qk"])
            P.op("dve", lambda e: e.tensor_copy(kd_bf, qk[:, :, 64:128]), reads=["qk"], writes=["kd_bf"])
            for ci in range(4):
                P.op("pe", lambda e, ci=ci: e.transpose(ps[TQK][0:64, ci * 128:(ci + 1) * 128], qk[:, ci, 0:64], ident),
                     reads=["qk", "vecs"], writes=[("ps", TQK)])
            P.op("act", lambda e: e.activation(out=qdT, in_=ps[TQK][0:64, :].rearrange("p (c n) -> p c n", c=4), func=AF.Copy),
                 reads=[("ps", TQK)], writes=["qdT"])
            for ci in range(4):
                P.op("pe", lambda e, ci=ci: e.transpose(ps[TQK][0:64, ci * 128:(ci + 1) * 128], qk[:, ci, 64:128], ident),
                     reads=["qk", "vecs"], writes=[("ps", TQK)])
            P.op("dve", lambda e: e.tensor_copy(kdT, ps[TQK][0:64, :].rearrange("p (c n) -> p c n", c=4)),
                 reads=[("ps", TQK)], writes=["kdT"])
            for ci in range(4):
                P.op("pe", lambda e, ci=ci: e.matmul(ps[SC][:, ci * 128:(ci + 1) * 128], lhsT=kdT[:, ci, :], rhs=qdT[:, ci, :],
                                                     start=True, stop=True), reads=["kdT", "qdT"], writes=[("ps", SC)])
            P.op("dve", lambda e: e.tensor_tensor(out=pTb, in0=ps[SC][:, :].rearrange("p (c n) -> p c n", c=4), in1=tri4, op=ALU.mult),
                 reads=[("ps", SC), "tri4"], writes=["pTb"])
            for ci in range(4):
                n = j * 4 + ci
                rb = nR["n"] % 2
                P.op("pe", lambda e, ci=ci, n=n: e.matmul(ps[OO][:, ci * 128:(ci + 1) * 128], lhsT=pTb[:, ci, :], rhs=v_b[:, ci, :],
                                                          start=True, stop=(n == 0)), reads=["pTb", "v_b"], writes=[("ps", OO)])
                if n > 0:
                    P.op("pe", lambda e, ci=ci, rb=rb: e.matmul(ps[OO][:, ci * 128:(ci + 1) * 128], lhsT=qdT[:, ci, :], rhs=Rbf[rb],
                                                                start=False, stop=True), reads=["qdT", ("Rbf", rb)], writes=[("ps", OO)])
                P.op("pe", lambda e, ci=ci: e.matmul(ps[KV][0:64, 0:128], lhsT=kd_bf[:, ci, :], rhs=v_b[:, ci, :], start=True, stop=True),
                     reads=["kd_bf", "v_b"], writes=[("ps", KV)])
                nR["n"] += 1
                rb2 = nR["n"] % 2
                if n == 0:
                    P.op("dve", lambda e: e.tensor_scalar(out=Sst, in0=ps[KV][0:64, 0:128], scalar1=gcol, scalar2=None, op0=ALU.mult),
                         reads=[("ps", KV), "vecs"], writes=["Sst"])
                else:
                    P.op("dve", lambda e: e.tensor_tensor(out=Sst, in0=ps[KV][0:64, 0:128], in1=Sst, op=ALU.add),
                         reads=[("ps", KV), "Sst"], writes=["Sst"])
                    P.op("dve", lambda e: e.tensor_scalar(out=Sst, in0=Sst, scalar1=gcol, scalar2=None, op0=ALU.mult),
                         reads=["Sst", "vecs"], writes=["Sst"])
                P.op("dve", lambda e, rb2=rb2: e.tensor_copy(Rbf[rb2], Sst), reads=["Sst"], writes=[("Rbf", rb2)])
            P.op("act", lambda e: e.activation(out=ob, in_=ps[OO][:, :].rearrange("p (c n) -> p c n", c=4), func=AF.Copy),
                 reads=[("ps", OO)], writes=["ob"])
            P.op("dve", lambda e: e.tensor_tensor(out=sq, in0=ob, in1=ob, op=ALU.mult), reads=["ob"], writes=["sq"])
            P.op("dve", lambda e: e.tensor_reduce(out=ssq, in_=sq, axis=mybir.AxisListType.X, op=ALU.add), reads=["sq"], writes=["ssq"])
            P.op("act", lambda e: e.activation(out=var, in_=ssq, func=AF.Sqrt, scale=1.0 / 128, bias=epsc),
                 reads=["ssq", "epsc"], writes=["var"])
            P.op("dve", lambda e: e.reciprocal(rstd, var), reads=["var"], writes=["rstd"])
            for ci in range(4):
                P.op("dve", lambda e, ci=ci: e.scalar_tensor_tensor(out=ob[:, ci, :], in0=ob[:, ci, :], scalar=rstd[:, ci:ci + 1],
                                                                    in1=rnw, op0=ALU.mult, op1=ALU.mult),
                     reads=["ob", "rstd", "vecs"], writes=["ob"])
            P.op("dve", lambda e: e.tensor_tensor(out=ozbB, in0=ob, in1=szb, op=ALU.mult), reads=["ob", "szb"], writes=["ozbB"])
            for ci in range(4):
                P.op("pe", lambda e, ci=ci: e.transpose(ps[SC][:, ci * 128:(ci + 1) * 128], ozbB[:, ci, :], ident),
                     reads=["ozbB", "vecs"], writes=[("ps", SC)])

            def fill(st, res):
                P.op("act", lambda e: e.activation(out=st, in_=ps[SC][:, :], func=AF.Copy), reads=[("ps", SC)], writes=[res])
            store_oz(ozstB, 128, j, fill)
        P.barrier()

    if "B" in passes:
        pass_b()

    def pass_c():
        A.off = mark
        wC = A.alloc([128, 16, 512], BF16)
        qTs = A.alloc([128, 2048], BF16)
        kTs = A.alloc([128, S], BF16)
        vTs = A.alloc([128, 2048], F32)
        Vs = A.alloc([128, 64, 128], BF16)
        accO = A.alloc([128, S], F32)
        accZ = A.alloc([128, S], F32)
        szT = A.alloc([128, 2048], F32)
        NPC = 3
        pTc = [A.alloc([128, 2, 2, 128], BF16) for _ in range(NPC)]
        rz = A.alloc([128, 512], F32)
        ozstC = [A.alloc([128, 512], BF16), A.alloc([128, 512], BF16)]
        PJc = [0, 1]
        SCc = [2, 3]
        PO, PZ, TRc = 4, 5, 6
        cntc = {"pj": 0, "sc": 0}
        SCALE_C = 128.0 ** -0.5

        for g in range(3):
            d = DIL[g]
            Ls = 128 * d
            ncols = 512 if g == 2 else 384
            c0 = WC0 + g * 384
            if g == 2:
                pass
            load_w(wC[:, :, 0:ncols], c0, ncols)
            load_U(0)
            tiles_per_span = max(1, Ls // 512)
            for j in range(ntiles):
                b = j % 2
                if j + 1 < ntiles:
                    load_U(j + 1)
                rd = [("U", b, 0), ("U", b, 1), ("U", b, 2), ("U", b, 3), ("w", 0), ("w", 1)]
                soff = (j * 512) % Ls
                for which in range(ncols // 128):
                    pb = PJc[cntc["pj"] % 2]
                    cntc["pj"] += 1
                    for c in range(16):
                        P.op("pe", lambda e, pb=pb, c=c, b=b, which=which: e.matmul(
                            ps[pb][:, :], lhsT=wC[:, c, which * 128:(which + 1) * 128], rhs=U[b][:, c, :],
                            start=(c == 0), stop=(c == 15)), reads=rd, writes=[("ps", pb)])
                    if which == 3:
                        P.op("act", lambda e, pb=pb, soff=soff: e.activation(out=szT[:, soff:soff + 512], in_=ps[pb][:, :],
                                                                            func=AF.Silu), reads=[("ps", pb)], writes=[("szT", j % 4)])
                        continue
                    if which == 0:
                        dst_buf, dbase, res = qTs, 0, ("qTs", j % tiles_per_span)
                    elif which == 1:
                        dst_buf, dbase, res = kTs, (j * 512) // Ls * Ls, ("kTs", j)
                    else:
                        dst_buf, dbase, res = vTs, 0, ("vTs", j % tiles_per_span)
                    if d == 1:
                        dst = dst_buf[:, dbase + soff:dbase + soff + 512] if which == 1 else dst_buf[:, 0:512]
                        src = ps[pb][:, :]
                    elif d == 4:
                        dst = dst_buf[:, dbase:dbase + 512].rearrange("p (r i) -> p r i", r=4)
                        src = ps[pb][:, :].rearrange("p (i r) -> p r i", r=4)
                    else:
                        q4 = soff // 512
                        dst = dst_buf[:, dbase:dbase + 2048].rearrange("p (r i) -> p r i", r=16)[:, :, 32 * q4:32 * q4 + 32]
                        src = ps[pb][:, :].rearrange("p (i r) -> p r i", r=16)
                    if which == 1:
                        P.op("act", lambda e, dst=dst, src=src: e.activation(out=dst, in_=src, func=AF.Copy),
                             reads=[("ps", pb)], writes=[res])
                    else:
                        P.op("dve", lambda e, dst=dst, src=src: e.tensor_copy(dst, src), reads=[("ps", pb)], writes=[res])
                if (j * 512 + 512) % Ls != 0 and Ls > 512:
                    continue
                if d == 1:
                    spans = [j * 4 + q for q in range(4)]
                else:
                    spans = [(j * 512) // Ls]
                blocks = []
                for n in spans:
                    for r in range(d):
                        blk = n * d + r
                        if d == 1:
                            lpos = (n - j * 4) * 128
                        else:
                            lpos = r * 128
                        blocks.append((n, r, blk, lpos))
                tl = [j - t_ for t_ in range(tiles_per_span)]
                for gi in range(0, len(blocks), 4):
                    grp = blocks[gi:gi + 4]
                    for k_, (n, r, blk, lpos) in enumerate(grp):
                        P.op("pe", lambda e, k_=k_, lpos=lpos: e.transpose(ps[TRc][:, k_ * 128:(k_ + 1) * 128],
                                                                          vTs[:, lpos:lpos + 128], ident),
                             reads=[("vTs", t_ % tiles_per_span) for t_ in tl] + ["vecs"], writes=[("ps", TRc)])
                    blk0 = grp[0][2]
                    P.op("dve", lambda e, blk0=blk0, ng=len(grp): e.tensor_copy(
                        Vs[:, blk0:blk0 + ng, :], ps[TRc][:, 0:ng * 128].rearrange("p (a b) -> p a b", a=ng)),
                        reads=[("ps", TRc)], writes=[("Vs", blk0 // 4)])
                for pi in range(0, len(blocks), 2):
                    pair = blocks[pi:pi + 2]
                    firsts = [p_[0] == 0 for p_ in pair]
                    sb_ = SCc[cntc["sc"] % 2]
                    slot = cntc["sc"] % NPC
                    cntc["sc"] += 1
                    qres = [("qTs", t_ % tiles_per_span) for t_ in tl]
                    for u, (n, r, blk, lpos) in enumerate(pair):
                        if not firsts[u]:
                            P.op("pe", lambda e, u=u, blk=blk, lpos=lpos, sb_=sb_, d=d: e.matmul(
                                ps[sb_][:, u * 256:u * 256 + 128], lhsT=kTs[:, (blk - d) * 128:(blk - d + 1) * 128],
                                rhs=qTs[:, lpos:lpos + 128], start=True, stop=True),
                                reads=qres + [("kTs", ((blk - d) * 128) // 512)], writes=[("ps", sb_)])
                        P.op("pe", lambda e, u=u, blk=blk, lpos=lpos, sb_=sb_: e.matmul(
                            ps[sb_][:, u * 256 + 128:u * 256 + 256], lhsT=kTs[:, blk * 128:(blk + 1) * 128],
                            rhs=qTs[:, lpos:lpos + 128], start=True, stop=True),
                            reads=qres + [("kTs", t_) for t_ in tl], writes=[("ps", sb_)])
                    psv = ps[sb_][:, :].rearrange("p (u h q) -> p u h q", u=2, h=2)
                    if not any(firsts):
                        P.op("act", lambda e, psv=psv, slot=slot: e.activation(out=pTc[slot], in_=psv, func=AF.Exp, scale=SCALE_C),
                             reads=[("ps", sb_)], writes=[("pTc", slot)])
                        P.op("dve", lambda e, slot=slot: e.tensor_tensor(out=pTc[slot], in0=pTc[slot], in1=mask_c, op=ALU.mult),
                             reads=[("pTc", slot), "consts"], writes=[("pTc", slot)])
                    else:
                        for u in range(len(pair)):
                            if firsts[u]:
                                P.op("act", lambda e, psv=psv, slot=slot, u=u: e.activation(
                                    out=pTc[slot][:, u, 1, :], in_=psv[:, u, 1, :], func=AF.Exp, scale=SCALE_C),
                                    reads=[("ps", sb_)], writes=[("pTc", slot)])
                                P.op("dve", lambda e, slot=slot, u=u: e.tensor_tensor(
                                    out=pTc[slot][:, u, 1, :], in0=pTc[slot][:, u, 1, :], in1=mask_c[:, u, 1, :], op=ALU.mult),
                                    reads=[("pTc", slot), "consts"], writes=[("pTc", slot)])
                            else:
                                P.op("act", lambda e, psv=psv, slot=slot, u=u: e.activation(
                                    out=pTc[slot][:, u, :, :], in_=psv[:, u, :, :], func=AF.Exp, scale=SCALE_C),
                                    reads=[("ps", sb_)], writes=[("pTc", slot)])
                                P.op("dve", lambda e, slot=slot, u=u: e.tensor_tensor(
                                    out=pTc[slot][:, u, :, :], in0=pTc[slot][:, u, :, :], in1=mask_c[:, u, :, :], op=ALU.mult),
                                    reads=[("pTc", slot), "consts"], writes=[("pTc", slot)])
                    for u, (n, r, blk, lpos) in enumerate(pair):
                        vres = [("Vs", blk // 4), ("Vs", max(0, blk - d) // 4)]
                        if not firsts[u]:
                            P.op("pe", lambda e, u=u, blk=blk, slot=slot, d=d: e.matmul(
                                ps[PO][:, u * 128:(u + 1) * 128], lhsT=Vs[:, blk - d, :], rhs=pTc[slot][:, u, 0, :],
                                start=True, stop=False), reads=[("pTc", slot)] + vres, writes=[("ps", PO)])
                        P.op("pe", lambda e, u=u, blk=blk, slot=slot, first=firsts[u]: e.matmul(
                            ps[PO][:, u * 128:(u + 1) * 128], lhsT=Vs[:, blk, :], rhs=pTc[slot][:, u, 1, :],
                            start=first, stop=True), reads=[("pTc", slot)] + vres, writes=[("ps", PO)])
                    for u, (n, r, blk, lpos) in enumerate(pair):
                        if not firsts[u]:
                            P.op("pe", lambda e, u=u, slot=slot: e.matmul(
                                ps[PZ][:, u * 128:(u + 1) * 128], lhsT=ones_bf, rhs=pTc[slot][:, u, 0, :],
                                start=True, stop=False), reads=[("pTc", slot), "consts"], writes=[("ps", PZ)])
                        P.op("pe", lambda e, u=u, slot=slot, first=firsts[u]: e.matmul(
                            ps[PZ][:, u * 128:(u + 1) * 128], lhsT=ones_bf, rhs=pTc[slot][:, u, 1, :],
                            start=first, stop=True), reads=[("pTc", slot), "consts"], writes=[("ps", PZ)])
                    n0, r0 = pair[0][0], pair[0][1]
                    if d == 1:
                        t0_ = n0 * 128
                        vO = accO[:, t0_:t0_ + 256]
                        vZ = accZ[:, t0_:t0_ + 256]
                        sO = ps[PO][:, 0:256]
                        sZ = ps[PZ][:, 0:256]
                    else:
                        vO = accO[:, n0 * Ls:(n0 + 1) * Ls].rearrange("p (i r) -> p r i", r=d)[:, r0:r0 + 2, :]
                        vZ = accZ[:, n0 * Ls:(n0 + 1) * Ls].rearrange("p (i r) -> p r i", r=d)[:, r0:r0 + 2, :]
                        sO = ps[PO][:, 0:256].rearrange("p (u q) -> p u q", u=2)
                        sZ = ps[PZ][:, 0:256].rearrange("p (u q) -> p u q", u=2)
                    tt = (n0 * Ls) // 512
                    ares = [("acc", tt + t_) for t_ in range(tiles_per_span)]
                    if g == 0:
                        P.op("dve", lambda e, vO=vO, sO=sO: e.tensor_copy(vO, sO), reads=[("ps", PO)], writes=ares)
                        P.op("act", lambda e, vZ=vZ, sZ=sZ: e.activation(out=vZ, in_=sZ, func=AF.Copy), reads=[("ps", PZ)],
                             writes=[("accz", a_[1]) for a_ in ares])
                    else:
                        P.op("dve", lambda e, vO=vO, sO=sO: e.tensor_tensor(out=vO, in0=vO, in1=sO, op=ALU.add),
                             reads=[("ps", PO)] + ares, writes=ares)
                        P.op("dve", lambda e, vZ=vZ, sZ=sZ: e.tensor_tensor(out=vZ, in0=vZ, in1=sZ, op=ALU.add),
                             reads=[("ps", PZ)] + [("accz", a_[1]) for a_ in ares], writes=[("accz", a_[1]) for a_ in ares])
                if g == 2:
                    n0 = (j * 512) // Ls
                    for q4 in range(4):
                        tt = n0 * 4 + q4
                        if tt >= ntiles:
                            continue
                        sl = slice(tt * 512, (tt + 1) * 512)

                        def fill(st, res, sl=sl, q4=q4, tt=tt):
                            P.op("dve", lambda e: e.reciprocal(rz, accZ[:, sl]), reads=[("accz", tt)], writes=["rz"])
                            P.op("dve", lambda e: e.tensor_tensor(out=rz, in0=rz, in1=accO[:, sl], op=ALU.mult),
                                 reads=["rz", ("acc", tt)], writes=["rz"])
                            P.op("dve", lambda e: e.tensor_tensor(out=st, in0=rz, in1=szT[:, q4 * 512:(q4 + 1) * 512], op=ALU.mult),
                                 reads=["rz", ("szT", q4)], writes=[res])
                        store_oz(ozstC, 256, tt, fill)
            P.barrier()
    if "C" in passes:
        pass_c()


def build_mix(layer, passes=("A", "B", "C"), ntiles=16, dbg=99):
    nc = bass.Bass("TRN2", target_bir_lowering=False)
    d_uT = nc.dram_tensor("uT", [D, S], BF16, kind="ExternalInput").ap()
    d_w = nc.dram_tensor("w_mix", [D, WMIX], F32, kind="ExternalInput").ap()
    d_vecs = nc.dram_tensor("vecs", [128, NV_TOT], F32, kind="ExternalInput").ap()
    d_rotC = nc.dram_tensor("rotC", [S, 128], F32, kind="ExternalInput").ap()
    d_rotS = nc.dram_tensor("rotS", [S, 128], F32, kind="ExternalInput").ap()
    d_oz = nc.dram_tensor("ozT", [384, S], BF16, kind="ExternalOutput").ap()
    uT_v = d_uT.rearrange("(c p) t -> p c t", p=128)
    with ExitStack() as es:
        arena_t = es.enter_context(nc.sbuf_tensor("arena", [128, 95 * 1024], BF16))
        A = Arena(arena_t[:, :], 190 * 1024)
        ps = _ps_banks(nc, es)
        P = Prog(nc, ALL_KEYS)
        emit_mix(P, A, ps, layer, lambda j, q4: uT_v[:, q4 * 4:(q4 + 1) * 4, j * 512:(j + 1) * 512], d_w, d_vecs, d_rotC, d_rotS, d_oz,
                 passes=passes, ntiles=ntiles, dbg=dbg)
        P.emit()
    return nc


def _const_tables():
    p = np.arange(128)
    tri = (p[None, :] >= p[:, None]).astype(np.float32)
    mprev = (p[:, None] >= p[None, :]).astype(np.float32)
    return tri, mprev, np.eye(128, dtype=np.float32)


def _rot_tables(h):
    dk = 64
    theta = 1.0 / (10000.0 ** np.linspace(0.0, 1.0, dk // 2, dtype=np.float32)).astype(np.float32)
    pos = np.arange(S, dtype=np.float32)
    ang = (pos[:, None] * theta[None, :]).astype(np.float32).astype(np.float64)
    cos = np.repeat(np.cos(ang), 2, axis=-1)
    sin = np.repeat(np.sin(ang), 2, axis=-1)
    sgn = np.tile(np.array([-1.0, 1.0]), dk // 2)[None, :]
    log_g = np.log1p(-np.exp2(-5.0 - float(h)))
    i = (np.arange(S) % 128).astype(np.float64)
    qdec = np.exp((i + 1.0) * log_g)[:, None]
    kdec = np.exp(-(i + 1.0) * log_g)[:, None] * dk ** -0.5
    C = np.concatenate([cos * qdec, cos * kdec], axis=1).astype(np.float32)
    Sg = np.concatenate([sin * sgn * qdec, sin * sgn * kdec], axis=1).astype(np.float32)
    g128 = float(np.exp(128.0 * log_g))
    return C, Sg, g128


def _mix_weight_cols(h):
    cols = []
    cols += list(range(O_AQ + h * 128, O_AQ + (h + 1) * 128))
    cols += list(range(O_AK + h * 128, O_AK + (h + 1) * 128))
    cols += list(range(O_AV + h * 128, O_AV + (h + 1) * 128))
    cols += list(range(O_AZ + h * 128, O_AZ + (h + 1) * 128))
    cols += list(range(O_BQ + h * 64, O_BQ + (h + 1) * 64))
    cols += list(range(O_BK + h * 64, O_BK + (h + 1) * 64))
    cols += list(range(O_BV + h * 128, O_BV + (h + 1) * 128))
    cols += list(range(O_BZ + h * 128, O_BZ + (h + 1) * 128))
    for g in range(3):
        base = g * 1024 + h * 128
        cols += list(range(O_CQ + base, O_CQ + base + 128))
        cols += list(range(O_CK + base, O_CK + base + 128))
        cols += list(range(O_CV + base, O_CV + base + 128))
    cols += list(range(O_CZ + h * 128, O_CZ + (h + 1) * 128))
    return np.array(cols)


def mix_inputs(h, w_in_l, lam_l, da_nw_l, ret_nw_l):
    tri, mprev, ident = _const_tables()
    C, Sg, g128 = _rot_tables(h)
    vecs = np.zeros((128, NV_TOT), np.float32)
    vecs[:, NV_LAM:NV_LAM + 256] = lam_l.reshape(1, 256)
    vecs[:, NV_DNW:NV_DNW + 128] = da_nw_l.reshape(1, 128)
    vecs[:, NV_RNW:NV_RNW + 128] = ret_nw_l.reshape(1, 128)
    vecs[:, NV_TRI:NV_TRI + 128] = tri
    vecs[:, NV_MPREV:NV_MPREV + 128] = mprev
    vecs[:, NV_IDENT:NV_IDENT + 128] = ident
    vecs[:, NV_G] = g128
    w = np.ascontiguousarray(w_in_l[:, _mix_weight_cols(h)])
    return dict(w_mix=w, vecs=vecs, rotC=C, rotS=Sg)


def emit_merge(P, A, ps, final, d_x, uT_v, oz_v, d_wg, d_wp, d_wo, d_cT, d_wada, d_brow, d_fnw, d_out):
    wg_v = d_wg.rearrange("(c p) n -> p c n", p=128)
    wo_v = d_wo.rearrange("(c p) n -> p c n", p=128)
    ones = A.alloc([128, 128], F32)
    gate_bc = A.alloc([128, D], F32)
    epsc = A.alloc([128, 1], F32)
    mT = A.alloc([128, 16, TOK], BF16)
    P.op("dve", lambda e: e.memset(ones, 1.0), writes=["ones"])
    P.op("dve", lambda e: e.memset(epsc, EPS), writes=["epsc"])
    mark = A.off
    uT = A.alloc([128, 16, TOK], BF16)
    oz = A.alloc([128, 24, TOK], BF16)
    wg = [A.alloc([128, 16, 512], BF16), A.alloc([128, 16, 512], BF16)]
    wp = [A.alloc([128, 8, 512], BF16), A.alloc([128, 8, 512], BF16)]
    mark2 = A.off
    for q in range(4):
        P.op("sp", lambda e, q=q: e.dma_start(out=uT[:, q * 4:(q + 1) * 4, :], in_=uT_v[:, q * 4:(q + 1) * 4, :]),
             writes=[("uT", q)], dma="uT")
    for q in range(6):
        P.op("sp", lambda e, q=q: e.dma_start(out=oz[:, q * 4:(q + 1) * 4, :], in_=(oz_v(q) if callable(oz_v) else oz_v[:, q * 4:(q + 1) * 4, :])),
             writes=[("oz", q)], dma="oz")
    for hf in range(2):
        A.off = mark2
        acc = emit_mod_acc(P, A, d_wada, d_cT, d_brow, cols=(4096 + 1024 * hf, 5120 + 1024 * hf))
        for n2 in range(2):
            n = hf * 2 + n2
            P.op("pe", lambda e, n=n, n2=n2, acc=acc: e.matmul(ps[4 + n][:, :], lhsT=ones, rhs=acc[:, n2 * 512:(n2 + 1) * 512],
                                                               start=True, stop=True),
                 reads=["acc", "ones"], writes=[("ps", 4 + n)])
            P.op("dve", lambda e, n=n: e.tensor_copy(gate_bc[:, n * 512:(n + 1) * 512], ps[4 + n][:, :]),
                 reads=[("ps", 4 + n)], writes=["gate_bc"])
    A.off = mark2
    accm = A.alloc([128, 4, TOK], F32)
    sig = [A.alloc([128, 512], F32), A.alloc([128, 512], F32)]
    tmp = sig
    units = [(ct, b) for ct in range(4) for b in range(3)]

    def load_unit(i):
        ct, b = units[i]
        sl = i % 2
        for half in range(2):
            P.op("pool", lambda e, half=half: e.dma_start(
                out=wg[sl][:, half * 8:(half + 1) * 8, :],
                in_=wg_v[:, half * 8:(half + 1) * 8, b * D + ct * 512:b * D + (ct + 1) * 512]),
                writes=[("wg", sl, half)], dma="wg%d" % sl)
        P.op("pool", lambda e: e.dma_start(out=wp[sl], in_=d_wp[b, :, ct * 512:(ct + 1) * 512].rearrange("(c p) n -> p c n", p=128)),
             writes=[("wp", sl)], dma="wp%d" % sl)

    load_unit(0)
    cnt = {"g": 0}
    for i, (ct, b) in enumerate(units):
        if i + 1 < len(units):
            load_unit(i + 1)
        sl = i % 2
        for m in range(4):
            for nt in range(2):
                k_ = cnt["g"] % 2
                cnt["g"] += 1
                G, Y = k_, 2 + k_
                for c in range(16):
                    P.op("pe", lambda e, c=c, m=m, nt=nt, G=G, sl=sl: e.matmul(
                        ps[G][:, :], lhsT=wg[sl][:, c, m * 128:(m + 1) * 128], rhs=uT[:, c, nt * 512:(nt + 1) * 512],
                        start=(c == 0), stop=(c == 15)),
                        reads=[("wg", sl, 0), ("wg", sl, 1)] + [("uT", q) for q in range(4)], writes=[("ps", G)])
                for kc in range(8):
                    P.op("pe", lambda e, kc=kc, m=m, nt=nt, Y=Y, sl=sl, b=b: e.matmul(
                        ps[Y][:, :], lhsT=wp[sl][:, kc, m * 128:(m + 1) * 128], rhs=oz[:, b * 8 + kc, nt * 512:(nt + 1) * 512],
                        start=(kc == 0), stop=(kc == 7)),
                        reads=[("wp", sl), ("oz", 2 * b), ("oz", 2 * b + 1)], writes=[("ps", Y)])
                P.op("act", lambda e, G=G, k_=k_: e.activation(out=sig[k_], in_=ps[G][:, :], func=AF.Sigmoid),
                     reads=[("ps", G)], writes=[("sig", k_)])
                av = accm[:, m, nt * 512:(nt + 1) * 512]
                ares = ("accm", m, nt)
                if b == 0:
                    P.op("dve", lambda e, Y=Y, k_=k_, av=av: e.tensor_tensor(out=av, in0=ps[Y][:, :], in1=sig[k_], op=ALU.mult),
                         reads=[("ps", Y), ("sig", k_)], writes=[ares])
                else:
                    P.op("dve", lambda e, Y=Y, k_=k_: e.tensor_tensor(out=tmp[k_], in0=ps[Y][:, :], in1=sig[k_], op=ALU.mult),
                         reads=[("ps", Y), ("sig", k_)], writes=[("sig", k_)])
                    if b == 1:
                        P.op("dve", lambda e, k_=k_, av=av: e.tensor_tensor(out=av, in0=av, in1=tmp[k_], op=ALU.add),
                             reads=[("sig", k_), ares], writes=[ares])
                    else:
                        mv = mT[:, ct * 4 + m, nt * 512:(nt + 1) * 512]
                        P.op("dve", lambda e, k_=k_, av=av, mv=mv: e.tensor_tensor(out=mv, in0=av, in1=tmp[k_], op=ALU.add),
                             reads=[("sig", k_), ares], writes=[("mT", ct)])
    P.barrier()
    A.off = mark
    wo = A.alloc([128, 16, D], BF16)
    xr = [A.alloc([128, D], F32), A.alloc([128, D], F32)]
    orow = [A.alloc([128, D], F32), A.alloc([128, D], F32)]
    junk = A.alloc([128, 512], F32)
    ssq = A.alloc([128, 8], F32)
    var = A.alloc([128, 2], F32)
    rstd = A.alloc([128, 2], F32)
    fnw = A.alloc([128, D], F32)
    for ct2 in range(4):
        for half in range(2):
            P.op("pool", lambda e, ct2=ct2, half=half: e.dma_start(
                out=wo[:, half * 8:(half + 1) * 8, ct2 * 512:(ct2 + 1) * 512],
                in_=wo_v[:, half * 8:(half + 1) * 8, ct2 * 512:(ct2 + 1) * 512]), writes=[("wo", ct2, half)], dma="wo")
    if final:
        P.op("sp", lambda e: e.dma_start(out=fnw, in_=d_fnw), writes=["fnw"], dma="misc")
    for t in range(8):
        xb = t % 2
        P.op("sp", lambda e, t=t, xb=xb: e.dma_start(out=xr[xb], in_=d_x[t * 128:(t + 1) * 128, :]), writes=[("xr", xb)],
             dma="x%d" % xb)
        for ct2 in range(4):
            ob_ = 4 + ct2
            for c in range(16):
                P.op("pe", lambda e, c=c, t=t, ct2=ct2, ob_=ob_: e.matmul(
                    ps[ob_][:, :], lhsT=mT[:, c, t * 128:(t + 1) * 128], rhs=wo[:, c, ct2 * 512:(ct2 + 1) * 512],
                    start=(c == 0), stop=(c == 15)),
                    reads=[("wo", ct2, 0), ("wo", ct2, 1)] + [("mT", q) for q in range(4)], writes=[("ps", ob_)])
            cs = slice(ct2 * 512, (ct2 + 1) * 512)
            P.op("dve", lambda e, ob_=ob_, cs=cs, xb=xb: e.tensor_tensor(out=orow[xb][:, cs], in0=ps[ob_][:, :], in1=gate_bc[:, cs],
                                                                        op=ALU.mult),
                 reads=[("ps", ob_), "gate_bc"], writes=[("orow", xb, ct2)])
            P.op("dve", lambda e, cs=cs, xb=xb: e.tensor_tensor(out=orow[xb][:, cs], in0=orow[xb][:, cs], in1=xr[xb][:, cs], op=ALU.add),
                 reads=[("orow", xb, ct2), ("xr", xb)], writes=[("orow", xb, ct2)])
            if final:
                P.op("dve", lambda e, cs=cs, xb=xb, ct2=ct2: e.scalar_tensor_tensor(
                    out=junk, in0=orow[xb][:, cs], scalar=1.0, in1=orow[xb][:, cs], op0=ALU.mult, op1=ALU.mult,
                    accum_out=ssq[:, xb * 4 + ct2:xb * 4 + ct2 + 1]),
                    reads=[("orow", xb, ct2)], writes=["junk", ("ssq", xb, ct2)])
        ores = [("orow", xb, q) for q in range(4)]
        if final:
            P.op("dve", lambda e, xb=xb: e.tensor_tensor(out=ssq[:, xb * 4:xb * 4 + 2], in0=ssq[:, xb * 4:xb * 4 + 2],
                                                         in1=ssq[:, xb * 4 + 2:xb * 4 + 4], op=ALU.add),
                 reads=[("ssq", xb, q) for q in range(4)], writes=[("ssq", xb, 0), ("ssq", xb, 1)])
            P.op("dve", lambda e, xb=xb: e.tensor_tensor(out=ssq[:, xb * 4:xb * 4 + 1], in0=ssq[:, xb * 4:xb * 4 + 1],
                                                         in1=ssq[:, xb * 4 + 1:xb * 4 + 2], op=ALU.add),
                 reads=[("ssq", xb, 0), ("ssq", xb, 1)], writes=[("ssq", xb, 0)])
            P.op("act", lambda e, xb=xb: e.activation(out=var[:, xb:xb + 1], in_=ssq[:, xb * 4:xb * 4 + 1], func=AF.Sqrt,
                                                      scale=1.0 / D, bias=epsc), reads=[("ssq", xb, 0), "epsc"], writes=[("var", xb)])
            P.op("dve", lambda e, xb=xb: e.reciprocal(rstd[:, xb:xb + 1], var[:, xb:xb + 1]), reads=[("var", xb)],
                 writes=[("rstd", xb)])
            P.op("dve", lambda e, xb=xb: e.scalar_tensor_tensor(out=orow[xb], in0=orow[xb], scalar=rstd[:, xb:xb + 1], in1=fnw,
                                                                op0=ALU.mult, op1=ALU.mult),
                 reads=ores + [("rstd", xb), "fnw"], writes=ores)
        P.op("sp", lambda e, t=t, xb=xb: e.dma_start(out=d_out[t * 128:(t + 1) * 128, :], in_=orow[xb]), reads=ores,
             dma="out%d" % xb)


def build_merge(final):
    nc = bass.Bass("TRN2", target_bir_lowering=False)
    d_x = nc.dram_tensor("x", [TOK, D], F32, kind="ExternalInput").ap()
    d_uT = nc.dram_tensor("uT", [D, TOK], BF16, kind="ExternalInput").ap()
    d_oz = nc.dram_tensor("ozT", [3072, TOK], BF16, kind="ExternalInput").ap()
    d_wg = nc.dram_tensor("w_g", [D, 3 * D], F32, kind="ExternalInput").ap()
    d_wp = nc.dram_tensor("w_p", [3, 1024, D], F32, kind="ExternalInput").ap()
    d_wo = nc.dram_tensor("w_out", [D, D], F32, kind="ExternalInput").ap()
    d_cT = nc.dram_tensor("cT", [128, 16], F32, kind="ExternalInput").ap()
    d_wada = nc.dram_tensor("w_ada", [D, 3 * D], F32, kind="ExternalInput").ap()
    d_brow = nc.dram_tensor("brow", [1, 3 * D], F32, kind="ExternalInput").ap()
    d_fnw = nc.dram_tensor("fnw", [128, D], F32, kind="ExternalInput").ap()
    d_out = nc.dram_tensor("out", [TOK, D], F32, kind="ExternalOutput").ap()
    uT_v = d_uT.rearrange("(c p) t -> p c t", p=128)
    oz_v = d_oz.rearrange("(c p) t -> p c t", p=128)
    with ExitStack() as es:
        arena_t = es.enter_context(nc.sbuf_tensor("arena", [128, 95 * 1024], BF16))
        A = Arena(arena_t[:, :], 190 * 1024)
        ps = _ps_banks(nc, es)
        P = Prog(nc, ALL_KEYS)
        emit_merge(P, A, ps, final, d_x, uT_v, oz_v, d_wg, d_wp, d_wo, d_cT, d_wada, d_brow, d_fnw, d_out)
        P.emit()
    return nc


def build_fused(nlayers=2):
    nc = bass.Bass("TRN2", target_bir_lowering=False)
    dt = nc.dram_tensor
    d_x = dt("x", [TOK, D], F32, kind="ExternalInput").ap()
    d_cT = dt("cT", [128, 16], F32, kind="ExternalInput").ap()
    d_ident = dt("ident", [128, 128], F32, kind="ExternalInput").ap()
    d_rotC = dt("rotC", [S, 128], F32, kind="ExternalInput").ap()
    d_rotS = dt("rotS", [S, 128], F32, kind="ExternalInput").ap()
    d_fnw = dt("fnw", [128, D], F32, kind="ExternalInput").ap()
    d_out = dt("out", [TOK, D], F32, kind="ExternalOutput").ap()
    L_ = []
    for L in range(nlayers):
        L_.append(dict(
            w_ada=dt("w_ada%d" % L, [D, 3 * D], F32, kind="ExternalInput").ap(),
            brow=dt("brow%d" % L, [1, 3 * D], F32, kind="ExternalInput").ap(),
            nwT=dt("nwT%d" % L, [128, 16], F32, kind="ExternalInput").ap(),
            w_mix=dt("w_mix%d" % L, [D, WMIX], F32, kind="ExternalInput").ap(),
            vecs=dt("vecs%d" % L, [128, NV_TOT], F32, kind="ExternalInput").ap(),
            w_g=dt("w_g%d" % L, [D, 3 * D], F32, kind="ExternalInput").ap(),
            w_p=dt("w_p%d" % L, [3, 1024, D], F32, kind="ExternalInput").ap(),
            w_out=dt("w_out%d" % L, [D, D], F32, kind="ExternalInput").ap(),
        ))
    uT_own = dt("uT_own", [D, TOK], BF16, kind="Internal").ap()
    uT_all = dt("uT_all", [NCORES * D, TOK], BF16, kind="Internal").ap()
    oz_loc = dt("oz_loc", [384, S], BF16, kind="Internal").ap()
    oz_all = dt("oz_all", [NCORES * 384, S], BF16, kind="Internal").ap()
    h_own = dt("h_own", [TOK, D], F32, kind="Internal").ap()
    pid = nc.partition_id()
    uT_all_v = uT_all.rearrange("(r c p) t -> p r c t", r=NCORES, p=128)
    oz_all_v = oz_all.rearrange("(h b p) t -> p b h t", h=NCORES, b=3, p=128)
    uT_own_v = uT_own.rearrange("(c p) t -> p c t", p=128)
    groups = [list(range(NCORES))]
    with ExitStack() as es:
        arena_t = es.enter_context(nc.sbuf_tensor("arena", [128, 95 * 1024], BF16))
        A = Arena(arena_t[:, :], 190 * 1024)
        ps = _ps_banks(nc, es)
        P = Prog(nc, ALL_KEYS)
        for L in range(nlayers):
            w = L_[L]
            src_x = d_x if L == 0 else h_own
            A.off = 0
            emit_prep(P, A, ps, src_x, d_cT, w["w_ada"], w["brow"], w["nwT"], d_ident, uT_own)
            P.barrier()
            P.op("pool", lambda e: e.collective_compute("AllGather", ALU.bypass, replica_groups=groups, ins=[uT_own], outs=[uT_all]),
                 writes=["uT_all"], dma="cc", inc=1)
            P.barrier()
            A.off = 0
            emit_mix(P, A, ps, L, lambda j, q4: uT_all_v[:, j // 2, q4 * 4:(q4 + 1) * 4, (j % 2) * 512:(j % 2) * 512 + 512],
                     w["w_mix"], w["vecs"], d_rotC, d_rotS, oz_loc)
            P.op("pool", lambda e: e.collective_compute("AllGather", ALU.bypass, replica_groups=groups, ins=[oz_loc], outs=[oz_all]),
                 writes=["oz_all"], dma="cc", inc=1)
            P.barrier()
            A.off = 0
            final = (L == nlayers - 1)
            emit_merge(P, A, ps, final, src_x, uT_own_v,
                       lambda q: oz_all_v[:, q // 2, (q % 2) * 4:(q % 2) * 4 + 4, bass.ds(pid * TOK, TOK)],
                       w["w_g"], w["w_p"], w["w_out"], d_cT, w["w_ada"], w["brow"], d_fnw, d_out if final else h_own)
            P.barrier()
        P.emit()
    return nc


def fused_inputs(i, inp):
    nl = inp["w_in"].shape[0]
    m = dict(x=np.ascontiguousarray(inp["x"].reshape(S, D)[i * TOK:(i + 1) * TOK]),
             cT=np.ascontiguousarray(inp["c"].reshape(16, 128).T), ident=np.eye(128, dtype=np.float32),
             fnw=np.ascontiguousarray(np.broadcast_to(inp["final_norm_w"].reshape(1, D), (128, D))))
    for L in range(nl):
        mi = mix_inputs(i, inp["w_in"][L], inp["lam"][L], inp["da_norm_w"][L], inp["ret_norm_w"][L])
        m["rotC"], m["rotS"] = mi["rotC"], mi["rotS"]
        m["w_mix%d" % L] = mi["w_mix"]
        m["vecs%d" % L] = mi["vecs"]
        m["w_ada%d" % L] = np.ascontiguousarray(inp["w_ada"][L])
        m["brow%d" % L] = np.ascontiguousarray(inp["b_ada"][L].reshape(1, -1))
        m["nwT%d" % L] = np.ascontiguousarray(inp["norm_w"][L].reshape(16, 128).T)
        m["w_g%d" % L] = np.ascontiguousarray(inp["w_in"][L][:, O_GA:O_GA + 3 * D])
        m["w_p%d" % L] = np.ascontiguousarray(np.stack([inp["w_proj_a"][L], inp["w_proj_b"][L], inp["w_proj_c"][L]]))
        m["w_out%d" % L] = np.ascontiguousarray(inp["w_out"][L])
    return m


def merge_inputs(x_own, uT_own, ozT_own, layer, inp):
    bf = _bf16_np()
    return dict(
        x=np.ascontiguousarray(x_own, dtype=np.float32), uT=np.ascontiguousarray(uT_own).astype(bf),
        ozT=np.ascontiguousarray(ozT_own).astype(bf),
        w_g=np.ascontiguousarray(inp["w_in"][layer][:, O_GA:O_GA + 3 * D]),
        w_p=np.ascontiguousarray(np.stack([inp["w_proj_a"][layer], inp["w_proj_b"][layer], inp["w_proj_c"][layer]])),
        w_out=np.ascontiguousarray(inp["w_out"][layer]),
        cT=np.ascontiguousarray(inp["c"].reshape(16, 128).T), w_ada=np.ascontiguousarray(inp["w_ada"][layer]),
        brow=np.ascontiguousarray(inp["b_ada"][layer].reshape(1, -1)),
        fnw=np.ascontiguousarray(np.broadcast_to(inp["final_norm_w"].reshape(1, D), (128, D))),
    )


_PROGS = {}
FUSED = False
_DBG = None


def _prog(key, builder):
    if key not in _PROGS:
        _PROGS[key] = builder()
    return _PROGS[key]


def _run(nc, in_maps):
    return run_bass_kernel_spmd(nc, in_maps, core_ids=list(range(NCORES))).results


def kernel(x, c, w_in, w_proj_a, w_proj_b, w_proj_c, w_out, w_ada, b_ada, norm_w, lam, da_norm_w, ret_norm_w,
           final_norm_w):
    inp = dict(x=x, c=c, w_in=w_in, w_proj_a=w_proj_a, w_proj_b=w_proj_b, w_proj_c=w_proj_c, w_out=w_out,
               w_ada=w_ada, b_ada=b_ada, norm_w=norm_w, lam=lam, da_norm_w=da_norm_w, ret_norm_w=ret_norm_w,
               final_norm_w=final_norm_w)
    inp = {k: np.asarray(v, dtype=np.float32) for k, v in inp.items()}
    if FUSED:
        nc = _prog("fused", build_fused)
        res = _run(nc, [fused_inputs(i, inp) for i in range(NCORES)])
        out = np.concatenate([np.asarray(res[i]["out"]) for i in range(NCORES)], axis=0)
        return out.reshape(1, S, D).astype(np.float32)
    h = np.ascontiguousarray(inp["x"].reshape(S, D))
    cT = np.ascontiguousarray(inp["c"].reshape(16, 128).T)
    ident = np.eye(128, dtype=np.float32)
    nlayers = inp["w_in"].shape[0]
    for L in range(nlayers):
        nc = _prog("prep", build_prep)
        maps = [dict(x=np.ascontiguousarray(h[i * TOK:(i + 1) * TOK]), cT=cT, w_ada=np.ascontiguousarray(inp["w_ada"][L]),
                     brow=np.ascontiguousarray(inp["b_ada"][L].reshape(1, -1)),
                     nwT=np.ascontiguousarray(inp["norm_w"][L].reshape(16, 128).T), ident=ident) for i in range(NCORES)]
        res = _run(nc, maps)
        uT_own = [np.asarray(res[i]["uT"]) for i in range(NCORES)]
        uT_full = np.ascontiguousarray(np.concatenate(uT_own, axis=1))
        if _DBG is not None:
            _DBG[('uT', L)] = uT_full
        nc = _prog(("mix", L), lambda: build_mix(L))
        maps = []
        for hd in range(NCORES):
            m = mix_inputs(hd, inp["w_in"][L], inp["lam"][L], inp["da_norm_w"][L], inp["ret_norm_w"][L])
            m["uT"] = uT_full
            maps.append(m)
        res = _run(nc, maps)
        ozT = [np.asarray(res[hd]["ozT"]) for hd in range(NCORES)]
        if _DBG is not None:
            _DBG[('ozT', L)] = ozT
        final = (L == nlayers - 1)
        nc = _prog(("merge", final), lambda: build_merge(final))
        maps = []
        for i in range(NCORES):
            ts = slice(i * TOK, (i + 1) * TOK)
            oz_own = np.concatenate([ozT[hd][b * 128:(b + 1) * 128, ts] for b in range(3) for hd in range(NCORES)], axis=0)
            maps.append(merge_inputs(h[ts], uT_own[i], oz_own, L, inp))
        res = _run(nc, maps)
        h = np.ascontiguousarray(np.concatenate([np.asarray(res[i]["out"]) for i in range(NCORES)], axis=0))
        if _DBG is not None:
            _DBG[('h', L)] = h
    return h.reshape(1, S, D).astype(np.float32)
```

```python
import math
from contextlib import ExitStack

import numpy as np
import concourse.bass as bass
import concourse.mybir as mybir
from concourse.bass_utils import run_bass_kernel_spmd

F32 = mybir.dt.float32
BF16 = mybir.dt.bfloat16
AF = mybir.ActivationFunctionType
ALU = mybir.AluOpType

D = 2048
S = 8192
NCORES = 8
TOK = S // NCORES
EPS = 1e-6
IN_SPLITS = (1024, 1024, 1024, 1024, 512, 512, 1024, 1024, 3072, 3072, 3072, 1024, 2048, 2048, 2048)
OFFS = [0] + list(np.cumsum(IN_SPLITS))
(O_AQ, O_AK, O_AV, O_AZ, O_BQ, O_BK, O_BV, O_BZ, O_CQ, O_CK, O_CV, O_CZ, O_GA, O_GB, O_GC) = OFFS[:15]
DIL = (1, 4, 16)
DBG2 = 99
ALL_KEYS = ["misc", "wk0", "wk1", "x0", "x1", "out", "w", "u0", "u1", "rot0", "rot1", "out0", "out1", "uT", "oz", "wg0", "wg1", "wp0", "wp1", "wo", "cc", "hx"]

ENGS = ("pe", "act", "dve", "pool", "sp")


class Prog:
    def __init__(self, nc, dma_keys):
        self.nc = nc
        self.ops = []
        self.last_writer = {}
        self.readers = {}
        self.eng_count = {e: 0 for e in ENGS}
        self.dma_keys = list(dma_keys)
        self.dma_count = {k: 0 for k in self.dma_keys}
        self.core_idx_ap = None
        self.core_idx = None

    def op(self, eng, fn, reads=(), writes=(), dma=None, extra_waits=None, inc=16):
        ps_r = [r for r in reads if isinstance(r, tuple) and r[0] == "ps"]
        if ps_r:
            reads = [r for r in reads if r not in ps_r]
            writes = list(writes) + ps_r
        deps = set()
        for r in reads:
            w = self.last_writer.get(r)
            if w is not None:
                deps.add(w)
        for r in writes:
            w = self.last_writer.get(r)
            if w is not None:
                deps.add(w)
            for x in self.readers.get(r, ()):
                deps.add(x)
        idx = len(self.ops)
        waits = dict(extra_waits or {})
        for d in deps:
            o = self.ops[d]
            t = o["token"]
            if t[0] == "e":
                if t[1] == "pe" and eng == "pe" and dma is None:
                    continue
                key = ("e", t[1])
                waits[key] = max(waits.get(key, 0), t[2])
            else:
                key = ("d", t[1])
                waits[key] = max(waits.get(key, 0), self.dma_count[t[1]])
        if dma is None:
            self.eng_count[eng] += 1
            token = ("e", eng, self.eng_count[eng])
        else:
            self.dma_count[dma] += inc
            token = ("d", dma, self.dma_count[dma])
        self.ops.append(dict(eng=eng, fn=fn, token=token, waits=waits, dma=dma, inc=inc))
        for r in reads:
            self.readers.setdefault(r, []).append(idx)
        for r in writes:
            self.last_writer[r] = idx
            self.readers[r] = []
        return idx

    def barrier(self):
        snap = {}
        for e in ENGS:
            if self.eng_count[e]:
                snap[("e", e)] = self.eng_count[e]
        for k in self.dma_keys:
            if self.dma_count[k]:
                snap[("d", k)] = self.dma_count[k]
        for e in ENGS:
            self.ops.append(dict(eng=e, fn=None, token=None, waits=dict(snap), dma=None, inc=0))
        self.last_writer = {}
        self.readers = {}

    def emit(self):
        nc = self.nc
        with ExitStack() as es:
            sems = {}
            for e in ENGS:
                sems[("e", e)] = es.enter_context(nc.semaphore("s_" + e))
            for k in self.dma_keys:
                sems[("d", k)] = es.enter_context(nc.semaphore("d_" + str(k)))
            block = es.enter_context(nc.Block())
            per_eng = {e: [o for o in self.ops if o["eng"] == e] for e in ENGS}
            final = {}
            for e in ENGS:
                if self.eng_count[e]:
                    final[("e", e)] = self.eng_count[e]
            for k in self.dma_keys:
                if self.dma_count[k]:
                    final[("d", k)] = self.dma_count[k]

            def run(e, h):
                waited = {}
                if e == "sp" and self.core_idx_ap is not None:
                    reg = h.alloc_register("coreidx")
                    h.reg_load(reg, self.core_idx_ap)
                    self.core_idx = h.snap(reg)
                for o in per_eng[e]:
                    for key, val in o["waits"].items():
                        if key == ("e", e) and e == "pe":
                            continue
                        if waited.get(key, 0) >= val:
                            continue
                        h.wait_ge(sems[key], val)
                        waited[key] = val
                    if o["fn"] is None:
                        continue
                    ins = o["fn"](h)
                    t = o["token"]
                    if t[0] == "e":
                        ins.then_inc(sems[("e", e)], 1)
                    else:
                        ins.then_inc(sems[("d", t[1])], o["inc"])
                if e == "sp":
                    for key, val in final.items():
                        if waited.get(key, 0) >= val:
                            continue
                        h.wait_ge(sems[key], val)

            @block.tensor
            def _(h):
                run("pe", h)

            @block.scalar
            def _(h):
                run("act", h)

            @block.vector
            def _(h):
                run("dve", h)

            @block.gpsimd
            def _(h):
                run("pool", h)

            @block.sync
            def _(h):
                run("sp", h)


class Arena:
    def __init__(self, ap, nbytes):
        self.ap = ap
        self.nbytes = nbytes
        self.off = 0

    def alloc(self, shape, dtype):
        n = 1
        for s_ in shape[1:]:
            n *= s_
        esz = 4 if dtype == F32 else 2
        nb = n * esz
        off = (self.off + 63) // 64 * 64
        assert off + nb <= self.nbytes, f"arena overflow {off + nb} > {self.nbytes}"
        self.off = off + nb
        v = self.ap[0:shape[0], off // 2:(off + nb) // 2]
        if dtype == F32:
            v = v.bitcast(F32)
        if len(shape) == 3:
            v = v.rearrange("p (a b) -> p a b", a=shape[1])
        elif len(shape) == 4:
            v = v.rearrange("p (a b c) -> p a b c", a=shape[1], b=shape[2])
        return v


class _PSList(list):
    pass


def _ps_banks(nc, es):
    t = es.enter_context(nc.psum_tensor("psall", [128, 4096], F32))
    lst = _PSList(t[:, i * 512:(i + 1) * 512] for i in range(8))
    lst.all = t
    return lst


def _bf16_np():
    import ml_dtypes
    return ml_dtypes.bfloat16


def emit_mod_acc(P, A, d_wada, d_cT, d_brow, cols=(0, 6144)):
    c0, c1 = cols
    n = c1 - c0
    cT = A.alloc([128, 16], F32)
    sc = A.alloc([128, 16], F32)
    acc = A.alloc([128, n], F32)
    wk = [A.alloc([128, n], F32), A.alloc([128, n], F32)]
    brow = A.alloc([1, n], F32)
    P.op("sp", lambda e: e.dma_start(out=cT, in_=d_cT), writes=["cT"], dma="misc")
    P.op("sp", lambda e: e.dma_start(out=brow, in_=d_brow[:, c0:c1]), writes=["brow"], dma="misc")
    P.op("act", lambda e: e.activation(out=sc, in_=cT, func=AF.Silu), reads=["cT"], writes=["sc"])
    for k in range(16):
        b = k % 2
        P.op("sp", lambda e, k=k, b=b: e.dma_start(out=wk[b], in_=d_wada[k * 128:(k + 1) * 128, c0:c1]),
             writes=[("wk", b)], dma="wk%d" % b)
        if k == 0:
            P.op("dve", lambda e, b=b: e.tensor_scalar(out=acc, in0=wk[b], scalar1=sc[:, 0:1], scalar2=None,
                                                        op0=ALU.mult), reads=[("wk", b), "sc"], writes=["acc"])
        else:
            P.op("dve", lambda e, k=k, b=b: e.scalar_tensor_tensor(out=acc, in0=wk[b], scalar=sc[:, k:k + 1], in1=acc,
                                                                   op0=ALU.mult, op1=ALU.add),
                 reads=[("wk", b), "sc", "acc"], writes=["acc"])
    P.op("dve", lambda e: e.tensor_tensor(out=acc[0:1, :], in0=acc[0:1, :], in1=brow[0:1, :], op=ALU.add),
         reads=["acc", "brow"], writes=["acc"])
    if c0 <= 2048 and c1 >= 4096:
        P.op("dve", lambda e: e.tensor_scalar(out=acc[0:1, 2048 - c0:4096 - c0], in0=acc[0:1, 2048 - c0:4096 - c0],
                                              scalar1=1.0, scalar2=None, op0=ALU.add), reads=["acc"], writes=["acc"])
    return acc


def emit_prep(P, A, ps, d_x, d_cT, d_wada, d_brow, d_nwT, d_ident, d_uT):
    ident = A.alloc([128, 128], F32)
    ones = A.alloc([128, 2], F32)
    nwT = A.alloc([128, 16], F32)
    P.op("sp", lambda e: e.dma_start(out=ident, in_=d_ident), writes=["ident"], dma="misc")
    P.op("sp", lambda e: e.dma_start(out=nwT, in_=d_nwT), writes=["nwT"], dma="misc")
    P.op("dve", lambda e: e.memset(ones, 1.0), writes=["ones"])
    acc = emit_mod_acc(P, A, d_wada, d_cT, d_brow, cols=(0, 4096))
    modT = A.alloc([128, 32], F32)
    AT = A.alloc([128, 16], F32)
    for j in range(32):
        P.op("pe", lambda e, j=j: e.matmul(ps[0][:, 2 * j:2 * j + 2], lhsT=acc[:, j * 128:(j + 1) * 128],
                                           rhs=ones[:, 0:2], start=True, stop=True),
             reads=["acc", "ones"], writes=[("ps", 0)])
    P.op("dve", lambda e: e.tensor_copy(modT, ps[0][:, 0:64:2]), reads=[("ps", 0)], writes=["modT"])
    P.op("dve", lambda e: e.tensor_tensor(out=AT, in0=modT[:, 16:32], in1=nwT, op=ALU.mult),
         reads=["modT", "nwT"], writes=["AT"])
    xt = [A.alloc([128, D], F32), A.alloc([128, D], F32)]
    junk = A.alloc([128, D], F32)
    uT = A.alloc([128, 16, TOK], BF16)
    ss = A.alloc([128, 8], F32)
    var = A.alloc([128, 8], F32)
    rstd = A.alloc([128, 8], F32)
    epsc = A.alloc([128, 1], F32)
    P.op("dve", lambda e: e.memset(epsc, EPS), writes=["epsc"])
    for t in range(8):
        b = t % 2
        P.op("sp", lambda e, t=t, b=b: e.dma_start(out=xt[b], in_=d_x[t * 128:(t + 1) * 128, :]),
             writes=[("xt", b)], dma="x%d" % b)
        P.op("act", lambda e, t=t, b=b: e.activation(out=junk, in_=xt[b], func=AF.Square, accum_out=ss[:, t:t + 1]),
             reads=[("xt", b)], writes=["junk", ("ss", t)])
        P.op("act", lambda e, t=t: e.activation(out=var[:, t:t + 1], in_=ss[:, t:t + 1], func=AF.Sqrt, scale=1.0 / D,
                                                bias=epsc), reads=[("ss", t), "epsc"], writes=[("var", t)])
        P.op("dve", lambda e, t=t: e.reciprocal(rstd[:, t:t + 1], var[:, t:t + 1]),
             reads=[("var", t)], writes=[("rstd", t)])
        P.op("dve", lambda e, t=t, b=b: e.tensor_scalar(out=xt[b], in0=xt[b], scalar1=rstd[:, t:t + 1], scalar2=None,
                                                        op0=ALU.mult), reads=[("xt", b), ("rstd", t)], writes=[("xt", b)])
        for g in range(4):
            pb = 1 + (t * 4 + g) % 4
            for q in range(4):
                c = g * 4 + q
                P.op("pe", lambda e, pb=pb, q=q, c=c, b=b: e.transpose(ps[pb][:, q * 128:(q + 1) * 128],
                                                                      xt[b][:, c * 128:(c + 1) * 128], ident),
                     reads=[("xt", b), "ident"], writes=[("ps", pb)])
            for q in range(4):
                c = g * 4 + q
                P.op("dve", lambda e, pb=pb, q=q, c=c, t=t: e.tensor_scalar(
                    out=uT[:, c, t * 128:(t + 1) * 128], in0=ps[pb][:, q * 128:(q + 1) * 128],
                    scalar1=AT[:, c:c + 1], scalar2=modT[:, c:c + 1], op0=ALU.mult, op1=ALU.add),
                    reads=[("ps", pb), "AT", "modT"], writes=[("uT", t)])
    P.op("sp", lambda e: e.dma_start(out=d_uT.rearrange("(c p) t -> p c t", p=128), in_=uT),
         reads=[("uT", t) for t in range(8)], dma="out")


def build_prep():
    nc = bass.Bass("TRN2", target_bir_lowering=False)
    d_x = nc.dram_tensor("x", [TOK, D], F32, kind="ExternalInput").ap()
    d_cT = nc.dram_tensor("cT", [128, 16], F32, kind="ExternalInput").ap()
    d_wada = nc.dram_tensor("w_ada", [D, 3 * D], F32, kind="ExternalInput").ap()
    d_brow = nc.dram_tensor("brow", [1, 3 * D], F32, kind="ExternalInput").ap()
    d_nwT = nc.dram_tensor("nwT", [128, 16], F32, kind="ExternalInput").ap()
    d_ident = nc.dram_tensor("ident", [128, 128], F32, kind="ExternalInput").ap()
    d_uT = nc.dram_tensor("uT", [D, TOK], BF16, kind="ExternalOutput").ap()
    with ExitStack() as es:
        arena_t = es.enter_context(nc.sbuf_tensor("arena", [128, 95 * 1024], BF16))
        A = Arena(arena_t[:, :], 190 * 1024)
        ps = _ps_banks(nc, es)
        P = Prog(nc, ALL_KEYS)
        emit_prep(P, A, ps, d_x, d_cT, d_wada, d_brow, d_nwT, d_ident, d_uT)
        P.emit()
    return nc


NV_LAM = 0
NV_DNW = 256
NV_RNW = 384
NV_TRI = 512
NV_MPREV = 640
NV_IDENT = 768
NV_G = 896
NV_TOT = 904
WMIX = 2176
WA0, WB0, WC0 = 0, 512, 896


def emit_mix(P, A, ps, layer, d_uT_tile, d_w, d_vecs, d_rotC, d_rotS, d_oz, passes=("A", "B", "C"), ntiles=16, dbg=99,
             after_pass=None):
    lam_init = 0.8 - 0.6 * math.exp(-0.3 * layer)
    w_v = d_w.rearrange("(c p) n -> p c n", p=128)
    vecs = A.alloc([128, NV_TOT], F32)
    P.op("sp", lambda e: e.dma_start(out=vecs, in_=d_vecs), writes=["vecs"], dma="misc")
    tri_bf = A.alloc([128, 128], BF16)
    mask_c = A.alloc([128, 2, 2, 128], BF16)
    ones_bf = A.alloc([128, 128], BF16)
    epsc = A.alloc([128, 1], F32)
    ident = vecs[:, NV_IDENT:NV_IDENT + 128]
    tri_f = vecs[:, NV_TRI:NV_TRI + 128]
    P.op("dve", lambda e: e.tensor_copy(tri_bf, tri_f), reads=["vecs"], writes=["consts"])
    for r_ in range(2):
        P.op("dve", lambda e, r_=r_: e.tensor_copy(mask_c[:, r_, 0, :], vecs[:, NV_MPREV:NV_MPREV + 128]),
             reads=["vecs"], writes=["consts"])
        P.op("dve", lambda e, r_=r_: e.tensor_copy(mask_c[:, r_, 1, :], tri_f), reads=["vecs"], writes=["consts"])
    P.op("dve", lambda e: e.memset(ones_bf, 1.0), writes=["consts"])
    P.op("dve", lambda e: e.memset(epsc, EPS), writes=["epsc"])
    U = [A.alloc([128, 16, 512], BF16), A.alloc([128, 16, 512], BF16)]
    mark = A.off

    def load_U(j):
        b = j % 2
        for q4 in range(4):
            P.op("sp", lambda e, j=j, b=b, q4=q4: e.dma_start(out=U[b][:, q4 * 4:(q4 + 1) * 4, :],
                                                             in_=d_uT_tile(j, q4)),
                 writes=[("U", b, q4)], dma="u%d" % b)

    def load_w(wt, c0, ncols):
        for half in range(2):
            P.op("pool", lambda e, half=half: e.dma_start(out=wt[:, half * 8:(half + 1) * 8, :],
                                                         in_=w_v[:, half * 8:(half + 1) * 8, c0:c0 + ncols]),
                 writes=[("w", half)], dma="w")

    ostate = {"n": 0}

    def store_oz(stage_alloc, row0, j, fill):
        b = ostate["n"] % 2
        ostate["n"] += 1
        st = stage_alloc[b]
        fill(st, ("ozst", b))
        dst = d_oz(row0, j) if callable(d_oz) else d_oz[row0:row0 + 128, j * 512:(j + 1) * 512]
        P.op("sp", lambda e, st=st, dst=dst: e.dma_start(out=dst, in_=st),
             reads=[("ozst", b)], dma="out%d" % b)

    def pass_a():
        A.off = mark
        wA = A.alloc([128, 16, 512], BF16)
        kT = A.alloc([128, S], BF16)
        V = A.alloc([128, 64, 144], BF16)
        qT = [A.alloc([128, 512], BF16), A.alloc([128, 512], BF16)]
        sz = [A.alloc([128, 4, 128], F32), A.alloc([128, 4, 128], F32)]
        NPT = 4
        pT = [A.alloc([128, 512], BF16) for _ in range(NPT)]
        om0 = A.alloc([128, 4, 128], F32)
        om1 = A.alloc([128, 4, 128], F32)
        osum = A.alloc([128, 4, 128], F32)
        junk = A.alloc([128, 128], F32)
        ozb = [A.alloc([128, 4, 128], F32), A.alloc([128, 4, 128], F32)]
        ozst = [A.alloc([128, 512], BF16), A.alloc([128, 512], BF16)]
        rs = A.alloc([128, 8], F32)
        ssq = A.alloc([128, 4], F32)
        var = A.alloc([128, 4], F32)
        rstd = A.alloc([128, 4], F32)
        lf = A.alloc([128, 8], F32)
        dnw = A.alloc([128, 128], F32)
        load_w(wA, WA0, 512)
        load_U(0)
        P.op("dve", lambda e: e.memset(V[:, :, 128:144], 1.0), writes=["Vones"])
        lam = vecs[:, NV_LAM:NV_LAM + 256]
        P.op("dve", lambda e: e.scalar_tensor_tensor(out=junk[:, 0:64], in0=lam[:, 0:64], scalar=1.0, in1=lam[:, 64:128],
                                                     op0=ALU.mult, op1=ALU.mult, accum_out=lf[:, 0:1]),
             reads=["vecs"], writes=["junk", "lf"])
        P.op("dve", lambda e: e.scalar_tensor_tensor(out=junk[:, 0:64], in0=lam[:, 128:192], scalar=1.0, in1=lam[:, 192:256],
                                                     op0=ALU.mult, op1=ALU.mult, accum_out=lf[:, 1:2]),
             reads=["vecs", "junk", "lf"], writes=["junk", "lf"])
        P.op("act", lambda e: e.activation(out=lf[:, 2:4], in_=lf[:, 0:2], func=AF.Exp), reads=["lf"], writes=["lf"])
        P.op("dve", lambda e: e.tensor_tensor(out=lf[:, 4:5], in0=lf[:, 3:4], in1=lf[:, 2:3], op=ALU.subtract),
             reads=["lf"], writes=["lf"])
        P.op("dve", lambda e: e.tensor_scalar(out=lf[:, 4:5], in0=lf[:, 4:5], scalar1=-lam_init, scalar2=None, op0=ALU.add),
             reads=["lf"], writes=["lf"])
        P.op("dve", lambda e: e.tensor_scalar(out=dnw, in0=vecs[:, NV_DNW:NV_DNW + 128], scalar1=1.0 - lam_init,
                                              scalar2=None, op0=ALU.mult), reads=["vecs"], writes=["dnw"])
        acc_b = [0, 1, 2, 3]
        sc_b = [4, 5]
        pj_b = [6, 7]
        cnt = {"s": 0, "pj": 0}

        def projA(j):
            b = j % 2
            rd = [("U", b, 0), ("U", b, 1), ("U", b, 2), ("U", b, 3), ("w", 0), ("w", 1)]
            for which in range(2 if DBG2 >= 1 else 1):
                pb = pj_b[cnt["pj"] % 2]
                cnt["pj"] += 1
                for c in range(16):
                    P.op("pe", lambda e, pb=pb, c=c, b=b, which=which: e.matmul(
                        ps[pb][:, :], lhsT=wA[:, c, which * 128:(which + 1) * 128], rhs=U[b][:, c, :],
                        start=(c == 0), stop=(c == 15)), reads=rd, writes=[("ps", pb)])
                if which == 0:
                    P.op("dve", lambda e, pb=pb, b=b: e.tensor_copy(qT[b], ps[pb][:, :]), reads=[("ps", pb)],
                         writes=[("qT", b)])
                else:
                    P.op("dve", lambda e, pb=pb, j=j: e.tensor_copy(kT[:, j * 512:(j + 1) * 512], ps[pb][:, :]),
                         reads=[("ps", pb)], writes=[("kT", j)])
            if DBG2 < 2:
                return
            for s_ in range({3: 4, 4: 4, 5: 2}.get(DBG2, 4) if DBG2 >= 3 else 1):
                pb = pj_b[cnt["pj"] % 2]
                cnt["pj"] += 1
                for c in range(16):
                    P.op("pe", lambda e, pb=pb, c=c, b=b, s_=s_: e.matmul(
                        ps[pb][:, 0:256], lhsT=U[b][:, c, s_ * 128:(s_ + 1) * 128], rhs=wA[:, c, 256:512],
                        start=(c == 0), stop=(c == 15)), reads=rd, writes=[("ps", pb)])
                if DBG2 != 4:
                    P.op("dve", lambda e, pb=pb, j=j, s_=s_: e.tensor_copy(V[:, j * 4 + s_, 0:128], ps[pb][:, 0:128]),
                         reads=[("ps", pb)], writes=[("V", j)])
                if DBG2 != 3:
                    P.op("act", lambda e, pb=pb, b=b, s_=s_: e.activation(out=sz[b][:, s_, :], in_=ps[pb][:, 128:256],
                                                                         func=AF.Silu), reads=[("ps", pb)], writes=[("sz", b)])

        def attnA(j, m):
            b = j % 2
            nkb = 4 * j + 4
            slots = {}

            def SE(kb):
                lo = max(0, kb - 4 * j)
                w_ = (4 - lo) * 128
                sb_ = sc_b[cnt["s"] % 2]
                slot = cnt["s"] % NPT
                cnt["s"] += 1
                slots[kb] = slot
                P.op("pe", lambda e: e.matmul(ps[sb_][:, 0:w_], lhsT=kT[64 * m:64 * m + 64, kb * 128:(kb + 1) * 128],
                                              rhs=qT[b][64 * m:64 * m + 64, lo * 128:512], start=True, stop=True),
                     reads=[("kT", kb // 4), ("qT", b)], writes=[("ps", sb_)])
                P.op("act", lambda e: e.activation(out=pT[slot][:, 0:w_], in_=ps[sb_][:, 0:w_], func=AF.Exp, scale=0.125),
                     reads=[("ps", sb_)], writes=[("pT", slot)])
                if kb >= 4 * j:
                    P.op("dve", lambda e: e.tensor_tensor(out=pT[slot][:, 0:128], in0=pT[slot][:, 0:128], in1=tri_bf,
                                                          op=ALU.mult), reads=[("pT", slot), "consts"], writes=[("pT", slot)])

            def PV(kb):
                lo = max(0, kb - 4 * j)
                slot = slots[kb]
                for qb in range(lo, 4):
                    P.op("pe", lambda e, qb=qb: e.matmul(ps[acc_b[qb]][:, 0:129],
                                                         lhsT=pT[slot][:, (qb - lo) * 128:(qb - lo + 1) * 128],
                                                         rhs=V[:, kb, 0:129], start=(kb == 0), stop=(kb == 4 * j + qb)),
                         reads=[("pT", slot), ("V", kb // 4), "Vones"], writes=[("ps", acc_b[qb])])

            SE(0)
            for kb in range(nkb):
                if kb + 1 < nkb:
                    SE(kb + 1)
                PV(kb)
            for qb in range(4):
                ab = acc_b[qb]
                col = m * 4 + qb
                P.op("dve", lambda e, ab=ab, col=col: e.reciprocal(rs[:, col:col + 1], ps[ab][:, 128:129]),
                     reads=[("ps", ab)], writes=[("rs", col)])
                if m == 0:
                    P.op("dve", lambda e, ab=ab, col=col, qb=qb: e.tensor_scalar(
                        out=om0[:, qb, :], in0=ps[ab][:, 0:128], scalar1=rs[:, col:col + 1], scalar2=None, op0=ALU.mult),
                        reads=[("ps", ab), ("rs", col)], writes=["om0"])
                else:
                    P.op("dve", lambda e, ab=ab, col=col, qb=qb: e.tensor_scalar(
                        out=om1[:, qb, :], in0=ps[ab][:, 0:128], scalar1=rs[:, col:col + 1], scalar2=lf[:, 4:5],
                        op0=ALU.mult, op1=ALU.mult), reads=[("ps", ab), ("rs", col), "lf"], writes=["om1"])

        def combineA(j):
            b = j % 2
            P.op("dve", lambda e: e.tensor_tensor(out=osum, in0=om0, in1=om1, op=ALU.add), reads=["om0", "om1"],
                 writes=["osum"])
            for qb in range(4):
                P.op("dve", lambda e, qb=qb: e.scalar_tensor_tensor(out=junk, in0=osum[:, qb, :], scalar=1.0, in1=osum[:, qb, :],
                                                                    op0=ALU.mult, op1=ALU.mult,
                                                                    accum_out=ssq[:, qb:qb + 1]),
                     reads=["osum"], writes=["junk", "ssq"])
            P.op("act", lambda e: e.activation(out=var, in_=ssq, func=AF.Sqrt, scale=1.0 / 128, bias=epsc),
                 reads=["ssq", "epsc"], writes=["var"])
            P.op("dve", lambda e: e.reciprocal(rstd, var), reads=["var"], writes=["rstd"])
            for qb in range(4):
                P.op("dve", lambda e, qb=qb: e.scalar_tensor_tensor(out=osum[:, qb, :], in0=osum[:, qb, :],
                                                                    scalar=rstd[:, qb:qb + 1], in1=dnw, op0=ALU.mult,
                                                                    op1=ALU.mult), reads=["osum", "rstd", "dnw"],
                     writes=["osum"])
            P.op("dve", lambda e: e.tensor_tensor(out=ozb[b], in0=osum, in1=sz[b], op=ALU.mult),
                 reads=["osum", ("sz", b)], writes=[("ozb", b)])

        def finA(j):
            b = j % 2
            pb = pj_b[cnt["pj"] % 2]
            cnt["pj"] += 1
            for qb in range(4):
                P.op("pe", lambda e, qb=qb: e.transpose(ps[pb][:, qb * 128:(qb + 1) * 128], ozb[b][:, qb, :], ident),
                     reads=[("ozb", b), "vecs"], writes=[("ps", pb)])

            def fill(st, res):
                P.op("dve", lambda e: e.tensor_copy(st, ps[pb][:, :]), reads=[("ps", pb)], writes=[res])
            store_oz(ozst, 0, j, fill)

        for j in range(ntiles):
            if j + 1 < ntiles:
                load_U(j + 1)
            if dbg >= 1:
                projA(j)
            if j > 0 and dbg >= 4:
                finA(j - 1)
            if dbg >= 2:
                attnA(j, 0)
                attnA(j, 1)
            if dbg >= 3:
                combineA(j)
        if dbg >= 4:
            finA(ntiles - 1)
        P.barrier()
    if "A" in passes:
        pass_a()
        if after_pass is not None:
            after_pass("A")

    def pass_b():
        A.off = mark
        wB = A.alloc([128, 16, 384], BF16)
        rC = [A.alloc([128, 4, 128], F32), A.alloc([128, 4, 128], F32)]
        rS = [A.alloc([128, 4, 128], F32), A.alloc([128, 4, 128], F32)]
        tri4 = A.alloc([128, 4, 128], F32)
        t1 = A.alloc([128, 4, 128], F32)
        t2 = A.alloc([128, 4, 128], F32)
        qk = A.alloc([128, 4, 128], F32)
        kd_bf = A.alloc([128, 4, 64], BF16)
        v_b = A.alloc([128, 4, 128], BF16)
        szb = A.alloc([128, 4, 128], F32)
        qdT = A.alloc([64, 4, 128], BF16)
        kdT = A.alloc([64, 4, 128], BF16)
        pTb = A.alloc([128, 4, 128], BF16)
        Sst = A.alloc([64, 128], F32)
        Rbf = [A.alloc([64, 128], BF16), A.alloc([64, 128], BF16)]
        ob = A.alloc([128, 4, 128], F32)
        sq = A.alloc([128, 4, 128], F32)
        ssq = A.alloc([128, 4], F32)
        var = A.alloc([128, 4], F32)
        rstd = A.alloc([128, 4], F32)
        ozbB = A.alloc([128, 4, 128], F32)
        ozstB = [A.alloc([128, 512], BF16), A.alloc([128, 512], BF16)]
        rnw = vecs[:, NV_RNW:NV_RNW + 128]
        gcol = vecs[0:64, NV_G:NV_G + 1]
        load_w(wB, WB0, 384)
        load_U(0)
        for ci in range(4):
            P.op("dve", lambda e, ci=ci: e.tensor_copy(tri4[:, ci, :], tri_f), reads=["vecs"], writes=["tri4"])

        def load_rot(j):
            b = j % 2
            P.op("sp", lambda e: e.dma_start(out=rC[b], in_=d_rotC[j * 512:(j + 1) * 512, :].rearrange("(c p) n -> p c n", p=128)),
                 writes=[("rC", b)], dma="rot%d" % b)
            P.op("sp", lambda e: e.dma_start(out=rS[b], in_=d_rotS[j * 512:(j + 1) * 512, :].rearrange("(c p) n -> p c n", p=128)),
                 writes=[("rS", b)], dma="rot%d" % b)

        load_rot(0)
        pj4 = ps.all[:, 0:2048].rearrange("p (c n) -> p c n", c=4)
        PJR = [("ps", 0), ("ps", 1), ("ps", 2), ("ps", 3)]
        TQK, SC, OO, KV = 4, 5, 6, 7
        nR = {"n": 0}

        def proj_chunk(j, ci):
            b = j % 2
            rd = [("U", b, 0), ("U", b, 1), ("U", b, 2), ("U", b, 3), ("w", 0), ("w", 1)]
            for c in range(16):
                P.op("pe", lambda e, c=c, ci=ci, b=b: e.matmul(ps[ci][:, 0:384], lhsT=U[b][:, c, ci * 128:(ci + 1) * 128],
                                                               rhs=wB[:, c, :], start=(c == 0), stop=(c == 15)),
                     reads=rd, writes=[("ps", ci)])

        if ntiles > 1:
            load_U(1)
        for ci in range(4):
            proj_chunk(0, ci)
        for j in range(ntiles):
            b = j % 2
            if j + 1 < ntiles:
                load_rot(j + 1)
            if j + 2 < ntiles:
                load_U(j + 2)
            P.op("dve", lambda e, b=b: e.tensor_tensor(out=t1, in0=pj4[:, :, 0:128], in1=rC[b], op=ALU.mult),
                 reads=PJR + [("rC", b)], writes=["t1"])
            P.op("dve", lambda e, b=b: e.tensor_tensor(out=t2[:, :, 0:128:2], in0=pj4[:, :, 1:128:2], in1=rS[b][:, :, 0:128:2],
                                                       op=ALU.mult), reads=PJR + [("rS", b)], writes=["t2e"])
            P.op("dve", lambda e, b=b: e.tensor_tensor(out=t2[:, :, 1:128:2], in0=pj4[:, :, 0:128:2], in1=rS[b][:, :, 1:128:2],
                                                       op=ALU.mult), reads=PJR + [("rS", b)], writes=["t2o"])
            P.op("act", lambda e: e.activation(out=v_b, in_=pj4[:, :, 128:256], func=AF.Copy), reads=PJR, writes=["v_b"])
            P.op("act", lambda e: e.activation(out=szb, in_=pj4[:, :, 256:384], func=AF.Silu), reads=PJR, writes=["szb"])
            P.op("dve", lambda e: e.tensor_tensor(out=qk, in0=t1, in1=t2, op=ALU.add), reads=["t1", "t2e", "t2o"], writes=["qk"])
            P.op("dve", lambda e: e.tensor_copy(kd_bf, qk[:, :, 64:128]), reads=["qk"], writes=["kd_bf"])
            for ci in range(4):
                P.op("pe", lambda e, ci=ci: e.transpose(ps[TQK][0:64, ci * 128:(ci + 1) * 128], qk[:, ci, 0:64], ident),
                     reads=["qk", "vecs"], writes=[("ps", TQK)])
            P.op("act", lambda e: e.activation(out=qdT, in_=ps[TQK][0:64, :].rearrange("p (c n) -> p c n", c=4), func=AF.Copy),
                 reads=[("ps", TQK)], writes=["qdT"])
            for ci in range(4):
                P.op("pe", lambda e, ci=ci: e.transpose(ps[TQK][0:64, ci * 128:(ci + 1) * 128], qk[:, ci, 64:128], ident),
                     reads=["qk", "vecs"], writes=[("ps", TQK)])
            P.op("dve", lambda e: e.tensor_copy(kdT, ps[TQK][0:64, :].rearrange("p (c n) -> p c n", c=4)),
                 reads=[("ps", TQK)], writes=["kdT"])
            for ci in range(4):
                P.op("pe", lambda e, ci=ci: e.matmul(ps[SC][:, ci * 128:(ci + 1) * 128], lhsT=kdT[:, ci, :], rhs=qdT[:, ci, :],
                                                     start=True, stop=True), reads=["kdT", "qdT"], writes=[("ps", SC)])
            P.op("dve", lambda e: e.tensor_tensor(out=pTb, in0=ps[SC][:, :].rearrange("p (c n) -> p c n", c=4), in1=tri4, op=ALU.mult),
                 reads=[("ps", SC), "tri4"], writes=["pTb"])
            for ci in range(4):
                n = j * 4 + ci
                rb = nR["n"] % 2
                if j + 1 < ntiles:
                    proj_chunk(j + 1, ci)
                P.op("pe", lambda e, ci=ci, n=n: e.matmul(ps[OO][:, ci * 128:(ci + 1) * 128], lhsT=pTb[:, ci, :], rhs=v_b[:, ci, :],
                                                          start=True, stop=(n == 0)), reads=["pTb", "v_b"], writes=[("ps", OO)])
                if n > 0:
                    P.op("pe", lambda e, ci=ci, rb=rb: e.matmul(ps[OO][:, ci * 128:(ci + 1) * 128], lhsT=qdT[:, ci, :], rhs=Rbf[rb],
                                                                start=False, stop=True), reads=["qdT", ("Rbf", rb)], writes=[("ps", OO)])
                P.op("pe", lambda e, ci=ci: e.matmul(ps[KV][0:64, 0:128], lhsT=kd_bf[:, ci, :], rhs=v_b[:, ci, :], start=True, stop=True),
                     reads=["kd_bf", "v_b"], writes=[("ps", KV)])
                nR["n"] += 1
                rb2 = nR["n"] % 2
                if n == 0:
                    P.op("dve", lambda e: e.tensor_scalar(out=Sst, in0=ps[KV][0:64, 0:128], scalar1=gcol, scalar2=None, op0=ALU.mult),
                         reads=[("ps", KV), "vecs"], writes=["Sst"])
                else:
                    P.op("dve", lambda e: e.tensor_tensor(out=Sst, in0=ps[KV][0:64, 0:128], in1=Sst, op=ALU.add),
                         reads=[("ps", KV), "Sst"], writes=["Sst"])
                    P.op("dve", lambda e: e.tensor_scalar(out=Sst, in0=Sst, scalar1=gcol, scalar2=None, op0=ALU.mult),
                         reads=["Sst", "vecs"], writes=["Sst"])
                P.op("dve", lambda e, rb2=rb2: e.tensor_copy(Rbf[rb2], Sst), reads=["Sst"], writes=[("Rbf", rb2)])
            P.op("act", lambda e: e.activation(out=ob, in_=ps[OO][:, :].rearrange("p (c n) -> p c n", c=4), func=AF.Copy),
                 reads=[("ps", OO)], writes=["ob"])
            P.op("dve", lambda e: e.tensor_tensor(out=sq, in0=ob, in1=ob, op=ALU.mult), reads=["ob"], writes=["sq"])
            P.op("dve", lambda e: e.tensor_reduce(out=ssq, in_=sq, axis=mybir.AxisListType.X, op=ALU.add), reads=["sq"], writes=["ssq"])
            P.op("act", lambda e: e.activation(out=var, in_=ssq, func=AF.Sqrt, scale=1.0 / 128, bias=epsc),
                 reads=["ssq", "epsc"], writes=["var"])
            P.op("dve", lambda e: e.reciprocal(rstd, var), reads=["var"], writes=["rstd"])
            for ci in range(4):
                P.op("dve", lambda e, ci=ci: e.scalar_tensor_tensor(out=ob[:, ci, :], in0=ob[:, ci, :], scalar=rstd[:, ci:ci + 1],
                                                                    in1=rnw, op0=ALU.mult, op1=ALU.mult),
                     reads=["ob", "rstd", "vecs"], writes=["ob"])
            P.op("dve", lambda e: e.tensor_tensor(out=ozbB, in0=ob, in1=szb, op=ALU.mult), reads=["ob", "szb"], writes=["ozbB"])
            for ci in range(4):
                P.op("pe", lambda e, ci=ci: e.transpose(ps[SC][:, ci * 128:(ci + 1) * 128], ozbB[:, ci, :], ident),
                     reads=["ozbB", "vecs"], writes=[("ps", SC)])

            def fill(st, res):
                P.op("act", lambda e: e.activation(out=st, in_=ps[SC][:, :], func=AF.Copy), reads=[("ps", SC)], writes=[res])
            store_oz(ozstB, 128, j, fill)
        P.barrier()

    if "B" in passes:
        pass_b()
        if after_pass is not None:
            after_pass("B")

    def pass_c():
        A.off = mark
        wC = A.alloc([128, 16, 512], BF16)
        qTs = A.alloc([128, 2048], BF16)
        kTs = A.alloc([128, S], BF16)
        vTs = A.alloc([128, 2048], F32)
        Vs = A.alloc([128, 64, 128], BF16)
        accO = A.alloc([128, S], F32)
        accZ = A.alloc([128, S], F32)
        szT = A.alloc([128, 2048], F32)
        NPC = 3
        pTc = [A.alloc([128, 2, 2, 128], BF16) for _ in range(NPC)]
        rz = A.alloc([128, 512], F32)
        ozstC = [A.alloc([128, 512], BF16), A.alloc([128, 512], BF16)]
        PJc = [0, 1]
        SCc = [2, 3]
        PO, PZ, TRc = 4, 5, 6
        cntc = {"pj": 0, "sc": 0}
        SCALE_C = 128.0 ** -0.5

        for g in range(3):
            d = DIL[g]
            Ls = 128 * d
            ncols = 512 if g == 2 else 384
            c0 = WC0 + g * 384
            if g == 2:
                pass
            load_w(wC[:, :, 0:ncols], c0, ncols)
            load_U(0)
            tiles_per_span = max(1, Ls // 512)
            for j in range(ntiles):
                b = j % 2
                if j + 1 < ntiles:
                    load_U(j + 1)
                rd = [("U", b, 0), ("U", b, 1), ("U", b, 2), ("U", b, 3), ("w", 0), ("w", 1)]
                soff = (j * 512) % Ls
                for which in range(ncols // 128):
                    pb = PJc[cntc["pj"] % 2]
                    cntc["pj"] += 1
                    for c in range(16):
                        P.op("pe", lambda e, pb=pb, c=c, b=b, which=which: e.matmul(
                            ps[pb][:, :], lhsT=wC[:, c, which * 128:(which + 1) * 128], rhs=U[b][:, c, :],
                            start=(c == 0), stop=(c == 15)), reads=rd, writes=[("ps", pb)])
                    if which == 3:
                        P.op("act", lambda e, pb=pb, soff=soff: e.activation(out=szT[:, soff:soff + 512], in_=ps[pb][:, :],
                                                                            func=AF.Silu), reads=[("ps", pb)], writes=[("szT", j % 4)])
                        continue
                    if which == 0:
                        dst_buf, dbase, res = qTs, 0, ("qTs", j % tiles_per_span)
                    elif which == 1:
                        dst_buf, dbase, res = kTs, (j * 512) // Ls * Ls, ("kTs", j)
                    else:
                        dst_buf, dbase, res = vTs, 0, ("vTs", j % tiles_per_span)
                    if d == 1:
                        dst = dst_buf[:, dbase + soff:dbase + soff + 512] if which == 1 else dst_buf[:, 0:512]
                        src = ps[pb][:, :]
                    elif d == 4:
                        dst = dst_buf[:, dbase:dbase + 512].rearrange("p (r i) -> p r i", r=4)
                        src = ps[pb][:, :].rearrange("p (i r) -> p r i", r=4)
                    else:
                        q4 = soff // 512
                        dst = dst_buf[:, dbase:dbase + 2048].rearrange("p (r i) -> p r i", r=16)[:, :, 32 * q4:32 * q4 + 32]
                        src = ps[pb][:, :].rearrange("p (i r) -> p r i", r=16)
                    if which == 1:
                        P.op("act", lambda e, dst=dst, src=src: e.activation(out=dst, in_=src, func=AF.Copy),
                             reads=[("ps", pb)], writes=[res])
                    else:
                        P.op("dve", lambda e, dst=dst, src=src: e.tensor_copy(dst, src), reads=[("ps", pb)], writes=[res])
                if (j * 512 + 512) % Ls != 0 and Ls > 512:
                    continue
                if d == 1:
                    spans = [j * 4 + q for q in range(4)]
                else:
                    spans = [(j * 512) // Ls]
                blocks = []
                for n in spans:
                    for r in range(d):
                        blk = n * d + r
                        if d == 1:
                            lpos = (n - j * 4) * 128
                        else:
                            lpos = r * 128
                        blocks.append((n, r, blk, lpos))
                tl = [j - t_ for t_ in range(tiles_per_span)]
                for gi in range(0, len(blocks), 4):
                    grp = blocks[gi:gi + 4]
                    for k_, (n, r, blk, lpos) in enumerate(grp):
                        P.op("pe", lambda e, k_=k_, lpos=lpos: e.transpose(ps[TRc][:, k_ * 128:(k_ + 1) * 128],
                                                                          vTs[:, lpos:lpos + 128], ident),
                             reads=[("vTs", t_ % tiles_per_span) for t_ in tl] + ["vecs"], writes=[("ps", TRc)])
                    blk0 = grp[0][2]
                    P.op("dve", lambda e, blk0=blk0, ng=len(grp): e.tensor_copy(
                        Vs[:, blk0:blk0 + ng, :], ps[TRc][:, 0:ng * 128].rearrange("p (a b) -> p a b", a=ng)),
                        reads=[("ps", TRc)], writes=[("Vs", blk0 // 4)])
                for pi in range(0, len(blocks), 2):
                    pair = blocks[pi:pi + 2]
                    firsts = [p_[0] == 0 for p_ in pair]
                    sb_ = SCc[cntc["sc"] % 2]
                    slot = cntc["sc"] % NPC
                    cntc["sc"] += 1
                    qres = [("qTs", t_ % tiles_per_span) for t_ in tl]
                    for u, (n, r, blk, lpos) in enumerate(pair):
                        if not firsts[u]:
                            P.op("pe", lambda e, u=u, blk=blk, lpos=lpos, sb_=sb_, d=d: e.matmul(
                                ps[sb_][:, u * 256:u * 256 + 128], lhsT=kTs[:, (blk - d) * 128:(blk - d + 1) * 128],
                                rhs=qTs[:, lpos:lpos + 128], start=True, stop=True),
                                reads=qres + [("kTs", ((blk - d) * 128) // 512)], writes=[("ps", sb_)])
                        P.op("pe", lambda e, u=u, blk=blk, lpos=lpos, sb_=sb_: e.matmul(
                            ps[sb_][:, u * 256 + 128:u * 256 + 256], lhsT=kTs[:, blk * 128:(blk + 1) * 128],
                            rhs=qTs[:, lpos:lpos + 128], start=True, stop=True),
                            reads=qres + [("kTs", t_) for t_ in tl], writes=[("ps", sb_)])
                    psv = ps[sb_][:, :].rearrange("p (u h q) -> p u h q", u=2, h=2)
                    if not any(firsts):
                        P.op("act", lambda e, psv=psv, slot=slot: e.activation(out=pTc[slot], in_=psv, func=AF.Exp, scale=SCALE_C),
                             reads=[("ps", sb_)], writes=[("pTc", slot)])
                        P.op("dve", lambda e, slot=slot: e.tensor_tensor(out=pTc[slot], in0=pTc[slot], in1=mask_c, op=ALU.mult),
                             reads=[("pTc", slot), "consts"], writes=[("pTc", slot)])
                    else:
                        for u in range(len(pair)):
                            if firsts[u]:
                                P.op("act", lambda e, psv=psv, slot=slot, u=u: e.activation(
                                    out=pTc[slot][:, u, 1, :], in_=psv[:, u, 1, :], func=AF.Exp, scale=SCALE_C),
                                    reads=[("ps", sb_)], writes=[("pTc", slot)])
                                P.op("dve", lambda e, slot=slot, u=u: e.tensor_tensor(
                                    out=pTc[slot][:, u, 1, :], in0=pTc[slot][:, u, 1, :], in1=mask_c[:, u, 1, :], op=ALU.mult),
                                    reads=[("pTc", slot), "consts"], writes=[("pTc", slot)])
                            else:
                                P.op("act", lambda e, psv=psv, slot=slot, u=u: e.activation(
                                    out=pTc[slot][:, u, :, :], in_=psv[:, u, :, :], func=AF.Exp, scale=SCALE_C),
                                    reads=[("ps", sb_)], writes=[("pTc", slot)])
                                P.op("dve", lambda e, slot=slot, u=u: e.tensor_tensor(
                                    out=pTc[slot][:, u, :, :], in0=pTc[slot][:, u, :, :], in1=mask_c[:, u, :, :], op=ALU.mult),
                                    reads=[("pTc", slot), "consts"], writes=[("pTc", slot)])
                    for u, (n, r, blk, lpos) in enumerate(pair):
                        vres = [("Vs", blk // 4), ("Vs", max(0, blk - d) // 4)]
                        if not firsts[u]:
                            P.op("pe", lambda e, u=u, blk=blk, slot=slot, d=d: e.matmul(
                                ps[PO][:, u * 128:(u + 1) * 128], lhsT=Vs[:, blk - d, :], rhs=pTc[slot][:, u, 0, :],
                                start=True, stop=False), reads=[("pTc", slot)] + vres, writes=[("ps", PO)])
                        P.op("pe", lambda e, u=u, blk=blk, slot=slot, first=firsts[u]: e.matmul(
                            ps[PO][:, u * 128:(u + 1) * 128], lhsT=Vs[:, blk, :], rhs=pTc[slot][:, u, 1, :],
                            start=first, stop=True), reads=[("pTc", slot)] + vres, writes=[("ps", PO)])
                    for u, (n, r, blk, lpos) in enumerate(pair):
                        if not firsts[u]:
                            P.op("pe", lambda e, u=u, slot=slot: e.matmul(
                                ps[PZ][:, u * 128:(u + 1) * 128], lhsT=ones_bf, rhs=pTc[slot][:, u, 0, :],
                                start=True, stop=False), reads=[("pTc", slot), "consts"], writes=[("ps", PZ)])
                        P.op("pe", lambda e, u=u, slot=slot, first=firsts[u]: e.matmul(
                            ps[PZ][:, u * 128:(u + 1) * 128], lhsT=ones_bf, rhs=pTc[slot][:, u, 1, :],
                            start=first, stop=True), reads=[("pTc", slot), "consts"], writes=[("ps", PZ)])
                    n0, r0 = pair[0][0], pair[0][1]
                    if d == 1:
                        t0_ = n0 * 128
                        vO = accO[:, t0_:t0_ + 256]
                        vZ = accZ[:, t0_:t0_ + 256]
                        sO = ps[PO][:, 0:256]
                        sZ = ps[PZ][:, 0:256]
                    else:
                        vO = accO[:, n0 * Ls:(n0 + 1) * Ls].rearrange("p (i r) -> p r i", r=d)[:, r0:r0 + 2, :]
                        vZ = accZ[:, n0 * Ls:(n0 + 1) * Ls].rearrange("p (i r) -> p r i", r=d)[:, r0:r0 + 2, :]
                        sO = ps[PO][:, 0:256].rearrange("p (u q) -> p u q", u=2)
                        sZ = ps[PZ][:, 0:256].rearrange("p (u q) -> p u q", u=2)
                    tt = (n0 * Ls) // 512
                    ares = [("acc", tt + t_) for t_ in range(tiles_per_span)]
                    if g == 0:
                        P.op("dve", lambda e, vO=vO, sO=sO: e.tensor_copy(vO, sO), reads=[("ps", PO)], writes=ares)
                        P.op("act", lambda e, vZ=vZ, sZ=sZ: e.activation(out=vZ, in_=sZ, func=AF.Copy), reads=[("ps", PZ)],
                             writes=[("accz", a_[1]) for a_ in ares])
                    else:
                        P.op("dve", lambda e, vO=vO, sO=sO: e.tensor_tensor(out=vO, in0=vO, in1=sO, op=ALU.add),
                             reads=[("ps", PO)] + ares, writes=ares)
                        P.op("dve", lambda e, vZ=vZ, sZ=sZ: e.tensor_tensor(out=vZ, in0=vZ, in1=sZ, op=ALU.add),
                             reads=[("ps", PZ)] + [("accz", a_[1]) for a_ in ares], writes=[("accz", a_[1]) for a_ in ares])
                if g == 2:
                    n0 = (j * 512) // Ls
                    for q4 in range(4):
                        tt = n0 * 4 + q4
                        if tt >= ntiles:
                            continue
                        sl = slice(tt * 512, (tt + 1) * 512)

                        def fill(st, res, sl=sl, q4=q4, tt=tt):
                            P.op("dve", lambda e: e.reciprocal(rz, accZ[:, sl]), reads=[("accz", tt)], writes=["rz"])
                            P.op("dve", lambda e: e.tensor_tensor(out=rz, in0=rz, in1=accO[:, sl], op=ALU.mult),
                                 reads=["rz", ("acc", tt)], writes=["rz"])
                            P.op("dve", lambda e: e.tensor_tensor(out=st, in0=rz, in1=szT[:, q4 * 512:(q4 + 1) * 512], op=ALU.mult),
                                 reads=["rz", ("szT", q4)], writes=[res])
                        store_oz(ozstC, 256, tt, fill)
            P.barrier()
    if "C" in passes:
        pass_c()
        if after_pass is not None:
            after_pass("C")


def build_mix(layer, passes=("A", "B", "C"), ntiles=16, dbg=99):
    nc = bass.Bass("TRN2", target_bir_lowering=False)
    d_uT = nc.dram_tensor("uT", [D, S], BF16, kind="ExternalInput").ap()
    d_w = nc.dram_tensor("w_mix", [D, WMIX], F32, kind="ExternalInput").ap()
    d_vecs = nc.dram_tensor("vecs", [128, NV_TOT], F32, kind="ExternalInput").ap()
    d_rotC = nc.dram_tensor("rotC", [S, 128], F32, kind="ExternalInput").ap()
    d_rotS = nc.dram_tensor("rotS", [S, 128], F32, kind="ExternalInput").ap()
    d_oz = nc.dram_tensor("ozT", [384, S], BF16, kind="ExternalOutput").ap()
    uT_v = d_uT.rearrange("(c p) t -> p c t", p=128)
    with ExitStack() as es:
        arena_t = es.enter_context(nc.sbuf_tensor("arena", [128, 95 * 1024], BF16))
        A = Arena(arena_t[:, :], 190 * 1024)
        ps = _ps_banks(nc, es)
        P = Prog(nc, ALL_KEYS)
        emit_mix(P, A, ps, layer, lambda j, q4: uT_v[:, q4 * 4:(q4 + 1) * 4, j * 512:(j + 1) * 512], d_w, d_vecs, d_rotC, d_rotS, d_oz,
                 passes=passes, ntiles=ntiles, dbg=dbg)
        P.emit()
    return nc


def _const_tables():
    p = np.arange(128)
    tri = (p[None, :] >= p[:, None]).astype(np.float32)
    mprev = (p[:, None] >= p[None, :]).astype(np.float32)
    return tri, mprev, np.eye(128, dtype=np.float32)


def _rot_tables(h):
    dk = 64
    theta = 1.0 / (10000.0 ** np.linspace(0.0, 1.0, dk // 2, dtype=np.float32)).astype(np.float32)
    pos = np.arange(S, dtype=np.float32)
    ang = (pos[:, None] * theta[None, :]).astype(np.float32).astype(np.float64)
    cos = np.repeat(np.cos(ang), 2, axis=-1)
    sin = np.repeat(np.sin(ang), 2, axis=-1)
    sgn = np.tile(np.array([-1.0, 1.0]), dk // 2)[None, :]
    log_g = np.log1p(-np.exp2(-5.0 - float(h)))
    i = (np.arange(S) % 128).astype(np.float64)
    qdec = np.exp((i + 1.0) * log_g)[:, None]
    kdec = np.exp(-(i + 1.0) * log_g)[:, None] * dk ** -0.5
    C = np.concatenate([cos * qdec, cos * kdec], axis=1).astype(np.float32)
    Sg = np.concatenate([sin * sgn * qdec, sin * sgn * kdec], axis=1).astype(np.float32)
    g128 = float(np.exp(128.0 * log_g))
    return C, Sg, g128


def _mix_weight_cols(h):
    cols = []
    cols += list(range(O_AQ + h * 128, O_AQ + (h + 1) * 128))
    cols += list(range(O_AK + h * 128, O_AK + (h + 1) * 128))
    cols += list(range(O_AV + h * 128, O_AV + (h + 1) * 128))
    cols += list(range(O_AZ + h * 128, O_AZ + (h + 1) * 128))
    cols += list(range(O_BQ + h * 64, O_BQ + (h + 1) * 64))
    cols += list(range(O_BK + h * 64, O_BK + (h + 1) * 64))
    cols += list(range(O_BV + h * 128, O_BV + (h + 1) * 128))
    cols += list(range(O_BZ + h * 128, O_BZ + (h + 1) * 128))
    for g in range(3):
        base = g * 1024 + h * 128
        cols += list(range(O_CQ + base, O_CQ + base + 128))
        cols += list(range(O_CK + base, O_CK + base + 128))
        cols += list(range(O_CV + base, O_CV + base + 128))
    cols += list(range(O_CZ + h * 128, O_CZ + (h + 1) * 128))
    return np.array(cols)


def mix_inputs(h, w_in_l, lam_l, da_nw_l, ret_nw_l):
    tri, mprev, ident = _const_tables()
    C, Sg, g128 = _rot_tables(h)
    vecs = np.zeros((128, NV_TOT), np.float32)
    vecs[:, NV_LAM:NV_LAM + 256] = lam_l.reshape(1, 256)
    vecs[:, NV_DNW:NV_DNW + 128] = da_nw_l.reshape(1, 128)
    vecs[:, NV_RNW:NV_RNW + 128] = ret_nw_l.reshape(1, 128)
    vecs[:, NV_TRI:NV_TRI + 128] = tri
    vecs[:, NV_MPREV:NV_MPREV + 128] = mprev
    vecs[:, NV_IDENT:NV_IDENT + 128] = ident
    vecs[:, NV_G] = g128
    w = np.ascontiguousarray(w_in_l[:, _mix_weight_cols(h)])
    return dict(w_mix=w, vecs=vecs, rotC=C, rotS=Sg)


def emit_merge(P, A, ps, final, d_x, uT_v, oz_v, d_wg, d_wp, d_wo, d_cT, d_wada, d_brow, d_fnw, d_out):
    wg_v = d_wg.rearrange("(c p) n -> p c n", p=128)
    wo_v = d_wo.rearrange("(c p) n -> p c n", p=128)
    ones = A.alloc([128, 128], F32)
    gate_bc = A.alloc([128, D], F32)
    epsc = A.alloc([128, 1], F32)
    mT = A.alloc([128, 16, TOK], BF16)
    P.op("dve", lambda e: e.memset(ones, 1.0), writes=["ones"])
    P.op("dve", lambda e: e.memset(epsc, EPS), writes=["epsc"])
    mark = A.off
    uT = A.alloc([128, 16, TOK], BF16)
    oz = A.alloc([128, 24, TOK], BF16)
    wg = [A.alloc([128, 16, 512], BF16), A.alloc([128, 16, 512], BF16)]
    wp = [A.alloc([128, 8, 512], BF16), A.alloc([128, 8, 512], BF16)]
    mark2 = A.off
    for q in range(4):
        P.op("sp", lambda e, q=q: e.dma_start(out=uT[:, q * 4:(q + 1) * 4, :], in_=uT_v[:, q * 4:(q + 1) * 4, :]),
             writes=[("uT", q)], dma="uT")
    for q in range(6):
        P.op("sp", lambda e, q=q: e.dma_start(out=oz[:, q * 4:(q + 1) * 4, :], in_=(oz_v(q) if callable(oz_v) else oz_v[:, q * 4:(q + 1) * 4, :])),
             writes=[("oz", q)], dma="oz")
    accm = A.alloc([128, 4, TOK], F32)
    sig = [A.alloc([128, 512], F32), A.alloc([128, 512], F32)]
    tmp = sig
    units = [(ct, b) for ct in range(4) for b in range(3)]

    def load_unit(i):
        ct, b = units[i]
        sl = i % 2
        for half in range(2):
            P.op("pool", lambda e, half=half: e.dma_start(
                out=wg[sl][:, half * 8:(half + 1) * 8, :],
                in_=wg_v[:, half * 8:(half + 1) * 8, b * D + ct * 512:b * D + (ct + 1) * 512]),
                writes=[("wg", sl, half)], dma="wg%d" % sl)
        P.op("pool", lambda e: e.dma_start(out=wp[sl], in_=d_wp[b, :, ct * 512:(ct + 1) * 512].rearrange("(c p) n -> p c n", p=128)),
             writes=[("wp", sl)], dma="wp%d" % sl)

    load_unit(0)
    cnt = {"g": 0}
    for i, (ct, b) in enumerate(units):
        if i + 1 < len(units):
            load_unit(i + 1)
        sl = i % 2
        for m in range(4):
            for nt in range(2):
                k_ = cnt["g"] % 2
                cnt["g"] += 1
                G, Y = k_, 2 + k_
                for c in range(16):
                    P.op("pe", lambda e, c=c, m=m, nt=nt, G=G, sl=sl: e.matmul(
                        ps[G][:, :], lhsT=wg[sl][:, c, m * 128:(m + 1) * 128], rhs=uT[:, c, nt * 512:(nt + 1) * 512],
                        start=(c == 0), stop=(c == 15)),
                        reads=[("wg", sl, 0), ("wg", sl, 1)] + [("uT", q) for q in range(4)], writes=[("ps", G)])
                for kc in range(8):
                    P.op("pe", lambda e, kc=kc, m=m, nt=nt, Y=Y, sl=sl, b=b: e.matmul(
                        ps[Y][:, :], lhsT=wp[sl][:, kc, m * 128:(m + 1) * 128], rhs=oz[:, b * 8 + kc, nt * 512:(nt + 1) * 512],
                        start=(kc == 0), stop=(kc == 7)),
                        reads=[("wp", sl), ("oz", 2 * b), ("oz", 2 * b + 1)], writes=[("ps", Y)])
                P.op("act", lambda e, G=G, k_=k_: e.activation(out=sig[k_], in_=ps[G][:, :], func=AF.Sigmoid),
                     reads=[("ps", G)], writes=[("sig", k_)])
                av = accm[:, m, nt * 512:(nt + 1) * 512]
                ares = ("accm", m, nt)
                if b == 0:
                    P.op("dve", lambda e, Y=Y, k_=k_, av=av: e.tensor_tensor(out=av, in0=ps[Y][:, :], in1=sig[k_], op=ALU.mult),
                         reads=[("ps", Y), ("sig", k_)], writes=[ares])
                else:
                    P.op("dve", lambda e, Y=Y, k_=k_: e.tensor_tensor(out=tmp[k_], in0=ps[Y][:, :], in1=sig[k_], op=ALU.mult),
                         reads=[("ps", Y), ("sig", k_)], writes=[("sig", k_)])
                    if b == 1:
                        P.op("dve", lambda e, k_=k_, av=av: e.tensor_tensor(out=av, in0=av, in1=tmp[k_], op=ALU.add),
                             reads=[("sig", k_), ares], writes=[ares])
                    else:
                        mv = mT[:, ct * 4 + m, nt * 512:(nt + 1) * 512]
                        P.op("dve", lambda e, k_=k_, av=av, mv=mv: e.tensor_tensor(out=mv, in0=av, in1=tmp[k_], op=ALU.add),
                             reads=[("sig", k_), ares], writes=[("mT", ct)])
    P.barrier()
    A.off = mark
    wo = A.alloc([128, 16, D], BF16)
    xr = [A.alloc([128, D], F32), A.alloc([128, D], F32)]
    orow = [A.alloc([128, D], F32), A.alloc([128, D], F32)]
    junk = A.alloc([128, 512], F32)
    ssq = A.alloc([128, 8], F32)
    var = A.alloc([128, 2], F32)
    rstd = A.alloc([128, 2], F32)
    fnw = A.alloc([128, D], F32)
    mark2 = A.off
    for ct2 in range(4):
        for half in range(2):
            P.op("pool", lambda e, ct2=ct2, half=half: e.dma_start(
                out=wo[:, half * 8:(half + 1) * 8, ct2 * 512:(ct2 + 1) * 512],
                in_=wo_v[:, half * 8:(half + 1) * 8, ct2 * 512:(ct2 + 1) * 512]), writes=[("wo", ct2, half)], dma="wo")
    if final:
        P.op("sp", lambda e: e.dma_start(out=fnw, in_=d_fnw), writes=["fnw"], dma="misc")
    for hf in range(2):
        A.off = mark2
        acc = emit_mod_acc(P, A, d_wada, d_cT, d_brow, cols=(4096 + 1024 * hf, 5120 + 1024 * hf))
        for n2 in range(2):
            n = hf * 2 + n2
            P.op("pe", lambda e, n=n, n2=n2, acc=acc: e.matmul(ps[4 + n][:, :], lhsT=ones, rhs=acc[:, n2 * 512:(n2 + 1) * 512],
                                                               start=True, stop=True),
                 reads=["acc", "ones"], writes=[("ps", 4 + n)])
            P.op("dve", lambda e, n=n: e.tensor_copy(gate_bc[:, n * 512:(n + 1) * 512], ps[4 + n][:, :]),
                 reads=[("ps", 4 + n)], writes=["gate_bc"])
    for t in range(8):
        xb = t % 2
        P.op("sp", lambda e, t=t, xb=xb: e.dma_start(out=xr[xb], in_=d_x[t * 128:(t + 1) * 128, :]), writes=[("xr", xb)],
             dma="x%d" % xb)
        for ct2 in range(4):
            ob_ = 4 + ct2
            for c in range(16):
                P.op("pe", lambda e, c=c, t=t, ct2=ct2, ob_=ob_: e.matmul(
                    ps[ob_][:, :], lhsT=mT[:, c, t * 128:(t + 1) * 128], rhs=wo[:, c, ct2 * 512:(ct2 + 1) * 512],
                    start=(c == 0), stop=(c == 15)),
                    reads=[("wo", ct2, 0), ("wo", ct2, 1)] + [("mT", q) for q in range(4)], writes=[("ps", ob_)])
            cs = slice(ct2 * 512, (ct2 + 1) * 512)
            P.op("dve", lambda e, ob_=ob_, cs=cs, xb=xb: e.tensor_tensor(out=orow[xb][:, cs], in0=ps[ob_][:, :], in1=gate_bc[:, cs],
                                                                        op=ALU.mult),
                 reads=[("ps", ob_), "gate_bc"], writes=[("orow", xb, ct2)])
            P.op("dve", lambda e, cs=cs, xb=xb: e.tensor_tensor(out=orow[xb][:, cs], in0=orow[xb][:, cs], in1=xr[xb][:, cs], op=ALU.add),
                 reads=[("orow", xb, ct2), ("xr", xb)], writes=[("orow", xb, ct2)])
            if final:
                P.op("dve", lambda e, cs=cs, xb=xb, ct2=ct2: e.scalar_tensor_tensor(
                    out=junk, in0=orow[xb][:, cs], scalar=1.0, in1=orow[xb][:, cs], op0=ALU.mult, op1=ALU.mult,
                    accum_out=ssq[:, xb * 4 + ct2:xb * 4 + ct2 + 1]),
                    reads=[("orow", xb, ct2)], writes=["junk", ("ssq", xb, ct2)])
        ores = [("orow", xb, q) for q in range(4)]
        if final:
            P.op("dve", lambda e, xb=xb: e.tensor_tensor(out=ssq[:, xb * 4:xb * 4 + 2], in0=ssq[:, xb * 4:xb * 4 + 2],
                                                         in1=ssq[:, xb * 4 + 2:xb * 4 + 4], op=ALU.add),
                 reads=[("ssq", xb, q) for q in range(4)], writes=[("ssq", xb, 0), ("ssq", xb, 1)])
            P.op("dve", lambda e, xb=xb: e.tensor_tensor(out=ssq[:, xb * 4:xb * 4 + 1], in0=ssq[:, xb * 4:xb * 4 + 1],
                                                         in1=ssq[:, xb * 4 + 1:xb * 4 + 2], op=ALU.add),
                 reads=[("ssq", xb, 0), ("ssq", xb, 1)], writes=[("ssq", xb, 0)])
            P.op("act", lambda e, xb=xb: e.activation(out=var[:, xb:xb + 1], in_=ssq[:, xb * 4:xb * 4 + 1], func=AF.Sqrt,
                                                      scale=1.0 / D, bias=epsc), reads=[("ssq", xb, 0), "epsc"], writes=[("var", xb)])
            P.op("dve", lambda e, xb=xb: e.reciprocal(rstd[:, xb:xb + 1], var[:, xb:xb + 1]), reads=[("var", xb)],
                 writes=[("rstd", xb)])
            P.op("dve", lambda e, xb=xb: e.scalar_tensor_tensor(out=orow[xb], in0=orow[xb], scalar=rstd[:, xb:xb + 1], in1=fnw,
                                                                op0=ALU.mult, op1=ALU.mult),
                 reads=ores + [("rstd", xb), "fnw"], writes=ores)
        P.op("sp", lambda e, t=t, xb=xb: e.dma_start(out=d_out[t * 128:(t + 1) * 128, :], in_=orow[xb]), reads=ores,
             dma="out%d" % xb)


def build_merge(final):
    nc = bass.Bass("TRN2", target_bir_lowering=False)
    d_x = nc.dram_tensor("x", [TOK, D], F32, kind="ExternalInput").ap()
    d_uT = nc.dram_tensor("uT", [D, TOK], BF16, kind="ExternalInput").ap()
    d_oz = nc.dram_tensor("ozT", [3072, TOK], BF16, kind="ExternalInput").ap()
    d_wg = nc.dram_tensor("w_g", [D, 3 * D], F32, kind="ExternalInput").ap()
    d_wp = nc.dram_tensor("w_p", [3, 1024, D], F32, kind="ExternalInput").ap()
    d_wo = nc.dram_tensor("w_out", [D, D], F32, kind="ExternalInput").ap()
    d_cT = nc.dram_tensor("cT", [128, 16], F32, kind="ExternalInput").ap()
    d_wada = nc.dram_tensor("w_ada", [D, 3 * D], F32, kind="ExternalInput").ap()
    d_brow = nc.dram_tensor("brow", [1, 3 * D], F32, kind="ExternalInput").ap()
    d_fnw = nc.dram_tensor("fnw", [128, D], F32, kind="ExternalInput").ap()
    d_out = nc.dram_tensor("out", [TOK, D], F32, kind="ExternalOutput").ap()
    uT_v = d_uT.rearrange("(c p) t -> p c t", p=128)
    oz_v = d_oz.rearrange("(c p) t -> p c t", p=128)
    with ExitStack() as es:
        arena_t = es.enter_context(nc.sbuf_tensor("arena", [128, 95 * 1024], BF16))
        A = Arena(arena_t[:, :], 190 * 1024)
        ps = _ps_banks(nc, es)
        P = Prog(nc, ALL_KEYS)
        emit_merge(P, A, ps, final, d_x, uT_v, oz_v, d_wg, d_wp, d_wo, d_cT, d_wada, d_brow, d_fnw, d_out)
        P.emit()
    return nc


def build_fused(nlayers=2):
    nc = bass.Bass("TRN2", target_bir_lowering=False)
    dt = nc.dram_tensor
    d_x = dt("x", [TOK, D], F32, kind="ExternalInput").ap()
    d_cT = dt("cT", [128, 16], F32, kind="ExternalInput").ap()
    d_ident = dt("ident", [128, 128], F32, kind="ExternalInput").ap()
    d_rotC = dt("rotC", [S, 128], F32, kind="ExternalInput").ap()
    d_rotS = dt("rotS", [S, 128], F32, kind="ExternalInput").ap()
    d_fnw = dt("fnw", [128, D], F32, kind="ExternalInput").ap()
    d_out = dt("out", [TOK, D], F32, kind="ExternalOutput").ap()
    L_ = []
    for L in range(nlayers):
        L_.append(dict(
            w_ada=dt("w_ada%d" % L, [D, 3 * D], F32, kind="ExternalInput").ap(),
            brow=dt("brow%d" % L, [1, 3 * D], F32, kind="ExternalInput").ap(),
            nwT=dt("nwT%d" % L, [128, 16], F32, kind="ExternalInput").ap(),
            w_mix=dt("w_mix%d" % L, [D, WMIX], F32, kind="ExternalInput").ap(),
            vecs=dt("vecs%d" % L, [128, NV_TOT], F32, kind="ExternalInput").ap(),
            w_g=dt("w_g%d" % L, [D, 3 * D], F32, kind="ExternalInput").ap(),
            w_p=dt("w_p%d" % L, [3, 1024, D], F32, kind="ExternalInput").ap(),
            w_out=dt("w_out%d" % L, [D, D], F32, kind="ExternalInput").ap(),
        ))
    uT_own = dt("uT_own", [D, TOK], BF16, kind="Internal").ap()
    uT_all = dt("uT_all", [NCORES * D, TOK], BF16, kind="Internal").ap()
    oz_loc = [dt("oz_loc%d" % b, [128, S], BF16, kind="Internal").ap() for b in range(3)]
    oz_all = [dt("oz_all%d" % b, [NCORES * 128, S], BF16, kind="Internal").ap() for b in range(3)]
    h_own = dt("h_own", [TOK, D], F32, kind="Internal").ap()
    d_cidx = dt("cidx", [1, 4], mybir.dt.int32, kind="ExternalInput").ap()
    uT_all_v = uT_all.rearrange("(r c p) t -> p r c t", r=NCORES, p=128)
    oz_all_v = [o.rearrange("(h p) t -> p h t", p=128) for o in oz_all]
    uT_own_v = uT_own.rearrange("(c p) t -> p c t", p=128)
    groups = [list(range(NCORES))]
    with ExitStack() as es:
        arena_t = es.enter_context(nc.sbuf_tensor("arena", [128, 95 * 1024], BF16))
        A = Arena(arena_t[:, :], 190 * 1024)
        ps = _ps_banks(nc, es)
        P = Prog(nc, ALL_KEYS)
        P.core_idx_ap = d_cidx[0:1, 0:1]
        def gather_oz(name):
            b = "ABC".index(name)
            P.op("pool", lambda e: e.collective_compute("AllGather", ALU.bypass, replica_groups=groups, ins=[oz_loc[b]],
                                                        outs=[oz_all[b]]), writes=[("oz_all", b)], dma="cc", inc=1)

        for L in range(nlayers):
            w = L_[L]
            src_x = d_x if L == 0 else h_own
            A.off = 0
            emit_prep(P, A, ps, src_x, d_cT, w["w_ada"], w["brow"], w["nwT"], d_ident, uT_own)
            P.barrier()
            P.op("pool", lambda e: e.collective_compute("AllGather", ALU.bypass, replica_groups=groups, ins=[uT_own], outs=[uT_all]),
                 writes=["uT_all"], dma="cc", inc=1)
            P.barrier()
            A.off = 0
            emit_mix(P, A, ps, L, lambda j, q4: uT_all_v[:, j // 2, q4 * 4:(q4 + 1) * 4, (j % 2) * 512:(j % 2) * 512 + 512],
                     w["w_mix"], w["vecs"], d_rotC, d_rotS,
                     lambda row0, j: oz_loc[row0 // 128][:, j * 512:(j + 1) * 512], after_pass=gather_oz)
            P.barrier()
            A.off = 0
            final = (L == nlayers - 1)
            emit_merge(P, A, ps, final, src_x, uT_own_v,
                       lambda q: oz_all_v[q // 2][:, (q % 2) * 4:(q % 2) * 4 + 4, bass.ds(P.core_idx * TOK, TOK)],
                       w["w_g"], w["w_p"], w["w_out"], d_cT, w["w_ada"], w["brow"], d_fnw, d_out if final else h_own)
            P.barrier()
        P.emit()
    return nc


def fused_inputs(i, inp):
    nl = inp["w_in"].shape[0]
    m = dict(x=np.ascontiguousarray(inp["x"].reshape(S, D)[i * TOK:(i + 1) * TOK]), cidx=np.array([[i, 0, 0, 0]], np.int32),
             cT=np.ascontiguousarray(inp["c"].reshape(16, 128).T), ident=np.eye(128, dtype=np.float32),
             fnw=np.ascontiguousarray(np.broadcast_to(inp["final_norm_w"].reshape(1, D), (128, D))))
    for L in range(nl):
        mi = mix_inputs(i, inp["w_in"][L], inp["lam"][L], inp["da_norm_w"][L], inp["ret_norm_w"][L])
        m["rotC"], m["rotS"] = mi["rotC"], mi["rotS"]
        m["w_mix%d" % L] = mi["w_mix"]
        m["vecs%d" % L] = mi["vecs"]
        m["w_ada%d" % L] = np.ascontiguousarray(inp["w_ada"][L])
        m["brow%d" % L] = np.ascontiguousarray(inp["b_ada"][L].reshape(1, -1))
        m["nwT%d" % L] = np.ascontiguousarray(inp["norm_w"][L].reshape(16, 128).T)
        m["w_g%d" % L] = np.ascontiguousarray(inp["w_in"][L][:, O_GA:O_GA + 3 * D])
        m["w_p%d" % L] = np.ascontiguousarray(np.stack([inp["w_proj_a"][L], inp["w_proj_b"][L], inp["w_proj_c"][L]]))
        m["w_out%d" % L] = np.ascontiguousarray(inp["w_out"][L])
    return m


def merge_inputs(x_own, uT_own, ozT_own, layer, inp):
    bf = _bf16_np()
    return dict(
        x=np.ascontiguousarray(x_own, dtype=np.float32), uT=np.ascontiguousarray(uT_own).astype(bf),
        ozT=np.ascontiguousarray(ozT_own).astype(bf),
        w_g=np.ascontiguousarray(inp["w_in"][layer][:, O_GA:O_GA + 3 * D]),
        w_p=np.ascontiguousarray(np.stack([inp["w_proj_a"][layer], inp["w_proj_b"][layer], inp["w_proj_c"][layer]])),
        w_out=np.ascontiguousarray(inp["w_out"][layer]),
        cT=np.ascontiguousarray(inp["c"].reshape(16, 128).T), w_ada=np.ascontiguousarray(inp["w_ada"][layer]),
        brow=np.ascontiguousarray(inp["b_ada"][layer].reshape(1, -1)),
        fnw=np.ascontiguousarray(np.broadcast_to(inp["final_norm_w"].reshape(1, D), (128, D))),
    )


_PROGS = {}
FUSED = False
_DBG = None


def _prog(key, builder):
    if key not in _PROGS:
        _PROGS[key] = builder()
    return _PROGS[key]


def _run(nc, in_maps):
    return run_bass_kernel_spmd(nc, in_maps, core_ids=list(range(NCORES))).results


def kernel(x, c, w_in, w_proj_a, w_proj_b, w_proj_c, w_out, w_ada, b_ada, norm_w, lam, da_norm_w, ret_norm_w,
           final_norm_w):
    inp = dict(x=x, c=c, w_in=w_in, w_proj_a=w_proj_a, w_proj_b=w_proj_b, w_proj_c=w_proj_c, w_out=w_out,
               w_ada=w_ada, b_ada=b_ada, norm_w=norm_w, lam=lam, da_norm_w=da_norm_w, ret_norm_w=ret_norm_w,
               final_norm_w=final_norm_w)
    inp = {k: np.asarray(v, dtype=np.float32) for k, v in inp.items()}
    if FUSED:
        nc = _prog("fused", build_fused)
        res = _run(nc, [fused_inputs(i, inp) for i in range(NCORES)])
        out = np.concatenate([np.asarray(res[i]["out"]) for i in range(NCORES)], axis=0)
        return out.reshape(1, S, D).astype(np.float32)
    h = np.ascontiguousarray(inp["x"].reshape(S, D))
    cT = np.ascontiguousarray(inp["c"].reshape(16, 128).T)
    ident = np.eye(128, dtype=np.float32)
    nlayers = inp["w_in"].shape[0]
    for L in range(nlayers):
        nc = _prog("prep", build_prep)
        maps = [dict(x=np.ascontiguousarray(h[i * TOK:(i + 1) * TOK]), cT=cT, w_ada=np.ascontiguousarray(inp["w_ada"][L]),
                     brow=np.ascontiguousarray(inp["b_ada"][L].reshape(1, -1)),
                     nwT=np.ascontiguousarray(inp["norm_w"][L].reshape(16, 128).T), ident=ident) for i in range(NCORES)]
        res = _run(nc, maps)
        uT_own = [np.asarray(res[i]["uT"]) for i in range(NCORES)]
        uT_full = np.ascontiguousarray(np.concatenate(uT_own, axis=1))
        if _DBG is not None:
            _DBG[('uT', L)] = uT_full
        nc = _prog(("mix", L), lambda: build_mix(L))
        maps = []
        for hd in range(NCORES):
            m = mix_inputs(hd, inp["w_in"][L], inp["lam"][L], inp["da_norm_w"][L], inp["ret_norm_w"][L])
            m["uT"] = uT_full
            maps.append(m)
        res = _run(nc, maps)
        ozT = [np.asarray(res[hd]["ozT"]) for hd in range(NCORES)]
        if _DBG is not None:
            _DBG[('ozT', L)] = ozT
        final = (L == nlayers - 1)
        nc = _prog(("merge", final), lambda: build_merge(final))
        maps = []
        for i in range(NCORES):
            ts = slice(i * TOK, (i + 1) * TOK)
            oz_own = np.concatenate([ozT[hd][b * 128:(b + 1) * 128, ts] for b in range(3) for hd in range(NCORES)], axis=0)
            maps.append(merge_inputs(h[ts], uT_own[i], oz_own, L, inp))
        res = _run(nc, maps)
        h = np.ascontiguousarray(np.concatenate([np.asarray(res[i]["out"]) for i in range(NCORES)], axis=0))
        if _DBG is not None:
            _DBG[('h', L)] = h
    return h.reshape(1, S, D).astype(np.float32)
```
